# Optimizing a Trainium2 kernel written in Bass

```python
import math
import jax, jax.numpy as jnp
from jax import lax
import numpy as np

D_MODEL = 4096
BATCH = 2
SEQ = 4096
DEPTH = 1
DEC_BATCH = 1
DEC_SEQ = 16384
PAST_LEN = 128

HEAD_DIM = 128
ROT_DIM = HEAD_DIM // 4
ROPE_THETA = 500000.0
DIL_GROUPS = ((128, 1), (512, 4), (2048, 16))
N_DIL = len(DIL_GROUPS)
A_HEADS = 8
A_BLOCK = 64
A_WIDTH = A_HEADS * HEAD_DIM
B_Q_HEADS = 16
B_KV_HEADS = 4
B_WINDOW = 128
B_Q_WIDTH = B_Q_HEADS * HEAD_DIM
B_KV_WIDTH = B_KV_HEADS * HEAD_DIM
D_FF = 11008
EPS = 1e-6
NEG_INF = -1e30
IN_SPLITS = (N_DIL * A_WIDTH, N_DIL * A_WIDTH, N_DIL * A_WIDTH,
             B_Q_WIDTH, B_KV_WIDTH, B_KV_WIDTH, D_MODEL, D_MODEL)
IN_WIDTH = sum(IN_SPLITS)

kernel_name = "hybrid_dilated_swa_gqa_sink_encoder"


def rmsnorm(x, g):
    xf = x.astype(jnp.float32)
    var = jnp.mean(xf * xf, axis=-1, keepdims=True)
    return (xf * lax.rsqrt(var + EPS)).astype(x.dtype) * g


def swiglu(x, w_gate, w_up, w_down):
    return (jax.nn.silu(x @ w_gate) * (x @ w_up)) @ w_down


def partial_rope(x, pos):
    half = ROT_DIM // 2
    inv = ROPE_THETA ** (-jnp.arange(half, dtype=jnp.float32) / half)
    ang = pos.astype(jnp.float32)[:, None] * inv[None, :]
    cos = jnp.cos(ang)[None, :, None, :]
    sin = jnp.sin(ang)[None, :, None, :]
    xr = x[..., :ROT_DIM].astype(jnp.float32)
    x1, x2 = xr[..., :half], xr[..., half:]
    rot = jnp.concatenate([x1 * cos - x2 * sin, x2 * cos + x1 * sin], axis=-1)
    return jnp.concatenate([rot.astype(x.dtype), x[..., ROT_DIM:]], axis=-1)


def banded_attention(q, k, v, half_window, blk, sink_logit=None):
    b, L, hq, dh = q.shape
    hk = k.shape[2]
    grp = hq // hk
    nb = -(-L // blk)
    pad = nb * blk - L
    qp = jnp.pad(q, ((0, 0), (0, pad), (0, 0), (0, 0)))
    kp = jnp.pad(k, ((0, 0), (blk, pad + blk), (0, 0), (0, 0)))
    vp = jnp.pad(v, ((0, 0), (blk, pad + blk), (0, 0), (0, 0)))
    qb = qp.reshape(b, nb, blk, hk, grp, dh)
    kb = kp.reshape(b, nb + 2, blk, hk, dh)
    vb = vp.reshape(b, nb + 2, blk, hk, dh)
    kn = jnp.concatenate([kb[:, :-2], kb[:, 1:-1], kb[:, 2:]], axis=2)
    vn = jnp.concatenate([vb[:, :-2], vb[:, 1:-1], vb[:, 2:]], axis=2)
    scale = 1.0 / math.sqrt(dh)
    s = jnp.einsum('bnqhgd,bnkhd->bnhgqk', qb, kn,
                   preferred_element_type=jnp.float32) * scale
    qpos = jnp.arange(nb)[:, None] * blk + jnp.arange(blk)[None, :]
    kpos = (jnp.arange(nb) * blk - blk)[:, None] + jnp.arange(3 * blk)[None, :]
    rel = kpos[:, None, :] - qpos[:, :, None]
    valid = (jnp.abs(rel) <= half_window) & (kpos[:, None, :] >= 0) & (kpos[:, None, :] < L)
    s = jnp.where(valid[None, :, None, None], s, NEG_INF)
    m = jnp.max(s, axis=-1, keepdims=True)
    if sink_logit is not None:
        sk = sink_logit.astype(jnp.float32).reshape(hk, grp)[None, None, :, :, None, None]
        m = jnp.maximum(m, sk)
    p = jnp.exp(s - m)
    den = jnp.sum(p, axis=-1, keepdims=True)
    if sink_logit is not None:
        den = den + jnp.exp(sk - m)
    o = jnp.einsum('bnhgqk,bnkhd->bnqhgd', p.astype(v.dtype), vn,
                   preferred_element_type=jnp.float32)
    den_t = jnp.transpose(den[..., 0], (0, 1, 4, 2, 3))[..., None]
    o = (o / den_t).astype(q.dtype).reshape(b, nb * blk, hq, dh)[:, :L]
    lse = jnp.transpose((m + jnp.log(den))[..., 0], (0, 1, 4, 2, 3)).reshape(b, nb * blk, hq)[:, :L]
    return o, lse


def dilated_attention(q, k, v, window, dilation):
    b, S, h, dh = q.shape
    L = S // dilation

    def to_sub(t):
        return t.reshape(b, L, dilation, h, dh).transpose(0, 2, 1, 3, 4).reshape(b * dilation, L, h, dh)

    o, lse = banded_attention(to_sub(q), to_sub(k), to_sub(v), (window // 2) // dilation, A_BLOCK)
    o = o.reshape(b, dilation, L, h, dh).transpose(0, 2, 1, 3, 4).reshape(b, S, h, dh)
    lse = lse.reshape(b, dilation, L, h).transpose(0, 2, 1, 3).reshape(b, S, h)
    return o, lse


def encoder_layer(x, g_ffn1, w1_gate, w1_up, w1_down, g_mix, w_in, sink_b,
                  w_branch_a, w_branch_b, w_out, g_ffn2, w2_gate, w2_up, w2_down):
    b, S, _ = x.shape
    pos = jnp.arange(S)
    h = x + 0.5 * swiglu(rmsnorm(x, g_ffn1), w1_gate, w1_up, w1_down)
    n = rmsnorm(h, g_mix)
    proj = n @ w_in
    qa, ka, va, qb, kb, vb, ga, gb = jnp.split(proj, np.cumsum(IN_SPLITS)[:-1].tolist(), axis=-1)
    qa = partial_rope(qa.reshape(b, S, N_DIL * A_HEADS, HEAD_DIM), pos).reshape(b, S, N_DIL, A_HEADS, HEAD_DIM)
    ka = partial_rope(ka.reshape(b, S, N_DIL * A_HEADS, HEAD_DIM), pos).reshape(b, S, N_DIL, A_HEADS, HEAD_DIM)
    va = va.reshape(b, S, N_DIL, A_HEADS, HEAD_DIM)
    outs, lses = [], []
    for gi, (win, dil) in enumerate(DIL_GROUPS):
        o, l = dilated_attention(qa[:, :, gi], ka[:, :, gi], va[:, :, gi], win, dil)
        outs.append(o)
        lses.append(l)
    alpha = jax.nn.softmax(jnp.stack(lses, axis=0), axis=0)
    ya = jnp.sum(alpha[..., None] * jnp.stack(outs, axis=0).astype(jnp.float32), axis=0)
    ya = ya.astype(x.dtype).reshape(b, S, A_WIDTH)
    qb = partial_rope(qb.reshape(b, S, B_Q_HEADS, HEAD_DIM), pos)
    kb = partial_rope(kb.reshape(b, S, B_KV_HEADS, HEAD_DIM), pos)
    vb = vb.reshape(b, S, B_KV_HEADS, HEAD_DIM)
    yb, _ = banded_attention(qb, kb, vb, B_WINDOW, B_WINDOW, sink_logit=sink_b)
    yb = yb.reshape(b, S, B_Q_WIDTH)
    merged = jax.nn.sigmoid(ga) * (ya @ w_branch_a) + jax.nn.sigmoid(gb) * (yb @ w_branch_b)
    h2 = h + merged @ w_out
    return h2 + 0.5 * swiglu(rmsnorm(h2, g_ffn2), w2_gate, w2_up, w2_down)


def trunk(x, g_ffn1, w1_gate, w1_up, w1_down, g_mix, w_in, sink_b, w_branch_a, w_branch_b,
          w_out, g_ffn2, w2_gate, w2_up, w2_down, g_final):
    for l in range(DEPTH):
        x = encoder_layer(x, g_ffn1[l], w1_gate[l], w1_up[l], w1_down[l], g_mix[l], w_in[l], sink_b[l],
                          w_branch_a[l], w_branch_b[l], w_out[l], g_ffn2[l], w2_gate[l], w2_up[l], w2_down[l])
    return rmsnorm(x, g_final)


def setup_inputs(seed: int = 0) -> dict:
    key = jax.random.key(seed)
    ks = jax.random.split(key, 20)
    f32 = jnp.float32

    def w(k, shape, fan_in):
        return jax.random.normal(k, shape, f32) * (fan_in ** -0.5)

    def gain(k, shape):
        return 1.0 + 0.05 * jax.random.normal(k, shape, f32)

    return {
        "x_prompt": jax.random.normal(ks[0], (BATCH, SEQ, D_MODEL), f32),
        "x_sample": jax.random.normal(ks[1], (DEC_BATCH, DEC_SEQ, D_MODEL), f32),
        "g_ffn1": gain(ks[2], (DEPTH, D_MODEL)),
        "w1_gate": w(ks[3], (DEPTH, D_MODEL, D_FF), D_MODEL),
        "w1_up": w(ks[4], (DEPTH, D_MODEL, D_FF), D_MODEL),
        "w1_down": w(ks[5], (DEPTH, D_FF, D_MODEL), D_FF),
        "g_mix": gain(ks[6], (DEPTH, D_MODEL)),
        "w_in": w(ks[7], (DEPTH, D_MODEL, IN_WIDTH), D_MODEL),
        "sink_b": 0.5 * jax.random.normal(ks[8], (DEPTH, B_Q_HEADS), f32),
        "w_branch_a": w(ks[9], (DEPTH, A_WIDTH, D_MODEL), A_WIDTH),
        "w_branch_b": w(ks[10], (DEPTH, B_Q_WIDTH, D_MODEL), B_Q_WIDTH),
        "w_out": w(ks[11], (DEPTH, D_MODEL, D_MODEL), D_MODEL),
        "g_ffn2": gain(ks[12], (DEPTH, D_MODEL)),
        "w2_gate": w(ks[13], (DEPTH, D_MODEL, D_FF), D_MODEL),
        "w2_up": w(ks[14], (DEPTH, D_MODEL, D_FF), D_MODEL),
        "w2_down": w(ks[15], (DEPTH, D_FF, D_MODEL), D_FF),
        "g_final": gain(ks[16], (D_MODEL,)),
    }


def reference(x_prompt, x_sample, g_ffn1, w1_gate, w1_up, w1_down, g_mix, w_in, sink_b,
              w_branch_a, w_branch_b, w_out, g_ffn2, w2_gate, w2_up, w2_down, g_final):
    y_prompt = trunk(x_prompt, g_ffn1, w1_gate, w1_up, w1_down, g_mix, w_in, sink_b, w_branch_a,
                     w_branch_b, w_out, g_ffn2, w2_gate, w2_up, w2_down, g_final)
    y_sample = trunk(x_sample, g_ffn1, w1_gate, w1_up, w1_down, g_mix, w_in, sink_b, w_branch_a,
                     w_branch_b, w_out, g_ffn2, w2_gate, w2_up, w2_down, g_final)
    return (y_prompt, y_sample)
```

```python
import math
from contextlib import ExitStack

import numpy as np
import ml_dtypes

import concourse.bass as bass
import concourse.mybir as mybir
from concourse.bass_utils import run_bass_kernel_spmd

F32 = mybir.dt.float32
BF16 = mybir.dt.bfloat16
AF = mybir.ActivationFunctionType
ALU = mybir.AluOpType
AX = mybir.AxisListType

N_CORES = 8
OWN = 3072
HALO = 1024
EXT = OWN + 2 * HALO
T = 512
NT_EXT = EXT // T
OWN_T0 = HALO // T
NT_OWN = OWN // T
HD = 128
ROT = 32
THETA = 500000.0
EPS = 1e-6
NEGBIG = -30000.0
DIL = (1, 4, 16)
NQ = 40
NK = 28
VC = 3584
OW = 129


class Cfg:
    def __init__(self, D=4096, F=11008, HP=4):
        self.D, self.F, self.HP = D, F, HP
        self.KC = D // 128
        self.FC = F // 128
        self.IN_W = 12288 + 2 * D


class _Op:
    __slots__ = ("eng", "fn", "dma", "cum", "deps", "signal", "sigval")


class Sched:
    ENGS = ("sp", "act", "dve", "pool", "pe")

    def __init__(self):
        self.ops = []
        self.last_w = {}
        self.readers = {}
        self.dma_cum = {}
        self.last_on = {}

    def add(self, eng, fn, reads=(), writes=(), dma=None, extra=()):
        op = _Op()
        op.eng, op.fn, op.dma, op.signal, op.sigval = eng, fn, dma, False, 0
        idx = len(self.ops)
        deps = set(extra)
        for r in reads:
            w = self.last_w.get(r)
            if w is not None:
                deps.add(w)
        for r in writes:
            w = self.last_w.get(r)
            if w is not None:
                deps.add(w)
            rd = self.readers.get(r)
            if rd:
                deps.update(rd.values())
        if dma is not None:
            self.dma_cum[dma] = self.dma_cum.get(dma, 0) + 16
            op.cum = self.dma_cum[dma]
        else:
            op.cum = 0
        rkey = ("dma", dma) if dma is not None else eng
        for r in reads:
            self.readers.setdefault(r, {})[rkey] = idx
        for r in writes:
            self.last_w[r] = idx
            self.readers[r] = {}
        fd = []
        for j in deps:
            o = self.ops[j]
            if o.dma is not None:
                fd.append(j)
            elif o.eng == eng and dma is None and eng == "pe":
                continue
            else:
                o.signal = True
                fd.append(j)
        op.deps = fd
        self.ops.append(op)
        if fn is not None:
            self.last_on[eng] = idx
        return idx

    def barrier(self):
        lasts = dict(self.last_on)
        dma_last = {}
        for i, o in enumerate(self.ops):
            if o.dma is not None:
                dma_last[o.dma] = i
        ex = list(lasts.values()) + list(dma_last.values())
        for e in self.ENGS:
            self.add(e, None, extra=[j for j in ex if j != lasts.get(e) or self.ops[j].dma is not None])
        self.last_on = {}

    def emit(self, nc, stack):
        cnt = {e: 0 for e in self.ENGS}
        for o in self.ops:
            if o.signal and o.dma is None:
                cnt[o.eng] += 1
                o.sigval = cnt[o.eng]
        esem = {e: stack.enter_context(nc.semaphore("s_" + e)) for e in self.ENGS}
        dsem = {}
        for k in self.dma_cum:
            dsem[k] = stack.enter_context(nc.semaphore("d%d" % len(dsem)))
        streams = {e: [] for e in self.ENGS}
        for o in self.ops:
            streams[o.eng].append(o)
        ops = self.ops
        final = [(dsem[k], v) for k, v in self.dma_cum.items()]

        def run(eng_name, E):
            waited = {}
            for o in streams[eng_name]:
                for j in o.deps:
                    d = ops[j]
                    if d.dma is not None:
                        sem, val = dsem[d.dma], d.cum
                    else:
                        sem, val = esem[d.eng], d.sigval
                    key = id(sem)
                    if waited.get(key, 0) < val:
                        E.wait_ge(sem, val)
                        waited[key] = val
                if o.fn is None:
                    continue
                ins = o.fn(E)
                if o.dma is not None:
                    ins.then_inc(dsem[o.dma], 16)
                elif o.signal:
                    ins.then_inc(esem[eng_name], 1)
            if eng_name == "sp":
                for sem, val in final:
                    E.wait_ge(sem, val)

        block = stack.enter_context(nc.Block())

        @block.sync
        def _(e):
            run("sp", e)

        @block.scalar
        def _(e):
            run("act", e)

        @block.vector
        def _(e):
            run("dve", e)

        @block.gpsimd
        def _(e):
            run("pool", e)

        @block.tensor
        def _(e):
            run("pe", e)


class Pipe:
    def __init__(self, nslots=3, pd=2):
        self.n = 0
        self.ns = nslots
        self.pd = pd
        self.pending = []

    def stage(self, compute, load=None):
        slot = None
        if load is not None:
            slot = self.n % self.ns
            self.n += 1
            load(slot)
        self.pending.append((compute, slot))
        while len(self.pending) > self.pd:
            c, s = self.pending.pop(0)
            c(s)

    def flush(self):
        while self.pending:
            c, s = self.pending.pop(0)
            c(s)


def build(cfg, piece_order=None, use_order=None):
    if use_order is None:
        use_order = []
    D, F, KC, FC, HP, IN_W = cfg.D, cfg.F, cfg.KC, cfg.FC, cfg.HP, cfg.IN_W
    nc = bass.Bass("TRN2", target_bir_lowering=False)
    S = Sched()
    P = Pipe()

    def din(name, shape, dt=F32):
        return nc.dram_tensor(name, list(shape), dt, kind="ExternalInput").ap()

    def dscr(name, shape, dt):
        return nc.dram_tensor(name, list(shape), dt, kind="Internal").ap()

    xe = din("xe", [EXT, D])
    wf = {
        "w1g": din("w1g", [D, F]), "w1u": din("w1u", [D, F]), "w1d": din("w1d", [F, D]),
        "win": din("win", [D, IN_W]), "wa": din("wa", [1024, D]), "wb": din("wb", [2048, D]),
        "wo": din("wo", [D, D]),
        "w2g": din("w2g", [D, F]), "w2u": din("w2u", [D, F]), "w2d": din("w2d", [F, D]),
    }
    gT_in = din("gT", [128, 3 * KC])
    gfin_in = din("gfin", [D])
    sink_in = din("sink", [16])
    ropeC_in = din("ropeC", [ROT, EXT])
    ropeS_in = din("ropeS", [ROT, EXT])
    seqA_in = din("seqA", [3, EXT], BF16)
    seqB_in = din("seqB", [3, EXT], BF16)
    bandA_in = din("bandA", [64, 192])
    bandB_in = din("bandB", [128, 384])
    ident_in = din("ident", [128, 128], BF16)
    prot_in = din("prot", [ROT, ROT], BF16)
    y_out = nc.dram_tensor("y", [OWN, D], F32, kind="ExternalOutput").ap()

    wb16 = {k: dscr(k + "_b", list(v.shape), BF16) for k, v in wf.items()}
    K_scr = dscr("K_scr", [NK, 128, EXT], BF16)
    Q_scr = dscr("Q_scr", [NQ, 128, OWN], BF16)
    V_scr = dscr("V_scr", [EXT, VC], BF16)
    G_scr = dscr("G_scr", [2 * KC, 128, OWN], BF16)
    h_scr = dscr("h_scr", [OWN, D], F32)
    O_scr = dscr("O_scr", [OWN, 24, OW], F32)
    YA_scr = dscr("YA_scr", [8, 128, OWN], BF16)
    YB_scr = dscr("YB_scr", [16, 128, OWN], BF16)

    HPC = -(-FC // HP)
    parts = []
    c = 0
    for i in range(HP):
        n = FC // HP + (1 if i < FC % HP else 0)
        if n:
            parts.append((c, n))
        c += n
    WSLOT = max(KC * 512, 12288, 8192)
    sizes = {}
    off = {}
    cur = 0

    def region(name, nbytes):
        nonlocal cur
        off[name] = cur
        sizes[name] = nbytes
        cur += (nbytes + 63) // 64 * 64

    region("gT", 3 * KC * 4)
    region("ident", 256)
    region("prot", 64)
    region("sink", 64)
    region("stat", 256)
    region("sil", 2 * 2048)
    region("mt", 2 * 2048)
    SMALL_END = cur
    region("hbuf", max(4 * D * 4, 28672))
    region("xnT", KC * 512 * 2)
    region("hid", max(HPC * 1024, 1024))
    region("wst", 3 * WSLOT)
    region("aux", max(24576, 6 * D))
    ARENA = cur
    ARENA = max(ARENA, 190 * 1024)
    stack = ExitStack()
    arena = stack.enter_context(nc.sbuf_tensor("arena", [128, ARENA // 4], F32))
    banks = [stack.enter_context(nc.psum_tensor("ps%d" % i, [128, 512], F32)) for i in range(8)]

    def carve(o, nbytes, dt):
        v = arena[:, o // 4:(o + nbytes) // 4]
        return v.bitcast(dt) if dt != F32 else v

    def reg(name, dt, o=0, nbytes=None):
        return carve(off[name] + o, sizes[name] - o if nbytes is None else nbytes, dt)

    hbuf = reg("hbuf", F32, 0, 4 * D * 4).rearrange("p (b d) -> p b d", b=4)
    xnT = reg("xnT", BF16).rearrange("p (k t) -> p k t", t=T)
    hid = reg("hid", BF16).rearrange("p (k t) -> p k t", t=T)
    wst = [reg("wst", BF16, s * WSLOT, WSLOT) for s in range(3)]
    xnb = reg("aux", BF16, 0, D * 2)
    gfin = reg("aux", F32, D * 2, D * 4)
    yT = reg("aux", BF16, 0, 24576).rearrange("p (k t) -> p k t", t=T)
    gT = reg("gT", F32)
    ident = reg("ident", BF16)
    prot = reg("prot", BF16)[0:ROT, :]
    sink = reg("sink", F32)
    stat = reg("stat", F32)
    sil = [reg("sil", F32, i * 2048, 2048) for i in range(2)]
    mt = [reg("mt", F32, i * 2048, 2048) for i in range(2)]
    HB = off["hbuf"]
    xb = [carve(HB + i * 2048, 2048, BF16).rearrange("p (c t) -> p c t", t=T) for i in range(2)]
    sgt = [carve(HB + 4096 + i * 2048, 2048, BF16).rearrange("p (c t) -> p c t", t=T) for i in range(2)]
    vb = [carve(HB + 8192 + i * 4096, 4096, BF16).rearrange("p (b c) -> p b c", b=4) for i in range(2)]
    rC = carve(HB + 16384, 2048, F32)[0:ROT, :]
    rS = carve(HB + 18432, 2048, F32)[0:ROT, :]
    rt1 = [carve(HB + 20480 + i * 2048, 2048, F32)[0:ROT, :] for i in range(2)]
    rt2 = [carve(HB + 24576 + i * 2048, 2048, F32)[0:ROT, :] for i in range(2)]
    HBT = ["xb0", "xb1", "sg0", "sg1", "vb0", "vb1", "rC", "rS", "rt10", "rt11", "rt20", "rt21"]
    HB_ALL = ["hbuf"] + HBT

    bank_ctr = [0]

    def alloc_bank():
        b = bank_ctr[0] % 8
        bank_ctr[0] += 1
        return b

    def bk(b):
        return banks[b][:]

    def bkbf(b):
        return banks[b][:].bitcast(BF16)

    def bn(b):
        return "ps%d" % b

    ctr = {"sil": 0, "mt": 0, "xb": 0, "sg": 0, "vb": 0, "rt": 0}

    def rr(name, n=2):
        v = ctr[name] % n
        ctr[name] += 1
        return v

    def ld_const(dst, src, name):
        S.add("sp", lambda E, d=dst, s=src: E.dma_start(out=d, in_=s), writes=[name], dma="c_" + name)

    ld_const(gT, gT_in, "gT")
    ld_const(ident, ident_in, "ident")
    ld_const(prot, prot_in, "prot")
    ld_const(sink[:, 0:16], sink_in.partition_broadcast(128), "sink")

    def piece_list(k):
        rows, cols = wf[k].shape
        if k in ("w1d", "w2d"):
            out = []
            for (c0, nch) in parts:
                CH = max(1, min(nch, WSLOT // 1024))
                for s0 in range(0, nch, CH):
                    sn = min(CH, nch - s0)
                    out.append(((c0 + s0) * 128, (c0 + s0 + sn) * 128, 0, cols))
            return out
        return [(0, rows, c, min(cols, c + 512)) for c in range(0, cols, 512)]

    pieces = {k: piece_list(k) for k in wf}

    def wpiece(k, row, col):
        for i, (r0, r1, c0, c1) in enumerate(pieces[k]):
            if r0 <= row < r1 and c0 <= col < c1:
                nm = "W_%s_%d" % (k, i)
                if (k, i) not in use_seen:
                    use_seen.add((k, i))
                    use_order.append((k, i))
                return nm
        raise AssertionError

    use_seen = set()
    cast_idx = []
    order = list(piece_order) if piece_order is not None else [(k, i) for k in wf for i in range(len(pieces[k]))]
    for (k, i) in order:
        r0, r1, c0, c1 = pieces[k][i]
        n = len(cast_idx)
        ex = [cast_idx[n - 16]] if n >= 16 else []
        cast_idx.append(S.add("pool", lambda E, d=wb16[k][r0:r1, c0:c1], s=wf[k][r0:r1, c0:c1]: E.dma_start(out=d, in_=s),
                              writes=["W_%s_%d" % (k, i)], dma="cast%d" % (n % 16), extra=ex))

    def norm_to_T(gcol, src_names, tag):
        def comp(_):
            for b in range(4):
                ss = stat[:, b:b + 1]
                rs = stat[:, 8 + b:9 + b]
                S.add("dve", lambda E, o=ss: E.memset(o, 0.0), writes=["stat"])
                S.add("act", lambda E, o=xnb, i=hbuf[:, b, :], a=ss: E.activation(out=o, in_=i, func=AF.Square, accum_out=a),
                      reads=src_names, writes=["xnb", "stat"])
                S.add("dve", lambda E, o=rs, i=ss: E.tensor_scalar(out=o, in0=i, scalar1=1.0 / D, scalar2=EPS, op0=ALU.mult, op1=ALU.add),
                      reads=["stat"], writes=["stat"])
                S.add("act", lambda E, o=rs: E.sqrt(out=o, in_=o), reads=["stat"], writes=["stat"])
                S.add("dve", lambda E, o=rs: E.reciprocal(out=o, in_=o), reads=["stat"], writes=["stat"])
                S.add("act", lambda E, o=xnb, i=hbuf[:, b, :], s=rs: E.mul(out=o, in_=i, mul=s),
                      reads=src_names + ["stat"], writes=["xnb"])
                gsz = min(8, KC)
                for k0 in range(0, KC, gsz):
                    bb = alloc_bank()
                    pv = bkbf(bb)
                    for j in range(gsz):
                        S.add("pe", lambda E, o=pv[:, j * 128:(j + 1) * 128], i=xnb[:, (k0 + j) * 128:(k0 + j + 1) * 128]:
                              E.transpose(out=o, in_=i, identity=ident), reads=["xnb", "ident"], writes=[bn(bb)])
                    gsl = gT[:, gcol * KC + k0: gcol * KC + k0 + gsz].unsqueeze(2).to_broadcast([128, gsz, 128])
                    S.add("dve", lambda E, o=xnT[:, k0:k0 + gsz, b * 128:(b + 1) * 128],
                          i=pv[:, 0:gsz * 128].rearrange("p (k t) -> p k t", t=128), g=gsl:
                          E.tensor_tensor(out=o, in0=i, in1=g, op=ALU.mult),
                          reads=[bn(bb), "gT"], writes=["xnT"])
        P.stage(comp)

    def ffn(wg, wu, wd, wtag):
        Wg, Wu, Wd = wb16[wg], wb16[wu], wb16[wd]
        for (c0, nch) in parts:
            for cc in range(c0, c0 + nch, 2):
                n2 = min(2, c0 + nch - cc)
                st = {}

                def ldg(slot, cc=cc, n2=n2):
                    v = wst[slot][:, 0:KC * n2 * 128].rearrange("p (k c) -> p k c", k=KC)
                    S.add("sp", lambda E, d=v, s=Wg[:, cc * 128:(cc + n2) * 128].rearrange("(k p) c -> p k c", p=128):
                          E.dma_start(out=d, in_=s), reads=[wpiece(wg, 0, cc * 128)], writes=["wst%d" % slot], dma="wst%d" % slot)

                def ldu(slot, cc=cc, n2=n2):
                    v = wst[slot][:, 0:KC * n2 * 128].rearrange("p (k c) -> p k c", k=KC)
                    S.add("sp", lambda E, d=v, s=Wu[:, cc * 128:(cc + n2) * 128].rearrange("(k p) c -> p k c", p=128):
                          E.dma_start(out=d, in_=s), reads=[wpiece(wu, 0, cc * 128)], writes=["wst%d" % slot], dma="wst%d" % slot)

                def cg(slot, n2=n2, st=st):
                    v = wst[slot][:, 0:KC * n2 * 128].rearrange("p (k c) -> p k c", k=KC)
                    st["g"] = []
                    for ci in range(n2):
                        bb = alloc_bank()
                        st["g"].append(bb)
                        for kc in range(KC):
                            S.add("pe", lambda E, o=bk(bb), l=v[:, kc, ci * 128:(ci + 1) * 128], r=xnT[:, kc, :], a=(kc == 0), z=(kc == KC - 1):
                                  E.matmul(o, l, r, start=a, stop=z), reads=["wst%d" % slot, "xnT"], writes=[bn(bb)])

                def cu(slot, n2=n2, st=st, cc=cc, c0=c0):
                    v = wst[slot][:, 0:KC * n2 * 128].rearrange("p (k c) -> p k c", k=KC)
                    for ci in range(n2):
                        bb = alloc_bank()
                        for kc in range(KC):
                            S.add("pe", lambda E, o=bk(bb), l=v[:, kc, ci * 128:(ci + 1) * 128], r=xnT[:, kc, :], a=(kc == 0), z=(kc == KC - 1):
                                  E.matmul(o, l, r, start=a, stop=z), reads=["wst%d" % slot, "xnT"], writes=[bn(bb)])
                        gb = st["g"][ci]
                        si = rr("sil")
                        S.add("act", lambda E, o=sil[si], i=bk(gb): E.activation(out=o, in_=i, func=AF.Silu),
                              reads=[bn(gb)], writes=["sil%d" % si])
                        S.add("dve", lambda E, o=hid[:, cc - c0 + ci, :], a=sil[si], b=bk(bb): E.tensor_tensor(out=o, in0=a, in1=b, op=ALU.mult),
                              reads=["sil%d" % si, bn(bb)], writes=["hid"])

                P.stage(cg, ldg)
                P.stage(cu, ldu)
            CH = max(1, min(nch, WSLOT // 1024))
            subs = [(s0, min(CH, nch - s0)) for s0 in range(0, nch, CH)]
            for cgi in range(max(1, D // 512)):
                ncol = min(512, D)
                st = {}
                for si_, (s0, sn) in enumerate(subs):
                    def ldd(slot, s0=s0, sn=sn, cgi=cgi, c0=c0, ncol=ncol):
                        v = wst[slot][:, 0:sn * ncol].rearrange("p (k c) -> p k c", k=sn)
                        src = Wd[(c0 + s0) * 128:(c0 + s0 + sn) * 128, cgi * ncol:(cgi + 1) * ncol].rearrange("(k p) c -> p k c", p=128)
                        S.add("sp", lambda E, d=v, s=src: E.dma_start(out=d, in_=s), reads=[wpiece(wd, (c0 + s0) * 128, 0)],
                              writes=["wst%d" % slot], dma="wst%d" % slot)

                    def cd(slot, s0=s0, sn=sn, cgi=cgi, nch=nch, st=st, first=(si_ == 0), last=(si_ == len(subs) - 1), ncol=ncol):
                        v = wst[slot][:, 0:sn * ncol].rearrange("p (k c) -> p k c", k=sn)
                        if first:
                            st["b"] = [alloc_bank() for _ in range(4)]
                        for b in range(4):
                            bb = st["b"][b]
                            for k in range(sn):
                                S.add("pe", lambda E, o=bk(bb)[:, 0:ncol], l=hid[:, s0 + k, b * 128:(b + 1) * 128], r=v[:, k, :],
                                      a=(s0 + k == 0), z=(s0 + k == nch - 1): E.matmul(o, l, r, start=a, stop=z),
                                      reads=["wst%d" % slot, "hid"], writes=[bn(bb)])
                        if last:
                            for b in range(4):
                                bb = st["b"][b]
                                hs = hbuf[:, b, cgi * ncol:(cgi + 1) * ncol]
                                S.add("dve", lambda E, o=hs, i=bk(bb)[:, 0:ncol]: E.scalar_tensor_tensor(out=o, in0=i, scalar=0.5, in1=o, op0=ALU.mult, op1=ALU.add),
                                      reads=[bn(bb), "hbuf"], writes=["hbuf"])
                    P.stage(cd, ldd)

    def proj_fm(col0, ncols, kind, tcol, scr, scr0, t0e):
        Wi = wb16["win"]
        for c in range(col0, col0 + ncols, 256):
            def ld(slot, c=c):
                v = wst[slot][:, 0:KC * 256].rearrange("p (k c) -> p k c", k=KC)
                S.add("sp", lambda E, d=v, s=Wi[:, c:c + 256].rearrange("(k p) c -> p k c", p=128): E.dma_start(out=d, in_=s),
                      reads=[wpiece("win", 0, c)], writes=["wst%d" % slot], dma="wst%d" % slot)

            def cp(slot, c=c):
                v = wst[slot][:, 0:KC * 256].rearrange("p (k c) -> p k c", k=KC)
                if kind == "g":
                    oi = rr("sg")
                    obuf, oname = sgt[oi], "sg%d" % oi
                else:
                    oi = rr("xb")
                    obuf, oname = xb[oi], "xb%d" % oi
                for ci in range(2):
                    bb = alloc_bank()
                    for kc in range(KC):
                        S.add("pe", lambda E, o=bk(bb), l=v[:, kc, ci * 128:(ci + 1) * 128], r=xnT[:, kc, :], a=(kc == 0), z=(kc == KC - 1):
                              E.matmul(o, l, r, start=a, stop=z), reads=["wst%d" % slot, "xnT"], writes=[bn(bb)])
                    if kind == "g":
                        S.add("act", lambda E, o=obuf[:, ci, :], i=bk(bb): E.activation(out=o, in_=i, func=AF.Sigmoid),
                              reads=[bn(bb)], writes=[oname])
                    else:
                        S.add("act", lambda E, o=obuf[:, ci, :], i=bk(bb): E.copy(out=o, in_=i),
                              reads=[bn(bb)], writes=[oname])
                        b2 = alloc_bank()
                        ri = rr("rt")
                        S.add("pe", lambda E, o=bk(b2)[0:ROT, :], l=prot, r=obuf[0:ROT, ci, :]: E.matmul(o, l, r, start=True, stop=True),
                              reads=[oname, "prot"], writes=[bn(b2)])
                        S.add("dve", lambda E, o=rt1[ri], a=obuf[0:ROT, ci, :], b=rC: E.tensor_tensor(out=o, in0=a, in1=b, op=ALU.mult),
                              reads=[oname, "rC"], writes=["rt1%d" % ri])
                        S.add("dve", lambda E, o=rt2[ri], a=bk(b2)[0:ROT, :], b=rS: E.tensor_tensor(out=o, in0=a, in1=b, op=ALU.mult),
                              reads=[bn(b2), "rS"], writes=["rt2%d" % ri])
                        S.add("dve", lambda E, o=obuf[0:ROT, ci, :], a=rt1[ri], b=rt2[ri]: E.tensor_tensor(out=o, in0=a, in1=b, op=ALU.add),
                              reads=["rt1%d" % ri, "rt2%d" % ri], writes=[oname])
                h0 = scr0 + (c - col0) // 128
                dst = scr[h0:h0 + 2, :, tcol:tcol + T].rearrange("c p t -> p c t")
                S.add("sp", lambda E, d=dst, s=obuf: E.dma_start(out=d, in_=s), reads=[oname], dma=oname)
            P.stage(cp, ld)

    def proj_v(col0, ncols, vcol0, t0e):
        Wi = wb16["win"]
        KH = max(1, KC // 2)
        halves = [(k0, min(KH, KC - k0)) for k0 in range(0, KC, KH)]
        for c in range(col0, col0 + ncols, 512):
            st = {}
            for hi, (k0, kn) in enumerate(halves):
                def ld(slot, c=c, k0=k0, kn=kn):
                    v = wst[slot][:, 0:kn * 512].rearrange("p (k c) -> p k c", k=kn)
                    S.add("sp", lambda E, d=v, s=Wi[k0 * 128:(k0 + kn) * 128, c:c + 512].rearrange("(k p) c -> p k c", p=128): E.dma_start(out=d, in_=s),
                          reads=[wpiece("win", 0, c)], writes=["wst%d" % slot], dma="wst%d" % slot)

                def cp(slot, c=c, k0=k0, kn=kn, st=st, first=(hi == 0), last=(hi == len(halves) - 1)):
                    v = wst[slot][:, 0:kn * 512].rearrange("p (k c) -> p k c", k=kn)
                    if first:
                        st["b"] = [alloc_bank() for _ in range(4)]
                        st["o"] = rr("vb")
                    oi = st["o"]
                    for b in range(4):
                        bb = st["b"][b]
                        for k in range(kn):
                            S.add("pe", lambda E, o=bk(bb), l=xnT[:, k0 + k, b * 128:(b + 1) * 128], r=v[:, k, :], a=(k0 + k == 0), z=(k0 + k == KC - 1):
                                  E.matmul(o, l, r, start=a, stop=z), reads=["wst%d" % slot, "xnT"], writes=[bn(bb)])
                    if last:
                        for b in range(4):
                            bb = st["b"][b]
                            S.add("act", lambda E, o=vb[oi][:, b, :], i=bk(bb): E.copy(out=o, in_=i), reads=[bn(bb)], writes=["vb%d" % oi])
                        vc = vcol0 + (c - col0)
                        dst = V_scr[t0e:t0e + T, vc:vc + 512].rearrange("(b p) c -> p b c", p=128)
                        S.add("sp", lambda E, d=dst, s=vb[oi]: E.dma_start(out=d, in_=s), reads=["vb%d" % oi], dma="vb%d" % oi)
                P.stage(cp, ld)

    for ti in range(NT_EXT):
        t0e = ti * T
        own = OWN_T0 <= ti < OWN_T0 + NT_OWN
        tcol = (ti - OWN_T0) * T

        def ldx(_, t0e=t0e):
            S.add("sp", lambda E, d=hbuf, s=xe[t0e:t0e + T, :].rearrange("(b p) d -> p b d", p=128): E.dma_start(out=d, in_=s),
                  writes=HB_ALL, dma="hbuf")
        P.stage(ldx)
        norm_to_T(0, ["hbuf"], "n1")
        ffn("w1g", "w1u", "w1d", "f1")
        if own:
            def sth(_, tcol=tcol):
                S.add("sp", lambda E, d=h_scr[tcol:tcol + T, :].rearrange("(b p) d -> p b d", p=128), s=hbuf: E.dma_start(out=d, in_=s),
                      reads=HB_ALL, dma="hst")
            P.stage(sth)
        norm_to_T(1, HB_ALL, "nm")

        def ldrope(_, t0e=t0e):
            S.add("sp", lambda E, d=rC, s=ropeC_in[:, t0e:t0e + T]: E.dma_start(out=d, in_=s), writes=["rC"], dma="rC")
            S.add("sp", lambda E, d=rS, s=ropeS_in[:, t0e:t0e + T]: E.dma_start(out=d, in_=s), writes=["rS"], dma="rS")
        P.stage(ldrope)
        far = ti == 0 or ti == NT_EXT - 1
        if far:
            proj_fm(3072 + 2048, 1024, "k", t0e, K_scr, 16, t0e)
            proj_v(6144 + 2048, 1024, 2048, t0e)
        else:
            proj_fm(3072, 3072, "k", t0e, K_scr, 0, t0e)
            proj_fm(11264, 512, "k", t0e, K_scr, 24, t0e)
            proj_v(6144, 3072, 0, t0e)
            proj_v(11776, 512, 3072, t0e)
        if own:
            proj_fm(0, 3072, "q", tcol, Q_scr, 0, t0e)
            proj_fm(9216, 2048, "q", tcol, Q_scr, 24, t0e)
            proj_fm(12288, 2 * D, "g", tcol, G_scr, 0, t0e)
    P.flush()
    S.barrier()

    acur = [SMALL_END]

    def acarve(nbytes, dt):
        o = acur[0]
        acur[0] += (nbytes + 63) // 64 * 64
        assert acur[0] <= ARENA
        return carve(o, nbytes, dt)

    NSET = 8
    kTb = [acarve(EXT * 2, BF16) for _ in range(2)]
    qTb = [acarve(OWN * 2, BF16) for _ in range(2)]
    vsb = [acarve(80 * 128 * 2, BF16) for _ in range(2)]
    sA = acarve(EXT * 2, BF16)[0:3, :]
    sB = acarve(EXT * 2, BF16)[0:3, :]
    bandA = acarve(192 * 4, F32)[0:64, :]
    bandB = acarve(384 * 4, F32)
    Smb = [acarve(384 * 4, F32) for _ in range(NSET)]
    Pb = [acarve(384 * 2, BF16) for _ in range(NSET)]
    PTb = [acarve(384 * 2, BF16) for _ in range(NSET)]
    ast = acarve(1024, F32)
    nsink = acarve(64, F32)
    amark = acur[0]
    obufs = [acarve(48 * OW * 4, F32) for _ in range(2)]

    S.add("sp", lambda E: E.dma_start(out=sA, in_=seqA_in), writes=["sA"], dma="sA")
    S.add("sp", lambda E: E.dma_start(out=sB, in_=seqB_in), writes=["sB"], dma="sB")
    S.add("sp", lambda E: E.dma_start(out=bandA, in_=bandA_in), writes=["bandA"], dma="bandA")
    S.add("sp", lambda E: E.dma_start(out=bandB, in_=bandB_in), writes=["bandB"], dma="bandB")
    S.add("dve", lambda E: E.tensor_scalar(out=nsink[:, 0:16], in0=sink[:, 0:16], scalar1=-1.0, scalar2=None, op0=ALU.mult),
          reads=["sink"], writes=["nsink"])
    scale = 1.0 / math.sqrt(HD)
    uctr = [0]

    def attn_batch(units, QN, NKEY, band, bname, is_b):
        NB = 3
        KB = NKEY // NB
        R = []
        for u in units:
            k = uctr[0] % NSET
            sc = (uctr[0] % 16) * 8
            uctr[0] += 1
            b1 = alloc_bank()
            b2 = alloc_bank() if is_b else b1
            R.append((k, sc, b1, b2))

        def regs(b1, b2):
            if is_b:
                return bk(b1)[0:QN, 0:NKEY], bk(b1)[0:128, NKEY:NKEY + 128], bkbf(b2)[0:KB, 0:NB * QN]
            return bk(b1)[0:QN, 0:NKEY], bk(b1)[0:QN, NKEY:NKEY + 128], bkbf(b1)[0:KB, 2 * (NKEY + 128):2 * (NKEY + 128) + NB * QN]

        for u, (k, sc, b1, b2) in zip(units, R):
            sps, ops_, tps = regs(b1, b2)
            S.add("pe", lambda E, o=sps, l=u["qsl"], r=u["ksl"]: E.matmul(o, l, r, start=True, stop=False),
                  reads=[u["kname"], u["qname"]], writes=[bn(b1)])
            S.add("pe", lambda E, o=sps, l=u["qtok"], r=u["ktok"]: E.matmul(o, l, r, start=False, stop=True),
                  reads=["sA", "sB"], writes=[bn(b1)])
        for u, (k, sc, b1, b2) in zip(units, R):
            sps, ops_, tps = regs(b1, b2)
            S.add("dve", lambda E, o=Smb[k][0:QN, 0:NKEY], i=sps, b=band: E.scalar_tensor_tensor(out=o, in0=i, scalar=scale, in1=b, op0=ALU.mult, op1=ALU.add),
                  reads=[bn(b1), bname], writes=["Sm%d" % k])
        for u, (k, sc, b1, b2) in zip(units, R):
            negm = ast[0:QN, sc:sc + 1]
            S.add("dve", lambda E, o=negm, i=Smb[k][0:QN, 0:NKEY]: E.reduce_max(out=o, in_=i, axis=AX.X, negate=True),
                  reads=["Sm%d" % k], writes=["ast%d" % sc])
            if is_b:
                S.add("dve", lambda E, o=negm, i=nsink[0:QN, u["hq"]:u["hq"] + 1]: E.tensor_tensor(out=o, in0=o, in1=i, op=ALU.min),
                      reads=["nsink"], writes=["ast%d" % sc])
            S.add("dve", lambda E, o=ast[0:QN, sc + 1:sc + 2]: E.memset(o, 0.0), writes=["astd%d" % sc])
        for u, (k, sc, b1, b2) in zip(units, R):
            negm = ast[0:QN, sc:sc + 1]
            den = ast[0:QN, sc + 1:sc + 2]
            S.add("act", lambda E, o=Pb[k][0:QN, 0:NKEY], i=Smb[k][0:QN, 0:NKEY], b=negm, a=den: E.activation(out=o, in_=i, func=AF.Exp, bias=b, scale=1.0, accum_out=a),
                  reads=["Sm%d" % k, "ast%d" % sc], writes=["P%d" % k, "astd%d" % sc])
            if is_b:
                S.add("act", lambda E, o=ast[0:QN, sc + 3:sc + 4], i=sink[0:QN, u["hq"]:u["hq"] + 1], b=negm: E.activation(out=o, in_=i, func=AF.Exp, bias=b, scale=1.0),
                      reads=["ast%d" % sc, "sink"], writes=["aste%d" % sc])
        for u, (k, sc, b1, b2) in zip(units, R):
            den = ast[0:QN, sc + 1:sc + 2]
            rden = ast[0:QN, sc + 2:sc + 3]
            if is_b:
                S.add("dve", lambda E, o=den, b=ast[0:QN, sc + 3:sc + 4]: E.tensor_tensor(out=o, in0=o, in1=b, op=ALU.add),
                      reads=["aste%d" % sc], writes=["astd%d" % sc])
            S.add("dve", lambda E, o=rden, i=den: E.reciprocal(out=o, in_=i), reads=["astd%d" % sc], writes=["astr%d" % sc])
            if is_b:
                S.add("dve", lambda E, o=Pb[k][0:QN, 0:NKEY], s=rden: E.tensor_scalar(out=o, in0=o, scalar1=s, scalar2=None, op0=ALU.mult),
                      reads=["astr%d" % sc], writes=["P%d" % k])
            else:
                S.add("act", lambda E, o=ast[0:QN, sc + 3:sc + 4], i=den: E.activation(out=o, in_=i, func=AF.Ln),
                      reads=["astd%d" % sc], writes=["aste%d" % sc])
        for u, (k, sc, b1, b2) in zip(units, R):
            sps, ops_, tps = regs(b1, b2)
            for b in range(NB):
                S.add("pe", lambda E, o=tps[:, b * QN:(b + 1) * QN], i=Pb[k][0:QN, b * KB:(b + 1) * KB]:
                      E.transpose(out=o, in_=i, identity=ident[0:QN, 0:QN]), reads=["P%d" % k, "ident"], writes=[bn(b2)])
        for u, (k, sc, b1, b2) in zip(units, R):
            sps, ops_, tps = regs(b1, b2)
            S.add("act", lambda E, o=PTb[k][0:KB, 0:NB * QN], i=tps: E.copy(out=o, in_=i), reads=[bn(b2)], writes=["PT%d" % k])
            if not is_b:
                S.add("pool", lambda E, o=u["lse"], a=ast[0:QN, sc + 3:sc + 4], b=ast[0:QN, sc:sc + 1]: E.tensor_tensor(out=o, in0=a, in1=b, op=ALU.subtract),
                      reads=["aste%d" % sc, "ast%d" % sc], writes=[u["oname"] + "l"])
        for u, (k, sc, b1, b2) in zip(units, R):
            sps, ops_, tps = regs(b1, b2)
            for b in range(NB):
                if is_b:
                    S.add("pe", lambda E, o=ops_, l=u["vblk"][b], r=PTb[k][0:KB, b * QN:(b + 1) * QN], a=(b == 0), z=(b == NB - 1):
                          E.matmul(o, l, r, start=a, stop=z), reads=["PT%d" % k, u["vname"]], writes=[bn(b1)])
                else:
                    S.add("pe", lambda E, o=ops_, l=PTb[k][0:KB, b * QN:(b + 1) * QN], r=u["vblk"][b], a=(b == 0), z=(b == NB - 1):
                          E.matmul(o, l, r, start=a, stop=z), reads=["PT%d" % k, u["vname"]], writes=[bn(b1)])
        for u, (k, sc, b1, b2) in zip(units, R):
            sps, ops_, tps = regs(b1, b2)
            if is_b:
                S.add("act", lambda E, o=u["out"], i=ops_: E.copy(out=o, in_=i), reads=[bn(b1)], writes=[u["oname"]])
            else:
                S.add("dve", lambda E, o=u["out"], i=ops_, s=ast[0:QN, sc + 2:sc + 3]: E.tensor_scalar(out=o, in0=i, scalar1=s, scalar2=None, op0=ALU.mult),
                      reads=[bn(b1), "astr%d" % sc], writes=[u["oname"]])

    GB = 4
    for g, r in enumerate(DIL):
        nb = EXT // (64 * r)
        nt = OWN // (64 * r)
        blk_base = HALO // (64 * r) - 1
        for h in range(8):
            hh = g * 8 + h
            s2 = hh % 2
            kT, qT, vs, ob = kTb[s2], qTb[s2], vsb[s2], obufs[s2]
            kname, qname, vname, oname = "kT%d" % s2, "qT%d" % s2, "vs%d" % s2, "ob%d" % s2
            cut = 0 if g == 2 else T
            S.add("sp", lambda E, d=kT[:, cut:EXT - cut], s=K_scr[hh][:, cut:EXT - cut]: E.dma_start(out=d, in_=s), writes=[kname], dma=kname)
            S.add("sp", lambda E, d=qT, s=Q_scr[hh]: E.dma_start(out=d, in_=s), writes=[qname], dma=qname)
            vb0 = cut // (64 * r)
            vv = vs[0:64, :].rearrange("p (j b d) -> p j b d", j=r, b=nb)
            if cut == 0:
                vsrc = bass.AP(V_scr.tensor, g * 1024 + h * 128, [[r * VC, 64], [VC, r], [r * 64 * VC, nb], [1, 128]])
                S.add("sp", lambda E, d=vv, s=vsrc: E.dma_start(out=d, in_=s), writes=[vname], dma=vname)
            else:
                for j in range(r):
                    vsrc = bass.AP(V_scr.tensor, g * 1024 + h * 128 + j * VC + vb0 * r * 64 * VC, [[r * VC, 64], [r * 64 * VC, nb - 2 * vb0], [1, 128]])
                    S.add("sp", lambda E, d=vv[:, j, vb0:nb - vb0, :], s=vsrc: E.dma_start(out=d, in_=s), writes=[vname], dma=vname)
            obv = ob[0:64, :].rearrange("p (j n c) -> p j n c", j=r, n=nt)
            units = []
            for j in range(r):
                for n in range(nt):
                    q0 = r * 64 * n + j
                    k0 = HALO + r * 64 * (n - 1) + j

                    def ss(a, cnt, r=r):
                        return slice(a, a + (cnt - 1) * r + 1, r)
                    qe = HALO + q0
                    units.append(dict(qsl=qT[:, ss(q0, 64)], ksl=kT[:, ss(k0, 192)], qtok=sA[:, ss(qe, 64)], ktok=sB[:, ss(k0, 192)],
                                      kname=kname, qname=qname, vname=vname, oname=oname,
                                      vblk=[vv[:, j, blk_base + n + b, :] for b in range(3)],
                                      out=obv[:, j, n, 0:128], lse=obv[:, j, n, 128:129]))
            for i0 in range(0, len(units), GB):
                attn_batch(units[i0:i0 + GB], 64, 192, bandA, "bandA", False)
            odst = bass.AP(O_scr.tensor, hh * OW, [[r * 24 * OW, 64], [24 * OW, r], [64 * r * 24 * OW, nt], [1, OW]])
            S.add("sp", lambda E, d=odst, s=obv: E.dma_start(out=d, in_=s), reads=[oname, oname + "l"], writes=["O_scr"], dma="ost")
    S.barrier()
    acur[0] = amark
    ybufs = [acarve(OWN * 2, BF16) for _ in range(2)]
    obs = [acarve(24 * OW * 4, F32) for _ in range(2)]
    wts = acarve(64 * 4, F32)
    yaf = acarve(1024 * 4, F32)
    yab = acarve(1024 * 2, BF16)
    yaTb = [acarve(8 * 128 * 2, BF16) for _ in range(2)]
    for tb in range(OWN // 128):
        s2 = tb % 2
        ob = obs[s2].rearrange("p (h c) -> p h c", c=OW)
        S.add("sp", lambda E, d=ob, s=O_scr[tb * 128:(tb + 1) * 128]: E.dma_start(out=d, in_=s), reads=["O_scr"], writes=["obs%d" % s2], dma="obs%d" % s2)
        lv = ob[:, :, 128].rearrange("p (g h) -> p g h", g=3)
        M = wts[:, 0:8]
        w = wts[:, 8:32].rearrange("p (g h) -> p g h", g=3)
        ws = wts[:, 32:40]
        S.add("dve", lambda E, o=M, a=lv[:, 0, :], b=lv[:, 1, :]: E.tensor_tensor(out=o, in0=a, in1=b, op=ALU.max), reads=["obs%d" % s2], writes=["wts"])
        S.add("dve", lambda E, o=M, b=lv[:, 2, :]: E.tensor_tensor(out=o, in0=o, in1=b, op=ALU.max), reads=["obs%d" % s2], writes=["wts"])
        S.add("dve", lambda E, o=w, a=lv, b=M.unsqueeze(1).to_broadcast([128, 3, 8]): E.tensor_tensor(out=o, in0=a, in1=b, op=ALU.subtract),
              reads=["obs%d" % s2], writes=["wts"])
        S.add("act", lambda E, o=wts[:, 8:32]: E.activation(out=o, in_=o, func=AF.Exp), reads=["wts"], writes=["wts"])
        S.add("dve", lambda E, o=ws, a=w[:, 0, :], b=w[:, 1, :]: E.tensor_tensor(out=o, in0=a, in1=b, op=ALU.add), reads=["wts"], writes=["wts"])
        S.add("dve", lambda E, o=ws, b=w[:, 2, :]: E.tensor_tensor(out=o, in0=o, in1=b, op=ALU.add), writes=["wts"])
        S.add("dve", lambda E, o=ws: E.reciprocal(out=o, in_=o), writes=["wts"])
        S.add("dve", lambda E, o=w, b=ws.unsqueeze(1).to_broadcast([128, 3, 8]): E.tensor_tensor(out=o, in0=o, in1=b, op=ALU.mult), writes=["wts"])
        yv = yaf.rearrange("p (h d) -> p h d", d=128)
        for gi in range(3):
            og = ob[:, gi * 8:(gi + 1) * 8, 0:128]
            wg_ = w[:, gi, :].unsqueeze(2).to_broadcast([128, 8, 128])
            if gi == 0:
                S.add("dve", lambda E, o=yv, a=og, b=wg_: E.tensor_tensor(out=o, in0=a, in1=b, op=ALU.mult), reads=["obs%d" % s2, "wts"], writes=["yaf"])
            else:
                S.add("pool", lambda E, o=og, a=og, b=wg_: E.tensor_tensor(out=o, in0=a, in1=b, op=ALU.mult), reads=["wts"], writes=["obs%d" % s2])
                S.add("dve", lambda E, o=yv, a=yv, b=og: E.tensor_tensor(out=o, in0=a, in1=b, op=ALU.add), reads=["obs%d" % s2], writes=["yaf"])
        S.add("act", lambda E, o=yab, i=yaf: E.copy(out=o, in_=i), reads=["yaf"], writes=["yab"])
        tb_ = alloc_bank()
        for hq in range(8):
            S.add("pe", lambda E, o=bkbf(tb_)[:, hq * 128:(hq + 1) * 128], i=yab[:, hq * 128:(hq + 1) * 128]: E.transpose(out=o, in_=i, identity=ident),
                  reads=["yab", "ident"], writes=[bn(tb_)])
        yTv = yaTb[s2].rearrange("p (h t) -> p h t", h=8)
        S.add("dve", lambda E, o=yaTb[s2], i=bkbf(tb_): E.tensor_copy(out=o, in_=i),
              reads=[bn(tb_)], writes=["yaT%d" % s2])
        S.add("sp", lambda E, d=YA_scr[:, :, tb * 128:(tb + 1) * 128].rearrange("h p t -> p h t"), s=yTv: E.dma_start(out=d, in_=s),
              reads=["yaT%d" % s2], dma="yaT%d" % s2)
    for hq in range(16):
        kv = hq // 4
        s2 = hq % 2
        qT, yb = qTb[s2], ybufs[s2]
        if hq % 4 == 0:
            kT, vs = kTb[kv % 2], vsb[kv % 2]
            kname, vname = "kT%d" % (kv % 2), "vs%d" % (kv % 2)
            S.add("sp", lambda E, d=kT[:, T:EXT - T], s=K_scr[24 + kv][:, T:EXT - T]: E.dma_start(out=d, in_=s), writes=[kname], dma=kname)
            vvB = vs[:, 0:40 * 128].rearrange("p (b d) -> p b d", d=128)
            S.add("sp", lambda E, d=vvB[:, 4:36, :], s=V_scr[T:EXT - T, 3072 + kv * 128:3072 + (kv + 1) * 128].rearrange("(b p) d -> p b d", p=128): E.dma_start(out=d, in_=s),
                  writes=[vname], dma=vname)
        qname = "qT%d" % s2
        S.add("sp", lambda E, d=qT, s=Q_scr[24 + hq]: E.dma_start(out=d, in_=s), writes=[qname], dma=qname)
        units = []
        for n in range(OWN // 128):
            q0 = 128 * n
            k0 = HALO + 128 * (n - 1)
            units.append(dict(qsl=qT[:, q0:q0 + 128], ksl=kT[:, k0:k0 + 384], qtok=sA[:, HALO + q0:HALO + q0 + 128], ktok=sB[:, k0:k0 + 384],
                              kname=kname, qname=qname, vname=vname, oname="yb%d" % s2, hq=hq,
                              vblk=[vvB[:, HALO // 128 + n - 1 + b, :] for b in range(3)], out=yb[:, q0:q0 + 128]))
        for i0 in range(0, len(units), GB):
            attn_batch(units[i0:i0 + GB], 128, 384, bandB, "bandB", True)
        S.add("sp", lambda E, d=YB_scr[hq], s=yb: E.dma_start(out=d, in_=s), reads=["yb%d" % s2], dma="yb%d" % s2)
    S.barrier()

    Wa, Wb_, Wo = wb16["wa"], wb16["wb"], wb16["wo"]
    for to in range(NT_OWN):
        tcol = to * T

        def ldy(_, tcol=tcol):
            S.add("sp", lambda E, d=yT[:, 0:8, :], s=YA_scr[:, :, tcol:tcol + T].rearrange("h p t -> p h t"): E.dma_start(out=d, in_=s),
                  writes=["xnb", "gfin"], dma="yT")
            S.add("sp", lambda E, d=yT[:, 8:24, :], s=YB_scr[:, :, tcol:tcol + T].rearrange("h p t -> p h t"): E.dma_start(out=d, in_=s),
                  writes=["xnb", "gfin"], dma="yT")
        P.stage(ldy)
        for dc0 in range(0, KC, 2):
            nd = min(2, KC - dc0)

            def ldb(slot, dc0=dc0, nd=nd):
                va = wst[slot][:, 0:8 * nd * 128].rearrange("p (k c) -> p k c", k=8)
                vbw = wst[slot][:, 8 * 256:8 * 256 + 16 * nd * 128].rearrange("p (k c) -> p k c", k=16)
                S.add("sp", lambda E, d=va, s=Wa[:, dc0 * 128:(dc0 + nd) * 128].rearrange("(k p) c -> p k c", p=128): E.dma_start(out=d, in_=s),
                      reads=[wpiece("wa", 0, dc0 * 128)], writes=["wst%d" % slot], dma="wst%d" % slot)
                S.add("sp", lambda E, d=vbw, s=Wb_[:, dc0 * 128:(dc0 + nd) * 128].rearrange("(k p) c -> p k c", p=128): E.dma_start(out=d, in_=s),
                      reads=[wpiece("wb", 0, dc0 * 128)], writes=["wst%d" % slot], dma="wst%d" % slot)

            def cb(slot, dc0=dc0, nd=nd, tcol=tcol):
                va = wst[slot][:, 0:8 * nd * 128].rearrange("p (k c) -> p k c", k=8)
                vbw = wst[slot][:, 8 * 256:8 * 256 + 16 * nd * 128].rearrange("p (k c) -> p k c", k=16)
                for ci in range(nd):
                    dc = dc0 + ci
                    gi = rr("sg")
                    gsrc = G_scr.rearrange("(a c) p t -> p a c t", a=2)[:, :, dc, tcol:tcol + T]
                    S.add("sp", lambda E, d=sgt[gi], s=gsrc: E.dma_start(out=d, in_=s), writes=["sg%d" % gi], dma="sg%d" % gi)
                    ba = alloc_bank()
                    for kc in range(8):
                        S.add("pe", lambda E, o=bk(ba), l=va[:, kc, ci * 128:(ci + 1) * 128], r=yT[:, kc, :], a=(kc == 0), z=(kc == 7):
                              E.matmul(o, l, r, start=a, stop=z), reads=["wst%d" % slot, "xnb", "gfin"], writes=[bn(ba)])
                    bb = alloc_bank()
                    for kc in range(16):
                        S.add("pe", lambda E, o=bk(bb), l=vbw[:, kc, ci * 128:(ci + 1) * 128], r=yT[:, 8 + kc, :], a=(kc == 0), z=(kc == 15):
                              E.matmul(o, l, r, start=a, stop=z), reads=["wst%d" % slot, "xnb", "gfin"], writes=[bn(bb)])
                    m1, m2 = rr("mt"), rr("sil")
                    S.add("dve", lambda E, o=mt[m1], a=bk(ba), b=sgt[gi][:, 0, :]: E.tensor_tensor(out=o, in0=a, in1=b, op=ALU.mult),
                          reads=[bn(ba), "sg%d" % gi], writes=["mt%d" % m1])
                    S.add("dve", lambda E, o=sil[m2], a=bk(bb), b=sgt[gi][:, 1, :]: E.tensor_tensor(out=o, in0=a, in1=b, op=ALU.mult),
                          reads=[bn(bb), "sg%d" % gi], writes=["sil%d" % m2])
                    S.add("pool", lambda E, o=xnT[:, dc, :], a=mt[m1], b=sil[m2]: E.tensor_tensor(out=o, in0=a, in1=b, op=ALU.add),
                          reads=["mt%d" % m1, "sil%d" % m2], writes=["xnT"])
            P.stage(cb, ldb)

        def ldh(_, tcol=tcol):
            S.add("sp", lambda E, d=hbuf, s=h_scr[tcol:tcol + T, :].rearrange("(b p) d -> p b d", p=128): E.dma_start(out=d, in_=s),
                  writes=HB_ALL, dma="hbuf")
        P.stage(ldh)
        KH = max(1, KC // 2)
        halves = [(k0, min(KH, KC - k0)) for k0 in range(0, KC, KH)]
        for c in range(0, D, 512):
            st = {}
            for hi, (k0, kn) in enumerate(halves):
                def ldo(slot, c=c, k0=k0, kn=kn):
                    v = wst[slot][:, 0:kn * 512].rearrange("p (k c) -> p k c", k=kn)
                    S.add("sp", lambda E, d=v, s=Wo[k0 * 128:(k0 + kn) * 128, c:c + 512].rearrange("(k p) c -> p k c", p=128): E.dma_start(out=d, in_=s),
                          reads=[wpiece("wo", 0, c)], writes=["wst%d" % slot], dma="wst%d" % slot)

                def co(slot, c=c, k0=k0, kn=kn, st=st, first=(hi == 0), last=(hi == len(halves) - 1)):
                    v = wst[slot][:, 0:kn * 512].rearrange("p (k c) -> p k c", k=kn)
                    if first:
                        st["b"] = [alloc_bank() for _ in range(4)]
                    for b in range(4):
                        bb = st["b"][b]
                        for k in range(kn):
                            S.add("pe", lambda E, o=bk(bb), l=xnT[:, k0 + k, b * 128:(b + 1) * 128], r=v[:, k, :], a=(k0 + k == 0), z=(k0 + k == KC - 1):
                                  E.matmul(o, l, r, start=a, stop=z), reads=["wst%d" % slot, "xnT"], writes=[bn(bb)])
                    if last:
                        for b in range(4):
                            bb = st["b"][b]
                            hs = hbuf[:, b, c:c + 512]
                            S.add("dve", lambda E, o=hs, i=bk(bb): E.tensor_tensor(out=o, in0=i, in1=o, op=ALU.add),
                                  reads=[bn(bb), "hbuf"], writes=["hbuf"])
                P.stage(co, ldo)
        norm_to_T(2, ["hbuf"], "n2")
        ffn("w2g", "w2u", "w2d", "f2")

        def fin(_, tcol=tcol):
            S.add("sp", lambda E, d=gfin, s=gfin_in.partition_broadcast(128): E.dma_start(out=d, in_=s), writes=["gfin"], dma="gfin")
            for b in range(4):
                ss = stat[:, 16 + b:17 + b]
                rs = stat[:, 24 + b:25 + b]
                S.add("dve", lambda E, o=ss: E.memset(o, 0.0), writes=["stat"])
                S.add("act", lambda E, o=xnb, i=hbuf[:, b, :], a=ss: E.activation(out=o, in_=i, func=AF.Square, accum_out=a),
                      reads=["hbuf"], writes=["xnb", "stat"])
                S.add("dve", lambda E, o=rs, i=ss: E.tensor_scalar(out=o, in0=i, scalar1=1.0 / D, scalar2=EPS, op0=ALU.mult, op1=ALU.add),
                      reads=["stat"], writes=["stat"])
                S.add("act", lambda E, o=rs: E.sqrt(out=o, in_=o), reads=["stat"], writes=["stat"])
                S.add("dve", lambda E, o=rs: E.reciprocal(out=o, in_=o), reads=["stat"], writes=["stat"])
                S.add("act", lambda E, o=hbuf[:, b, :], s=rs: E.mul(out=o, in_=o, mul=s),
                      reads=["hbuf", "stat"], writes=["hbuf"])
                S.add("dve", lambda E, o=hbuf[:, b, :], g=gfin: E.tensor_tensor(out=o, in0=o, in1=g, op=ALU.mult),
                      reads=["hbuf", "gfin"], writes=["hbuf"])
            S.add("sp", lambda E, d=y_out[tcol:tcol + T, :].rearrange("(b p) d -> p b d", p=128), s=hbuf: E.dma_start(out=d, in_=s),
                  reads=HB_ALL, dma="yst")
        P.stage(fin)
    P.flush()
    S.emit(nc, stack)
    stack.close()
    return nc


def _host_tables(seq_lens):
    tot = sum(seq_lens)
    seq_id = np.concatenate([np.full(n, i, np.int64) for i, n in enumerate(seq_lens)])
    pos = np.concatenate([np.arange(n, dtype=np.int64) for n in seq_lens])
    half = ROT // 2
    inv = (THETA ** (-np.arange(half, dtype=np.float32) / np.float32(half))).astype(np.float32)
    out = []
    for c in range(N_CORES):
        g0 = c * OWN - HALO
        idx = np.arange(g0, g0 + EXT)
        valid = (idx >= 0) & (idx < tot)
        ci = np.clip(idx, 0, tot - 1)
        sid = np.where(valid, seq_id[ci], 3)
        p = np.where(valid, pos[ci], 0).astype(np.float32)
        ang = p[None, :] * inv[:, None]
        cs, sn = np.cos(ang).astype(np.float32), np.sin(ang).astype(np.float32)
        ropeC = np.concatenate([cs, cs], 0)
        ropeS = np.concatenate([-sn, sn], 0)
        onehot = (sid[None, :] == np.arange(3)[:, None])
        seqA = onehot.astype(np.float32).astype(ml_dtypes.bfloat16)
        seqB = (np.float32(NEGBIG) * (1.0 - onehot.astype(np.float32))).astype(ml_dtypes.bfloat16)
        out.append(dict(ropeC=np.ascontiguousarray(ropeC), ropeS=np.ascontiguousarray(ropeS), seqA=seqA, seqB=seqB, idx=idx, valid=valid))
    return out


def _consts():
    i = np.arange(64)[:, None]
    k = np.arange(192)[None, :]
    bandA = np.where(np.abs(k - 64 - i) <= 64, 0.0, NEGBIG).astype(np.float32)
    i = np.arange(128)[:, None]
    k = np.arange(384)[None, :]
    bandB = np.where(np.abs(k - 128 - i) <= 128, 0.0, NEGBIG).astype(np.float32)
    ident = np.eye(128, dtype=np.float32).astype(ml_dtypes.bfloat16)
    prot = np.zeros((ROT, ROT), np.float32)
    for m in range(ROT):
        prot[(m + 16) % ROT, m] = 1.0
    return bandA, bandB, ident, prot.astype(ml_dtypes.bfloat16)


_NC_CACHE = {}


def run(cfg, x_prompt, x_sample, g_ffn1, w1_gate, w1_up, w1_down, g_mix, w_in, sink_b, w_branch_a, w_branch_b,
        w_out, g_ffn2, w2_gate, w2_up, w2_down, g_final, trace=False):
    D = cfg.D
    f = lambda a: np.ascontiguousarray(np.asarray(a, dtype=np.float32))
    xp, xs = f(x_prompt), f(x_sample)
    seq_lens = [xp.shape[1]] * xp.shape[0] + [xs.shape[1]] * xs.shape[0]
    X = np.concatenate([xp.reshape(-1, D), xs.reshape(-1, D)], 0)
    tot = X.shape[0]
    assert tot == N_CORES * OWN
    tabs = _host_tables(seq_lens)
    bandA, bandB, ident, prot = _consts()
    KC = cfg.KC
    gT = np.concatenate([f(g)[0].reshape(KC, 128).T for g in (g_ffn1, g_mix, g_ffn2)], 1)
    shared = dict(
        w1g=f(w1_gate)[0], w1u=f(w1_up)[0], w1d=f(w1_down)[0], win=f(w_in)[0], wa=f(w_branch_a)[0], wb=f(w_branch_b)[0],
        wo=f(w_out)[0], w2g=f(w2_gate)[0], w2u=f(w2_up)[0], w2d=f(w2_down)[0],
        gT=np.ascontiguousarray(gT), gfin=f(g_final), sink=f(sink_b)[0],
        bandA=bandA, bandB=bandB, ident=ident, prot=prot,
    )
    in_maps = []
    for c in range(N_CORES):
        t = tabs[c]
        xe = np.zeros((EXT, D), np.float32)
        xe[t["valid"]] = X[t["idx"][t["valid"]]]
        m = dict(shared)
        m.update(xe=xe, ropeC=t["ropeC"], ropeS=t["ropeS"], seqA=t["seqA"], seqB=t["seqB"])
        in_maps.append(m)
    key = (cfg.D, cfg.F, cfg.HP)
    if key not in _NC_CACHE:
        uo = []
        build(cfg, None, uo)
        _NC_CACHE[key] = build(cfg, uo)
    nc = _NC_CACHE[key]
    res = run_bass_kernel_spmd(nc, in_maps, core_ids=list(range(N_CORES)), trace=trace)
    Y = np.concatenate([np.asarray(r["y"]) for r in res.results], 0).astype(np.float32)
    n0 = xp.shape[0] * xp.shape[1]
    return (Y[:n0].reshape(xp.shape), Y[n0:].reshape(xs.shape)), res


def kernel(**inputs):
    out, _ = run(Cfg(), **inputs)
    return out
```

```python
import math
from contextlib import ExitStack

import numpy as np
import ml_dtypes

import concourse.bass as bass
import concourse.mybir as mybir
from concourse.bass_utils import run_bass_kernel_spmd

F32 = mybir.dt.float32
BF16 = mybir.dt.bfloat16
AF = mybir.ActivationFunctionType
ALU = mybir.AluOpType
AX = mybir.AxisListType

N_CORES = 8
OWN = 3072
HALO = 1024
EXT = OWN + 2 * HALO
T = 512
NT_EXT = EXT // T
OWN_T0 = HALO // T
NT_OWN = OWN // T
HD = 128
ROT = 32
THETA = 500000.0
EPS = 1e-6
NEGBIG = -30000.0
DIL = (1, 4, 16)
NQ = 40
NK = 28
VC = 3584
OW = 129


class Cfg:
    def __init__(self, D=4096, F=11008, HP=4):
        self.D, self.F, self.HP = D, F, HP
        self.KC = D // 128
        self.FC = F // 128
        self.IN_W = 12288 + 2 * D


class _Op:
    __slots__ = ("eng", "fn", "dma", "cum", "deps", "signal", "sigval")


class Sched:
    ENGS = ("sp", "act", "dve", "pool", "pe")

    def __init__(self):
        self.ops = []
        self.last_w = {}
        self.readers = {}
        self.dma_cum = {}
        self.last_on = {}

    def add(self, eng, fn, reads=(), writes=(), dma=None, extra=()):
        op = _Op()
        op.eng, op.fn, op.dma, op.signal, op.sigval = eng, fn, dma, False, 0
        idx = len(self.ops)
        deps = set(extra)
        for r in reads:
            w = self.last_w.get(r)
            if w is not None:
                deps.add(w)
        for r in writes:
            w = self.last_w.get(r)
            if w is not None:
                deps.add(w)
            rd = self.readers.get(r)
            if rd:
                deps.update(rd.values())
        if dma is not None:
            self.dma_cum[dma] = self.dma_cum.get(dma, 0) + 16
            op.cum = self.dma_cum[dma]
        else:
            op.cum = 0
        rkey = ("dma", dma) if dma is not None else eng
        for r in reads:
            self.readers.setdefault(r, {})[rkey] = idx
        for r in writes:
            self.last_w[r] = idx
            self.readers[r] = {}
        fd = []
        for j in deps:
            o = self.ops[j]
            if o.dma is not None:
                fd.append(j)
            elif o.eng == eng and dma is None and eng == "pe":
                continue
            else:
                o.signal = True
                fd.append(j)
        op.deps = fd
        self.ops.append(op)
        if fn is not None:
            self.last_on[eng] = idx
        return idx

    def barrier(self):
        lasts = dict(self.last_on)
        dma_last = {}
        for i, o in enumerate(self.ops):
            if o.dma is not None:
                dma_last[o.dma] = i
        ex = list(lasts.values()) + list(dma_last.values())
        for e in self.ENGS:
            self.add(e, None, extra=[j for j in ex if j != lasts.get(e) or self.ops[j].dma is not None])
        self.last_on = {}

    def emit(self, nc, stack):
        cnt = {e: 0 for e in self.ENGS}
        for o in self.ops:
            if o.signal and o.dma is None:
                cnt[o.eng] += 1
                o.sigval = cnt[o.eng]
        esem = {e: stack.enter_context(nc.semaphore("s_" + e)) for e in self.ENGS}
        dsem = {}
        for k in self.dma_cum:
            dsem[k] = stack.enter_context(nc.semaphore("d%d" % len(dsem)))
        streams = {e: [] for e in self.ENGS}
        for o in self.ops:
            streams[o.eng].append(o)
        ops = self.ops
        final = [(dsem[k], v) for k, v in self.dma_cum.items()]

        def run(eng_name, E):
            waited = {}
            for o in streams[eng_name]:
                for j in o.deps:
                    d = ops[j]
                    if d.dma is not None:
                        sem, val = dsem[d.dma], d.cum
                    else:
                        sem, val = esem[d.eng], d.sigval
                    key = id(sem)
                    if waited.get(key, 0) < val:
                        E.wait_ge(sem, val)
                        waited[key] = val
                if o.fn is None:
                    continue
                ins = o.fn(E)
                if o.dma is not None:
                    ins.then_inc(dsem[o.dma], 16)
                elif o.signal:
                    ins.then_inc(esem[eng_name], 1)
            if eng_name == "sp":
                for sem, val in final:
                    E.wait_ge(sem, val)

        block = stack.enter_context(nc.Block())

        @block.sync
        def _(e):
            run("sp", e)

        @block.scalar
        def _(e):
            run("act", e)

        @block.vector
        def _(e):
            run("dve", e)

        @block.gpsimd
        def _(e):
            run("pool", e)

        @block.tensor
        def _(e):
            run("pe", e)


class Pipe:
    def __init__(self, nslots=3, pd=2):
        self.n = 0
        self.ns = nslots
        self.pd = pd
        self.pending = []

    def stage(self, compute, load=None):
        slot = None
        if load is not None:
            slot = self.n % self.ns
            self.n += 1
            load(slot)
        self.pending.append((compute, slot))
        while len(self.pending) > self.pd:
            c, s = self.pending.pop(0)
            c(s)

    def flush(self):
        while self.pending:
            c, s = self.pending.pop(0)
            c(s)


def build(cfg, piece_order=None, use_order=None):
    if use_order is None:
        use_order = []
    D, F, KC, FC, HP, IN_W = cfg.D, cfg.F, cfg.KC, cfg.FC, cfg.HP, cfg.IN_W
    nc = bass.Bass("TRN2", target_bir_lowering=False)
    S = Sched()
    P = Pipe()

    def din(name, shape, dt=F32):
        return nc.dram_tensor(name, list(shape), dt, kind="ExternalInput").ap()

    def dscr(name, shape, dt):
        return nc.dram_tensor(name, list(shape), dt, kind="Internal").ap()

    xe = din("xe", [EXT, D])
    wf = {
        "w1g": din("w1g", [D, F]), "w1u": din("w1u", [D, F]), "w1d": din("w1d", [F, D]),
        "win": din("win", [D, IN_W]), "wa": din("wa", [1024, D]), "wb": din("wb", [2048, D]),
        "wo": din("wo", [D, D]),
        "w2g": din("w2g", [D, F]), "w2u": din("w2u", [D, F]), "w2d": din("w2d", [F, D]),
    }
    gT_in = din("gT", [128, 3 * KC])
    gfin_in = din("gfin", [D])
    sink_in = din("sink", [16])
    ropeC_in = din("ropeC", [ROT, EXT])
    ropeS_in = din("ropeS", [ROT, EXT])
    seqA_in = din("seqA", [3, EXT], BF16)
    seqB_in = din("seqB", [3, EXT], BF16)
    bandA_in = din("bandA", [64, 192])
    bandB_in = din("bandB", [128, 384])
    ident_in = din("ident", [128, 128], BF16)
    prot_in = din("prot", [ROT, ROT], BF16)
    y_out = nc.dram_tensor("y", [OWN, D], F32, kind="ExternalOutput").ap()

    wb16 = {k: dscr(k + "_b", list(v.shape), BF16) for k, v in wf.items()}
    K_scr = dscr("K_scr", [NK, 128, EXT], BF16)
    Q_scr = dscr("Q_scr", [NQ, 128, OWN], BF16)
    V_scr = dscr("V_scr", [EXT, VC], BF16)
    G_scr = dscr("G_scr", [2 * KC, 128, OWN], BF16)
    h_scr = dscr("h_scr", [OWN, D], F32)
    O_scr = dscr("O_scr", [OWN, 24, OW], F32)
    YA_scr = dscr("YA_scr", [8, 128, OWN], BF16)
    YB_scr = dscr("YB_scr", [16, 128, OWN], BF16)

    HPC = -(-FC // HP)
    parts = []
    c = 0
    for i in range(HP):
        n = FC // HP + (1 if i < FC % HP else 0)
        if n:
            parts.append((c, n))
        c += n
    WSLOT = max(KC * 512, 12288, 8192)
    sizes = {}
    off = {}
    cur = 0

    def region(name, nbytes):
        nonlocal cur
        off[name] = cur
        sizes[name] = nbytes
        cur += (nbytes + 63) // 64 * 64

    region("gT", 3 * KC * 4)
    region("ident", 256)
    region("prot", 64)
    region("sink", 64)
    region("stat", 256)
    region("sil", 2 * 2048)
    region("mt", 2 * 2048)
    SMALL_END = cur
    region("hbuf", max(4 * D * 4, 28672))
    region("xnT", KC * 512 * 2)
    region("hid", max(HPC * 1024, 1024))
    region("wst", 3 * WSLOT)
    region("aux", max(24576, 6 * D))
    ARENA = cur
    ARENA = max(ARENA, 190 * 1024)
    stack = ExitStack()
    arena = stack.enter_context(nc.sbuf_tensor("arena", [128, ARENA // 4], F32))
    banks = [stack.enter_context(nc.psum_tensor("ps%d" % i, [128, 512], F32)) for i in range(8)]

    def carve(o, nbytes, dt):
        v = arena[:, o // 4:(o + nbytes) // 4]
        return v.bitcast(dt) if dt != F32 else v

    def reg(name, dt, o=0, nbytes=None):
        return carve(off[name] + o, sizes[name] - o if nbytes is None else nbytes, dt)

    hbuf = reg("hbuf", F32, 0, 4 * D * 4).rearrange("p (b d) -> p b d", b=4)
    xnT = reg("xnT", BF16).rearrange("p (k t) -> p k t", t=T)
    hid = reg("hid", BF16).rearrange("p (k t) -> p k t", t=T)
    wst = [reg("wst", BF16, s * WSLOT, WSLOT) for s in range(3)]
    xnb = reg("aux", BF16, 0, D * 2)
    gfin = reg("aux", F32, D * 2, D * 4)
    yT = reg("aux", BF16, 0, 24576).rearrange("p (k t) -> p k t", t=T)
    gT = reg("gT", F32)
    ident = reg("ident", BF16)
    prot = reg("prot", BF16)[0:ROT, :]
    sink = reg("sink", F32)
    stat = reg("stat", F32)
    sil = [reg("sil", F32, i * 2048, 2048) for i in range(2)]
    mt = [reg("mt", F32, i * 2048, 2048) for i in range(2)]
    HB = off["hbuf"]
    xb = [carve(HB + i * 2048, 2048, BF16).rearrange("p (c t) -> p c t", t=T) for i in range(2)]
    sgt = [carve(HB + 4096 + i * 2048, 2048, BF16).rearrange("p (c t) -> p c t", t=T) for i in range(2)]
    vb = [carve(HB + 8192 + i * 4096, 4096, BF16).rearrange("p (b c) -> p b c", b=4) for i in range(2)]
    rC = carve(HB + 16384, 2048, F32)[0:ROT, :]
    rS = carve(HB + 18432, 2048, F32)[0:ROT, :]
    rt1 = [carve(HB + 20480 + i * 2048, 2048, F32)[0:ROT, :] for i in range(2)]
    rt2 = [carve(HB + 24576 + i * 2048, 2048, F32)[0:ROT, :] for i in range(2)]
    HBT = ["xb0", "xb1", "sg0", "sg1", "vb0", "vb1", "rC", "rS", "rt10", "rt11", "rt20", "rt21"]
    HB_ALL = ["hbuf"] + HBT

    bank_ctr = [0]

    def alloc_bank():
        b = bank_ctr[0] % 8
        bank_ctr[0] += 1
        return b

    def bk(b):
        return banks[b][:]

    def bkbf(b):
        return banks[b][:].bitcast(BF16)

    def bn(b):
        return "ps%d" % b

    ctr = {"sil": 0, "mt": 0, "xb": 0, "sg": 0, "vb": 0, "rt": 0}

    def rr(name, n=2):
        v = ctr[name] % n
        ctr[name] += 1
        return v

    def ld_const(dst, src, name):
        S.add("sp", lambda E, d=dst, s=src: E.dma_start(out=d, in_=s), writes=[name], dma="c_" + name)

    ld_const(gT, gT_in, "gT")
    ld_const(ident, ident_in, "ident")
    ld_const(prot, prot_in, "prot")
    ld_const(sink[:, 0:16], sink_in.partition_broadcast(128), "sink")

    def piece_list(k):
        rows, cols = wf[k].shape
        if k in ("w1d", "w2d"):
            return [(c0 * 128, (c0 + nch) * 128, 0, cols) for (c0, nch) in parts]
        if k in ("w1g", "w1u", "w2g", "w2u"):
            return [(0, rows, c0 * 128, (c0 + nch) * 128) for (c0, nch) in parts]
        if k == "win":
            b = [0, 3072, 6144, 9216, 11264, 11776, 12288, 12288 + D, 12288 + 2 * D]
            return [(0, rows, b[i], b[i + 1]) for i in range(len(b) - 1)]
        return [(0, rows, 0, cols)]

    pieces = {k: piece_list(k) for k in wf}

    def wpiece(k, row, col):
        for i, (r0, r1, c0, c1) in enumerate(pieces[k]):
            if r0 <= row < r1 and c0 <= col < c1:
                nm = "W_%s_%d" % (k, i)
                if (k, i) not in use_seen:
                    use_seen.add((k, i))
                    use_order.append((k, i))
                return nm
        raise AssertionError

    use_seen = set()
    cast_idx = []
    order = list(piece_order) if piece_order is not None else [(k, i) for k in wf for i in range(len(pieces[k]))]
    for (k, i) in order:
        r0, r1, c0, c1 = pieces[k][i]
        n = len(cast_idx)
        ex = [cast_idx[n - 16]] if n >= 16 else []
        cast_idx.append(S.add("pool", lambda E, d=wb16[k][r0:r1, c0:c1], s=wf[k][r0:r1, c0:c1]: E.dma_start(out=d, in_=s),
                              writes=["W_%s_%d" % (k, i)], dma="cast%d" % (n % 16), extra=ex))

    def norm_to_T(gcol, src_names, tag):
        def comp(_):
            for b in range(4):
                ss = stat[:, b:b + 1]
                rs = stat[:, 8 + b:9 + b]
                S.add("dve", lambda E, o=ss: E.memset(o, 0.0), writes=["stat"])
                S.add("act", lambda E, o=xnb, i=hbuf[:, b, :], a=ss: E.activation(out=o, in_=i, func=AF.Square, accum_out=a),
                      reads=src_names, writes=["xnb", "stat"])
                S.add("dve", lambda E, o=rs, i=ss: E.tensor_scalar(out=o, in0=i, scalar1=1.0 / D, scalar2=EPS, op0=ALU.mult, op1=ALU.add),
                      reads=["stat"], writes=["stat"])
                S.add("act", lambda E, o=rs: E.sqrt(out=o, in_=o), reads=["stat"], writes=["stat"])
                S.add("dve", lambda E, o=rs: E.reciprocal(out=o, in_=o), reads=["stat"], writes=["stat"])
                S.add("act", lambda E, o=xnb, i=hbuf[:, b, :], s=rs: E.mul(out=o, in_=i, mul=s),
                      reads=src_names + ["stat"], writes=["xnb"])
                gsz = min(8, KC)
                for k0 in range(0, KC, gsz):
                    bb = alloc_bank()
                    pv = bkbf(bb)
                    for j in range(gsz):
                        S.add("pe", lambda E, o=pv[:, j * 128:(j + 1) * 128], i=xnb[:, (k0 + j) * 128:(k0 + j + 1) * 128]:
                              E.transpose(out=o, in_=i, identity=ident), reads=["xnb", "ident"], writes=[bn(bb)])
                    gsl = gT[:, gcol * KC + k0: gcol * KC + k0 + gsz].unsqueeze(2).to_broadcast([128, gsz, 128])
                    S.add("dve", lambda E, o=xnT[:, k0:k0 + gsz, b * 128:(b + 1) * 128],
                          i=pv[:, 0:gsz * 128].rearrange("p (k t) -> p k t", t=128), g=gsl:
                          E.tensor_tensor(out=o, in0=i, in1=g, op=ALU.mult),
                          reads=[bn(bb), "gT"], writes=["xnT"])
        P.stage(comp)

    def ffn(wg, wu, wd, wtag):
        Wg, Wu, Wd = wb16[wg], wb16[wu], wb16[wd]
        for (c0, nch) in parts:
            for cc in range(c0, c0 + nch, 2):
                n2 = min(2, c0 + nch - cc)
                st = {}

                def ldg(slot, cc=cc, n2=n2):
                    v = wst[slot][:, 0:KC * n2 * 128].rearrange("p (k c) -> p k c", k=KC)
                    S.add("sp", lambda E, d=v, s=Wg[:, cc * 128:(cc + n2) * 128].rearrange("(k p) c -> p k c", p=128):
                          E.dma_start(out=d, in_=s), reads=[wpiece(wg, 0, cc * 128)], writes=["wst%d" % slot], dma="wst%d" % slot)

                def ldu(slot, cc=cc, n2=n2):
                    v = wst[slot][:, 0:KC * n2 * 128].rearrange("p (k c) -> p k c", k=KC)
                    S.add("sp", lambda E, d=v, s=Wu[:, cc * 128:(cc + n2) * 128].rearrange("(k p) c -> p k c", p=128):
                          E.dma_start(out=d, in_=s), reads=[wpiece(wu, 0, cc * 128)], writes=["wst%d" % slot], dma="wst%d" % slot)

                def cg(slot, n2=n2, st=st):
                    v = wst[slot][:, 0:KC * n2 * 128].rearrange("p (k c) -> p k c", k=KC)
                    st["g"] = []
                    for ci in range(n2):
                        bb = alloc_bank()
                        st["g"].append(bb)
                        for kc in range(KC):
                            S.add("pe", lambda E, o=bk(bb), l=v[:, kc, ci * 128:(ci + 1) * 128], r=xnT[:, kc, :], a=(kc == 0), z=(kc == KC - 1):
                                  E.matmul(o, l, r, start=a, stop=z), reads=["wst%d" % slot, "xnT"], writes=[bn(bb)])

                def cu(slot, n2=n2, st=st, cc=cc, c0=c0):
                    v = wst[slot][:, 0:KC * n2 * 128].rearrange("p (k c) -> p k c", k=KC)
                    for ci in range(n2):
                        bb = alloc_bank()
                        for kc in range(KC):
                            S.add("pe", lambda E, o=bk(bb), l=v[:, kc, ci * 128:(ci + 1) * 128], r=xnT[:, kc, :], a=(kc == 0), z=(kc == KC - 1):
                                  E.matmul(o, l, r, start=a, stop=z), reads=["wst%d" % slot, "xnT"], writes=[bn(bb)])
                        gb = st["g"][ci]
                        si = rr("sil")
                        S.add("act", lambda E, o=sil[si], i=bk(gb): E.activation(out=o, in_=i, func=AF.Silu),
                              reads=[bn(gb)], writes=["sil%d" % si])
                        S.add("dve", lambda E, o=hid[:, cc - c0 + ci, :], a=sil[si], b=bk(bb): E.tensor_tensor(out=o, in0=a, in1=b, op=ALU.mult),
                              reads=["sil%d" % si, bn(bb)], writes=["hid"])

                P.stage(cg, ldg)
                P.stage(cu, ldu)
            CH = max(1, min(nch, WSLOT // 1024))
            subs = [(s0, min(CH, nch - s0)) for s0 in range(0, nch, CH)]
            for cgi in range(max(1, D // 512)):
                ncol = min(512, D)
                st = {}
                for si_, (s0, sn) in enumerate(subs):
                    def ldd(slot, s0=s0, sn=sn, cgi=cgi, c0=c0, ncol=ncol):
                        v = wst[slot][:, 0:sn * ncol].rearrange("p (k c) -> p k c", k=sn)
                        src = Wd[(c0 + s0) * 128:(c0 + s0 + sn) * 128, cgi * ncol:(cgi + 1) * ncol].rearrange("(k p) c -> p k c", p=128)
                        S.add("sp", lambda E, d=v, s=src: E.dma_start(out=d, in_=s), reads=[wpiece(wd, (c0 + s0) * 128, 0)],
                              writes=["wst%d" % slot], dma="wst%d" % slot)

                    def cd(slot, s0=s0, sn=sn, cgi=cgi, nch=nch, st=st, first=(si_ == 0), last=(si_ == len(subs) - 1), ncol=ncol):
                        v = wst[slot][:, 0:sn * ncol].rearrange("p (k c) -> p k c", k=sn)
                        if first:
                            st["b"] = [alloc_bank() for _ in range(4)]
                        for b in range(4):
                            bb = st["b"][b]
                            for k in range(sn):
                                S.add("pe", lambda E, o=bk(bb)[:, 0:ncol], l=hid[:, s0 + k, b * 128:(b + 1) * 128], r=v[:, k, :],
                                      a=(s0 + k == 0), z=(s0 + k == nch - 1): E.matmul(o, l, r, start=a, stop=z),
                                      reads=["wst%d" % slot, "hid"], writes=[bn(bb)])
                        if last:
                            for b in range(4):
                                bb = st["b"][b]
                                hs = hbuf[:, b, cgi * ncol:(cgi + 1) * ncol]
                                S.add("dve", lambda E, o=hs, i=bk(bb)[:, 0:ncol]: E.scalar_tensor_tensor(out=o, in0=i, scalar=0.5, in1=o, op0=ALU.mult, op1=ALU.add),
                                      reads=[bn(bb), "hbuf"], writes=["hbuf"])
                    P.stage(cd, ldd)

    def proj_fm(col0, ncols, kind, tcol, scr, scr0, t0e):
        Wi = wb16["win"]
        for c in range(col0, col0 + ncols, 256):
            def ld(slot, c=c):
                v = wst[slot][:, 0:KC * 256].rearrange("p (k c) -> p k c", k=KC)
                S.add("sp", lambda E, d=v, s=Wi[:, c:c + 256].rearrange("(k p) c -> p k c", p=128): E.dma_start(out=d, in_=s),
                      reads=[wpiece("win", 0, c)], writes=["wst%d" % slot], dma="wst%d" % slot)

            def cp(slot, c=c):
                v = wst[slot][:, 0:KC * 256].rearrange("p (k c) -> p k c", k=KC)
                if kind == "g":
                    oi = rr("sg")
                    obuf, oname = sgt[oi], "sg%d" % oi
                else:
                    oi = rr("xb")
                    obuf, oname = xb[oi], "xb%d" % oi
                for ci in range(2):
                    bb = alloc_bank()
                    for kc in range(KC):
                        S.add("pe", lambda E, o=bk(bb), l=v[:, kc, ci * 128:(ci + 1) * 128], r=xnT[:, kc, :], a=(kc == 0), z=(kc == KC - 1):
                              E.matmul(o, l, r, start=a, stop=z), reads=["wst%d" % slot, "xnT"], writes=[bn(bb)])
                    if kind == "g":
                        S.add("act", lambda E, o=obuf[:, ci, :], i=bk(bb): E.activation(out=o, in_=i, func=AF.Sigmoid),
                              reads=[bn(bb)], writes=[oname])
                    else:
                        S.add("act", lambda E, o=obuf[:, ci, :], i=bk(bb): E.copy(out=o, in_=i),
                              reads=[bn(bb)], writes=[oname])
                        b2 = alloc_bank()
                        ri = rr("rt")
                        S.add("pe", lambda E, o=bk(b2)[0:ROT, :], l=prot, r=obuf[0:ROT, ci, :]: E.matmul(o, l, r, start=True, stop=True),
                              reads=[oname, "prot"], writes=[bn(b2)])
                        S.add("dve", lambda E, o=rt1[ri], a=obuf[0:ROT, ci, :], b=rC: E.tensor_tensor(out=o, in0=a, in1=b, op=ALU.mult),
                              reads=[oname, "rC"], writes=["rt1%d" % ri])
                        S.add("dve", lambda E, o=rt2[ri], a=bk(b2)[0:ROT, :], b=rS: E.tensor_tensor(out=o, in0=a, in1=b, op=ALU.mult),
                              reads=[bn(b2), "rS"], writes=["rt2%d" % ri])
                        S.add("dve", lambda E, o=obuf[0:ROT, ci, :], a=rt1[ri], b=rt2[ri]: E.tensor_tensor(out=o, in0=a, in1=b, op=ALU.add),
                              reads=["rt1%d" % ri, "rt2%d" % ri], writes=[oname])
                h0 = scr0 + (c - col0) // 128
                dst = scr[h0:h0 + 2, :, tcol:tcol + T].rearrange("c p t -> p c t")
                S.add("sp", lambda E, d=dst, s=obuf: E.dma_start(out=d, in_=s), reads=[oname], dma=oname)
            P.stage(cp, ld)

    def proj_v(col0, ncols, vcol0, t0e):
        Wi = wb16["win"]
        KH = max(1, KC // 2)
        halves = [(k0, min(KH, KC - k0)) for k0 in range(0, KC, KH)]
        for c in range(col0, col0 + ncols, 512):
            st = {}
            for hi, (k0, kn) in enumerate(halves):
                def ld(slot, c=c, k0=k0, kn=kn):
                    v = wst[slot][:, 0:kn * 512].rearrange("p (k c) -> p k c", k=kn)
                    S.add("sp", lambda E, d=v, s=Wi[k0 * 128:(k0 + kn) * 128, c:c + 512].rearrange("(k p) c -> p k c", p=128): E.dma_start(out=d, in_=s),
                          reads=[wpiece("win", 0, c)], writes=["wst%d" % slot], dma="wst%d" % slot)

                def cp(slot, c=c, k0=k0, kn=kn, st=st, first=(hi == 0), last=(hi == len(halves) - 1)):
                    v = wst[slot][:, 0:kn * 512].rearrange("p (k c) -> p k c", k=kn)
                    if first:
                        st["b"] = [alloc_bank() for _ in range(4)]
                        st["o"] = rr("vb")
                    oi = st["o"]
                    for b in range(4):
                        bb = st["b"][b]
                        for k in range(kn):
                            S.add("pe", lambda E, o=bk(bb), l=xnT[:, k0 + k, b * 128:(b + 1) * 128], r=v[:, k, :], a=(k0 + k == 0), z=(k0 + k == KC - 1):
                                  E.matmul(o, l, r, start=a, stop=z), reads=["wst%d" % slot, "xnT"], writes=[bn(bb)])
                    if last:
                        for b in range(4):
                            bb = st["b"][b]
                            S.add("act", lambda E, o=vb[oi][:, b, :], i=bk(bb): E.copy(out=o, in_=i), reads=[bn(bb)], writes=["vb%d" % oi])
                        vc = vcol0 + (c - col0)
                        dst = V_scr[t0e:t0e + T, vc:vc + 512].rearrange("(b p) c -> p b c", p=128)
                        S.add("sp", lambda E, d=dst, s=vb[oi]: E.dma_start(out=d, in_=s), reads=["vb%d" % oi], dma="vb%d" % oi)
                P.stage(cp, ld)

    for ti in range(NT_EXT):
        t0e = ti * T
        own = OWN_T0 <= ti < OWN_T0 + NT_OWN
        tcol = (ti - OWN_T0) * T

        def ldx(_, t0e=t0e):
            S.add("sp", lambda E, d=hbuf, s=xe[t0e:t0e + T, :].rearrange("(b p) d -> p b d", p=128): E.dma_start(out=d, in_=s),
                  writes=HB_ALL, dma="hbuf")
        P.stage(ldx)
        norm_to_T(0, ["hbuf"], "n1")
        ffn("w1g", "w1u", "w1d", "f1")
        if own:
            def sth(_, tcol=tcol):
                S.add("sp", lambda E, d=h_scr[tcol:tcol + T, :].rearrange("(b p) d -> p b d", p=128), s=hbuf: E.dma_start(out=d, in_=s),
                      reads=HB_ALL, dma="hst")
            P.stage(sth)
        norm_to_T(1, HB_ALL, "nm")

        def ldrope(_, t0e=t0e):
            S.add("sp", lambda E, d=rC, s=ropeC_in[:, t0e:t0e + T]: E.dma_start(out=d, in_=s), writes=["rC"], dma="rC")
            S.add("sp", lambda E, d=rS, s=ropeS_in[:, t0e:t0e + T]: E.dma_start(out=d, in_=s), writes=["rS"], dma="rS")
        P.stage(ldrope)
        far = ti == 0 or ti == NT_EXT - 1
        if far:
            proj_fm(3072 + 2048, 1024, "k", t0e, K_scr, 16, t0e)
            proj_v(6144 + 2048, 1024, 2048, t0e)
        else:
            proj_fm(3072, 3072, "k", t0e, K_scr, 0, t0e)
            proj_fm(11264, 512, "k", t0e, K_scr, 24, t0e)
            proj_v(6144, 3072, 0, t0e)
            proj_v(11776, 512, 3072, t0e)
        if own:
            proj_fm(0, 3072, "q", tcol, Q_scr, 0, t0e)
            proj_fm(9216, 2048, "q", tcol, Q_scr, 24, t0e)
            proj_fm(12288, 2 * D, "g", tcol, G_scr, 0, t0e)
    P.flush()
    S.barrier()

    acur = [SMALL_END]

    def acarve(nbytes, dt):
        o = acur[0]
        acur[0] += (nbytes + 63) // 64 * 64
        assert acur[0] <= ARENA
        return carve(o, nbytes, dt)

    NSET = 8
    kTb = [acarve(EXT * 2, BF16) for _ in range(2)]
    qTb = [acarve(OWN * 2, BF16) for _ in range(2)]
    vsb = [acarve(80 * 128 * 2, BF16) for _ in range(2)]
    sA = acarve(EXT * 2, BF16)[0:3, :]
    sB = acarve(EXT * 2, BF16)[0:3, :]
    bandA = acarve(192 * 4, F32)[0:64, :]
    bandB = acarve(384 * 4, F32)
    Smb = [acarve(384 * 4, F32) for _ in range(NSET)]
    Pb = [acarve(384 * 2, BF16) for _ in range(NSET)]
    PTb = [acarve(384 * 2, BF16) for _ in range(NSET)]
    ast = acarve(1024, F32)
    nsink = acarve(64, F32)
    amark = acur[0]
    obufs = [acarve(48 * OW * 4, F32) for _ in range(2)]

    S.add("sp", lambda E: E.dma_start(out=sA, in_=seqA_in), writes=["sA"], dma="sA")
    S.add("sp", lambda E: E.dma_start(out=sB, in_=seqB_in), writes=["sB"], dma="sB")
    S.add("sp", lambda E: E.dma_start(out=bandA, in_=bandA_in), writes=["bandA"], dma="bandA")
    S.add("sp", lambda E: E.dma_start(out=bandB, in_=bandB_in), writes=["bandB"], dma="bandB")
    S.add("dve", lambda E: E.tensor_scalar(out=nsink[:, 0:16], in0=sink[:, 0:16], scalar1=-1.0, scalar2=None, op0=ALU.mult),
          reads=["sink"], writes=["nsink"])
    scale = 1.0 / math.sqrt(HD)
    uctr = [0]

    def attn_batch(units, QN, NKEY, band, bname, is_b):
        NB = 3
        KB = NKEY // NB
        R = []
        for u in units:
            k = uctr[0] % NSET
            sc = (uctr[0] % 16) * 8
            uctr[0] += 1
            b1 = alloc_bank()
            b2 = alloc_bank() if is_b else b1
            R.append((k, sc, b1, b2))

        def regs(b1, b2):
            if is_b:
                return bk(b1)[0:QN, 0:NKEY], bk(b1)[0:128, NKEY:NKEY + 128], bkbf(b2)[0:KB, 0:NB * QN]
            return bk(b1)[0:QN, 0:NKEY], bk(b1)[0:QN, NKEY:NKEY + 128], bkbf(b1)[0:KB, 2 * (NKEY + 128):2 * (NKEY + 128) + NB * QN]

        for u, (k, sc, b1, b2) in zip(units, R):
            sps, ops_, tps = regs(b1, b2)
            S.add("pe", lambda E, o=sps, l=u["qsl"], r=u["ksl"]: E.matmul(o, l, r, start=True, stop=False),
                  reads=[u["kname"], u["qname"]], writes=[bn(b1)])
            S.add("pe", lambda E, o=sps, l=u["qtok"], r=u["ktok"]: E.matmul(o, l, r, start=False, stop=True),
                  reads=["sA", "sB"], writes=[bn(b1)])
        for u, (k, sc, b1, b2) in zip(units, R):
            sps, ops_, tps = regs(b1, b2)
            S.add("dve", lambda E, o=Smb[k][0:QN, 0:NKEY], i=sps, b=band: E.scalar_tensor_tensor(out=o, in0=i, scalar=scale, in1=b, op0=ALU.mult, op1=ALU.add),
                  reads=[bn(b1), bname], writes=["Sm%d" % k])
        for u, (k, sc, b1, b2) in zip(units, R):
            negm = ast[0:QN, sc:sc + 1]
            S.add("dve", lambda E, o=negm, i=Smb[k][0:QN, 0:NKEY]: E.reduce_max(out=o, in_=i, axis=AX.X, negate=True),
                  reads=["Sm%d" % k], writes=["ast%d" % sc])
            if is_b:
                S.add("dve", lambda E, o=negm, i=nsink[0:QN, u["hq"]:u["hq"] + 1]: E.tensor_tensor(out=o, in0=o, in1=i, op=ALU.min),
                      reads=["nsink"], writes=["ast%d" % sc])
            S.add("dve", lambda E, o=ast[0:QN, sc + 1:sc + 2]: E.memset(o, 0.0), writes=["astd%d" % sc])
        for u, (k, sc, b1, b2) in zip(units, R):
            negm = ast[0:QN, sc:sc + 1]
            den = ast[0:QN, sc + 1:sc + 2]
            S.add("act", lambda E, o=Pb[k][0:QN, 0:NKEY], i=Smb[k][0:QN, 0:NKEY], b=negm, a=den: E.activation(out=o, in_=i, func=AF.Exp, bias=b, scale=1.0, accum_out=a),
                  reads=["Sm%d" % k, "ast%d" % sc], writes=["P%d" % k, "astd%d" % sc])
            if is_b:
                S.add("act", lambda E, o=ast[0:QN, sc + 3:sc + 4], i=sink[0:QN, u["hq"]:u["hq"] + 1], b=negm: E.activation(out=o, in_=i, func=AF.Exp, bias=b, scale=1.0),
                      reads=["ast%d" % sc, "sink"], writes=["aste%d" % sc])
        for u, (k, sc, b1, b2) in zip(units, R):
            den = ast[0:QN, sc + 1:sc + 2]
            rden = ast[0:QN, sc + 2:sc + 3]
            if is_b:
                S.add("dve", lambda E, o=den, b=ast[0:QN, sc + 3:sc + 4]: E.tensor_tensor(out=o, in0=o, in1=b, op=ALU.add),
                      reads=["aste%d" % sc], writes=["astd%d" % sc])
            S.add("dve", lambda E, o=rden, i=den: E.reciprocal(out=o, in_=i), reads=["astd%d" % sc], writes=["astr%d" % sc])
            if is_b:
                S.add("dve", lambda E, o=Pb[k][0:QN, 0:NKEY], s=rden: E.tensor_scalar(out=o, in0=o, scalar1=s, scalar2=None, op0=ALU.mult),
                      reads=["astr%d" % sc], writes=["P%d" % k])
            else:
                S.add("act", lambda E, o=ast[0:QN, sc + 3:sc + 4], i=den: E.activation(out=o, in_=i, func=AF.Ln),
                      reads=["astd%d" % sc], writes=["aste%d" % sc])
        for u, (k, sc, b1, b2) in zip(units, R):
            sps, ops_, tps = regs(b1, b2)
            for b in range(NB):
                S.add("pe", lambda E, o=tps[:, b * QN:(b + 1) * QN], i=Pb[k][0:QN, b * KB:(b + 1) * KB]:
                      E.transpose(out=o, in_=i, identity=ident[0:QN, 0:QN]), reads=["P%d" % k, "ident"], writes=[bn(b2)])
        for u, (k, sc, b1, b2) in zip(units, R):
            sps, ops_, tps = regs(b1, b2)
            S.add("act", lambda E, o=PTb[k][0:KB, 0:NB * QN], i=tps: E.copy(out=o, in_=i), reads=[bn(b2)], writes=["PT%d" % k])
            if not is_b:
                S.add("pool", lambda E, o=u["lse"], a=ast[0:QN, sc + 3:sc + 4], b=ast[0:QN, sc:sc + 1]: E.tensor_tensor(out=o, in0=a, in1=b, op=ALU.subtract),
                      reads=["aste%d" % sc, "ast%d" % sc], writes=[u["oname"] + "l"])
        for u, (k, sc, b1, b2) in zip(units, R):
            sps, ops_, tps = regs(b1, b2)
            for b in range(NB):
                if is_b:
                    S.add("pe", lambda E, o=ops_, l=u["vblk"][b], r=PTb[k][0:KB, b * QN:(b + 1) * QN], a=(b == 0), z=(b == NB - 1):
                          E.matmul(o, l, r, start=a, stop=z), reads=["PT%d" % k, u["vname"]], writes=[bn(b1)])
                else:
                    S.add("pe", lambda E, o=ops_, l=PTb[k][0:KB, b * QN:(b + 1) * QN], r=u["vblk"][b], a=(b == 0), z=(b == NB - 1):
                          E.matmul(o, l, r, start=a, stop=z), reads=["PT%d" % k, u["vname"]], writes=[bn(b1)])
        for u, (k, sc, b1, b2) in zip(units, R):
            sps, ops_, tps = regs(b1, b2)
            if is_b:
                S.add("act", lambda E, o=u["out"], i=ops_: E.copy(out=o, in_=i), reads=[bn(b1)], writes=[u["oname"]])
            else:
                S.add("dve", lambda E, o=u["out"], i=ops_, s=ast[0:QN, sc + 2:sc + 3]: E.tensor_scalar(out=o, in0=i, scalar1=s, scalar2=None, op0=ALU.mult),
                      reads=[bn(b1), "astr%d" % sc], writes=[u["oname"]])

    GB = 4
    for g, r in enumerate(DIL):
        nb = EXT // (64 * r)
        nt = OWN // (64 * r)
        blk_base = HALO // (64 * r) - 1
        for h in range(8):
            hh = g * 8 + h
            s2 = hh % 2
            kT, qT, vs, ob = kTb[s2], qTb[s2], vsb[s2], obufs[s2]
            kname, qname, vname, oname = "kT%d" % s2, "qT%d" % s2, "vs%d" % s2, "ob%d" % s2
            cut = 0 if g == 2 else T
            S.add("sp", lambda E, d=kT[:, cut:EXT - cut], s=K_scr[hh][:, cut:EXT - cut]: E.dma_start(out=d, in_=s), writes=[kname], dma=kname)
            S.add("sp", lambda E, d=qT, s=Q_scr[hh]: E.dma_start(out=d, in_=s), writes=[qname], dma=qname)
            vb0 = cut // (64 * r)
            vv = vs[0:64, :].rearrange("p (j b d) -> p j b d", j=r, b=nb)
            if cut == 0:
                vsrc = bass.AP(V_scr.tensor, g * 1024 + h * 128, [[r * VC, 64], [VC, r], [r * 64 * VC, nb], [1, 128]])
                S.add("sp", lambda E, d=vv, s=vsrc: E.dma_start(out=d, in_=s), writes=[vname], dma=vname)
            else:
                for j in range(r):
                    vsrc = bass.AP(V_scr.tensor, g * 1024 + h * 128 + j * VC + vb0 * r * 64 * VC, [[r * VC, 64], [r * 64 * VC, nb - 2 * vb0], [1, 128]])
                    S.add("sp", lambda E, d=vv[:, j, vb0:nb - vb0, :], s=vsrc: E.dma_start(out=d, in_=s), writes=[vname], dma=vname)
            obv = ob[0:64, :].rearrange("p (j n c) -> p j n c", j=r, n=nt)
            units = []
            for j in range(r):
                for n in range(nt):
                    q0 = r * 64 * n + j
                    k0 = HALO + r * 64 * (n - 1) + j

                    def ss(a, cnt, r=r):
                        return slice(a, a + (cnt - 1) * r + 1, r)
                    qe = HALO + q0
                    units.append(dict(qsl=qT[:, ss(q0, 64)], ksl=kT[:, ss(k0, 192)], qtok=sA[:, ss(qe, 64)], ktok=sB[:, ss(k0, 192)],
                                      kname=kname, qname=qname, vname=vname, oname=oname,
                                      vblk=[vv[:, j, blk_base + n + b, :] for b in range(3)],
                                      out=obv[:, j, n, 0:128], lse=obv[:, j, n, 128:129]))
            for i0 in range(0, len(units), GB):
                attn_batch(units[i0:i0 + GB], 64, 192, bandA, "bandA", False)
            odst = bass.AP(O_scr.tensor, hh * OW, [[r * 24 * OW, 64], [24 * OW, r], [64 * r * 24 * OW, nt], [1, OW]])
            S.add("sp", lambda E, d=odst, s=obv: E.dma_start(out=d, in_=s), reads=[oname, oname + "l"], writes=["O_scr"], dma="ost")
    S.barrier()
    acur[0] = amark
    ybufs = [acarve(OWN * 2, BF16) for _ in range(2)]
    obs = [acarve(24 * OW * 4, F32) for _ in range(2)]
    wts = acarve(64 * 4, F32)
    yaf = acarve(1024 * 4, F32)
    yab = acarve(1024 * 2, BF16)
    yaTb = [acarve(8 * 128 * 2, BF16) for _ in range(2)]
    def combine_block(tb):
        s2 = tb % 2
        ob = obs[s2].rearrange("p (h c) -> p h c", c=OW)
        S.add("sp", lambda E, d=ob, s=O_scr[tb * 128:(tb + 1) * 128]: E.dma_start(out=d, in_=s), reads=["O_scr"], writes=["obs%d" % s2], dma="obs%d" % s2)
        lv = ob[:, :, 128].rearrange("p (g h) -> p g h", g=3)
        M = wts[:, 0:8]
        w = wts[:, 8:32].rearrange("p (g h) -> p g h", g=3)
        ws = wts[:, 32:40]
        S.add("dve", lambda E, o=M, a=lv[:, 0, :], b=lv[:, 1, :]: E.tensor_tensor(out=o, in0=a, in1=b, op=ALU.max), reads=["obs%d" % s2], writes=["wts"])
        S.add("dve", lambda E, o=M, b=lv[:, 2, :]: E.tensor_tensor(out=o, in0=o, in1=b, op=ALU.max), reads=["obs%d" % s2], writes=["wts"])
        S.add("dve", lambda E, o=w, a=lv, b=M.unsqueeze(1).to_broadcast([128, 3, 8]): E.tensor_tensor(out=o, in0=a, in1=b, op=ALU.subtract),
              reads=["obs%d" % s2], writes=["wts"])
        S.add("act", lambda E, o=wts[:, 8:32]: E.activation(out=o, in_=o, func=AF.Exp), reads=["wts"], writes=["wts"])
        S.add("dve", lambda E, o=ws, a=w[:, 0, :], b=w[:, 1, :]: E.tensor_tensor(out=o, in0=a, in1=b, op=ALU.add), reads=["wts"], writes=["wts"])
        S.add("dve", lambda E, o=ws, b=w[:, 2, :]: E.tensor_tensor(out=o, in0=o, in1=b, op=ALU.add), writes=["wts"])
        S.add("dve", lambda E, o=ws: E.reciprocal(out=o, in_=o), writes=["wts"])
        S.add("dve", lambda E, o=w, b=ws.unsqueeze(1).to_broadcast([128, 3, 8]): E.tensor_tensor(out=o, in0=o, in1=b, op=ALU.mult), writes=["wts"])
        yv = yaf.rearrange("p (h d) -> p h d", d=128)
        for gi in range(3):
            og = ob[:, gi * 8:(gi + 1) * 8, 0:128]
            wg_ = w[:, gi, :].unsqueeze(2).to_broadcast([128, 8, 128])
            if gi == 0:
                S.add("dve", lambda E, o=yv, a=og, b=wg_: E.tensor_tensor(out=o, in0=a, in1=b, op=ALU.mult), reads=["obs%d" % s2, "wts"], writes=["yaf"])
            else:
                S.add("pool", lambda E, o=og, a=og, b=wg_: E.tensor_tensor(out=o, in0=a, in1=b, op=ALU.mult), reads=["wts"], writes=["obs%d" % s2])
                S.add("dve", lambda E, o=yv, a=yv, b=og: E.tensor_tensor(out=o, in0=a, in1=b, op=ALU.add), reads=["obs%d" % s2], writes=["yaf"])
        S.add("act", lambda E, o=yab, i=yaf: E.copy(out=o, in_=i), reads=["yaf"], writes=["yab"])
        tb_ = alloc_bank()
        for hq in range(8):
            S.add("pe", lambda E, o=bkbf(tb_)[:, hq * 128:(hq + 1) * 128], i=yab[:, hq * 128:(hq + 1) * 128]: E.transpose(out=o, in_=i, identity=ident),
                  reads=["yab", "ident"], writes=[bn(tb_)])
        yTv = yaTb[s2].rearrange("p (h t) -> p h t", h=8)
        S.add("dve", lambda E, o=yaTb[s2], i=bkbf(tb_): E.tensor_copy(out=o, in_=i),
              reads=[bn(tb_)], writes=["yaT%d" % s2])
        S.add("sp", lambda E, d=YA_scr[:, :, tb * 128:(tb + 1) * 128].rearrange("h p t -> p h t"), s=yTv: E.dma_start(out=d, in_=s),
              reads=["yaT%d" % s2], dma="yaT%d" % s2)

    comb_next = [0]

    def maybe_combine(force=False):
        if comb_next[0] < OWN // 128:
            combine_block(comb_next[0])
            comb_next[0] += 1

    for hq in range(16):
        kv = hq // 4
        s2 = hq % 2
        qT, yb = qTb[s2], ybufs[s2]
        if hq % 4 == 0:
            kT, vs = kTb[kv % 2], vsb[kv % 2]
            kname, vname = "kT%d" % (kv % 2), "vs%d" % (kv % 2)
            S.add("sp", lambda E, d=kT[:, T:EXT - T], s=K_scr[24 + kv][:, T:EXT - T]: E.dma_start(out=d, in_=s), writes=[kname], dma=kname)
            vvB = vs[:, 0:40 * 128].rearrange("p (b d) -> p b d", d=128)
            S.add("sp", lambda E, d=vvB[:, 4:36, :], s=V_scr[T:EXT - T, 3072 + kv * 128:3072 + (kv + 1) * 128].rearrange("(b p) d -> p b d", p=128): E.dma_start(out=d, in_=s),
                  writes=[vname], dma=vname)
        qname = "qT%d" % s2
        S.add("sp", lambda E, d=qT, s=Q_scr[24 + hq]: E.dma_start(out=d, in_=s), writes=[qname], dma=qname)
        units = []
        for n in range(OWN // 128):
            q0 = 128 * n
            k0 = HALO + 128 * (n - 1)
            units.append(dict(qsl=qT[:, q0:q0 + 128], ksl=kT[:, k0:k0 + 384], qtok=sA[:, HALO + q0:HALO + q0 + 128], ktok=sB[:, k0:k0 + 384],
                              kname=kname, qname=qname, vname=vname, oname="yb%d" % s2, hq=hq,
                              vblk=[vvB[:, HALO // 128 + n - 1 + b, :] for b in range(3)], out=yb[:, q0:q0 + 128]))
        for bi, i0 in enumerate(range(0, len(units), GB)):
            attn_batch(units[i0:i0 + GB], 128, 384, bandB, "bandB", True)
            if bi % 3 == 1:
                maybe_combine()
        S.add("sp", lambda E, d=YB_scr[hq], s=yb: E.dma_start(out=d, in_=s), reads=["yb%d" % s2], dma="yb%d" % s2)
    while comb_next[0] < OWN // 128:
        maybe_combine()
    S.barrier()

    Wa, Wb_, Wo = wb16["wa"], wb16["wb"], wb16["wo"]
    for to in range(NT_OWN):
        tcol = to * T

        def ldy(_, tcol=tcol):
            S.add("sp", lambda E, d=yT[:, 0:8, :], s=YA_scr[:, :, tcol:tcol + T].rearrange("h p t -> p h t"): E.dma_start(out=d, in_=s),
                  writes=["xnb", "gfin"], dma="yT")
            S.add("sp", lambda E, d=yT[:, 8:24, :], s=YB_scr[:, :, tcol:tcol + T].rearrange("h p t -> p h t"): E.dma_start(out=d, in_=s),
                  writes=["xnb", "gfin"], dma="yT")
        P.stage(ldy)
        for dc0 in range(0, KC, 2):
            nd = min(2, KC - dc0)

            def ldb(slot, dc0=dc0, nd=nd):
                va = wst[slot][:, 0:8 * nd * 128].rearrange("p (k c) -> p k c", k=8)
                vbw = wst[slot][:, 8 * 256:8 * 256 + 16 * nd * 128].rearrange("p (k c) -> p k c", k=16)
                S.add("sp", lambda E, d=va, s=Wa[:, dc0 * 128:(dc0 + nd) * 128].rearrange("(k p) c -> p k c", p=128): E.dma_start(out=d, in_=s),
                      reads=[wpiece("wa", 0, dc0 * 128)], writes=["wst%d" % slot], dma="wst%d" % slot)
                S.add("sp", lambda E, d=vbw, s=Wb_[:, dc0 * 128:(dc0 + nd) * 128].rearrange("(k p) c -> p k c", p=128): E.dma_start(out=d, in_=s),
                      reads=[wpiece("wb", 0, dc0 * 128)], writes=["wst%d" % slot], dma="wst%d" % slot)

            def cb(slot, dc0=dc0, nd=nd, tcol=tcol):
                va = wst[slot][:, 0:8 * nd * 128].rearrange("p (k c) -> p k c", k=8)
                vbw = wst[slot][:, 8 * 256:8 * 256 + 16 * nd * 128].rearrange("p (k c) -> p k c", k=16)
                for ci in range(nd):
                    dc = dc0 + ci
                    gi = rr("sg")
                    gsrc = G_scr.rearrange("(a c) p t -> p a c t", a=2)[:, :, dc, tcol:tcol + T]
                    S.add("sp", lambda E, d=sgt[gi], s=gsrc: E.dma_start(out=d, in_=s), writes=["sg%d" % gi], dma="sg%d" % gi)
                    ba = alloc_bank()
                    for kc in range(8):
                        S.add("pe", lambda E, o=bk(ba), l=va[:, kc, ci * 128:(ci + 1) * 128], r=yT[:, kc, :], a=(kc == 0), z=(kc == 7):
                              E.matmul(o, l, r, start=a, stop=z), reads=["wst%d" % slot, "xnb", "gfin"], writes=[bn(ba)])
                    bb = alloc_bank()
                    for kc in range(16):
                        S.add("pe", lambda E, o=bk(bb), l=vbw[:, kc, ci * 128:(ci + 1) * 128], r=yT[:, 8 + kc, :], a=(kc == 0), z=(kc == 15):
                              E.matmul(o, l, r, start=a, stop=z), reads=["wst%d" % slot, "xnb", "gfin"], writes=[bn(bb)])
                    m1, m2 = rr("mt"), rr("sil")
                    S.add("dve", lambda E, o=mt[m1], a=bk(ba), b=sgt[gi][:, 0, :]: E.tensor_tensor(out=o, in0=a, in1=b, op=ALU.mult),
                          reads=[bn(ba), "sg%d" % gi], writes=["mt%d" % m1])
                    S.add("dve", lambda E, o=sil[m2], a=bk(bb), b=sgt[gi][:, 1, :]: E.tensor_tensor(out=o, in0=a, in1=b, op=ALU.mult),
                          reads=[bn(bb), "sg%d" % gi], writes=["sil%d" % m2])
                    S.add("pool", lambda E, o=xnT[:, dc, :], a=mt[m1], b=sil[m2]: E.tensor_tensor(out=o, in0=a, in1=b, op=ALU.add),
                          reads=["mt%d" % m1, "sil%d" % m2], writes=["xnT"])
            P.stage(cb, ldb)

        def ldh(_, tcol=tcol):
            S.add("sp", lambda E, d=hbuf, s=h_scr[tcol:tcol + T, :].rearrange("(b p) d -> p b d", p=128): E.dma_start(out=d, in_=s),
                  writes=HB_ALL, dma="hbuf")
        P.stage(ldh)
        KH = max(1, KC // 2)
        halves = [(k0, min(KH, KC - k0)) for k0 in range(0, KC, KH)]
        for c in range(0, D, 512):
            st = {}
            for hi, (k0, kn) in enumerate(halves):
                def ldo(slot, c=c, k0=k0, kn=kn):
                    v = wst[slot][:, 0:kn * 512].rearrange("p (k c) -> p k c", k=kn)
                    S.add("sp", lambda E, d=v, s=Wo[k0 * 128:(k0 + kn) * 128, c:c + 512].rearrange("(k p) c -> p k c", p=128): E.dma_start(out=d, in_=s),
                          reads=[wpiece("wo", 0, c)], writes=["wst%d" % slot], dma="wst%d" % slot)

                def co(slot, c=c, k0=k0, kn=kn, st=st, first=(hi == 0), last=(hi == len(halves) - 1)):
                    v = wst[slot][:, 0:kn * 512].rearrange("p (k c) -> p k c", k=kn)
                    if first:
                        st["b"] = [alloc_bank() for _ in range(4)]
                    for b in range(4):
                        bb = st["b"][b]
                        for k in range(kn):
                            S.add("pe", lambda E, o=bk(bb), l=xnT[:, k0 + k, b * 128:(b + 1) * 128], r=v[:, k, :], a=(k0 + k == 0), z=(k0 + k == KC - 1):
                                  E.matmul(o, l, r, start=a, stop=z), reads=["wst%d" % slot, "xnT"], writes=[bn(bb)])
                    if last:
                        for b in range(4):
                            bb = st["b"][b]
                            hs = hbuf[:, b, c:c + 512]
                            S.add("dve", lambda E, o=hs, i=bk(bb): E.tensor_tensor(out=o, in0=i, in1=o, op=ALU.add),
                                  reads=[bn(bb), "hbuf"], writes=["hbuf"])
                P.stage(co, ldo)
        norm_to_T(2, ["hbuf"], "n2")
        ffn("w2g", "w2u", "w2d", "f2")

        def fin(_, tcol=tcol):
            S.add("sp", lambda E, d=gfin, s=gfin_in.partition_broadcast(128): E.dma_start(out=d, in_=s), writes=["gfin"], dma="gfin")
            for b in range(4):
                ss = stat[:, 16 + b:17 + b]
                rs = stat[:, 24 + b:25 + b]
                S.add("dve", lambda E, o=ss: E.memset(o, 0.0), writes=["stat"])
                S.add("act", lambda E, o=xnb, i=hbuf[:, b, :], a=ss: E.activation(out=o, in_=i, func=AF.Square, accum_out=a),
                      reads=["hbuf"], writes=["xnb", "stat"])
                S.add("dve", lambda E, o=rs, i=ss: E.tensor_scalar(out=o, in0=i, scalar1=1.0 / D, scalar2=EPS, op0=ALU.mult, op1=ALU.add),
                      reads=["stat"], writes=["stat"])
                S.add("act", lambda E, o=rs: E.sqrt(out=o, in_=o), reads=["stat"], writes=["stat"])
                S.add("dve", lambda E, o=rs: E.reciprocal(out=o, in_=o), reads=["stat"], writes=["stat"])
                S.add("act", lambda E, o=hbuf[:, b, :], s=rs: E.mul(out=o, in_=o, mul=s),
                      reads=["hbuf", "stat"], writes=["hbuf"])
                S.add("dve", lambda E, o=hbuf[:, b, :], g=gfin: E.tensor_tensor(out=o, in0=o, in1=g, op=ALU.mult),
                      reads=["hbuf", "gfin"], writes=["hbuf"])
            S.add("sp", lambda E, d=y_out[tcol:tcol + T, :].rearrange("(b p) d -> p b d", p=128), s=hbuf: E.dma_start(out=d, in_=s),
                  reads=HB_ALL, dma="yst")
        P.stage(fin)
    P.flush()
    S.emit(nc, stack)
    stack.close()
    return nc


def _host_tables(seq_lens):
    tot = sum(seq_lens)
    seq_id = np.concatenate([np.full(n, i, np.int64) for i, n in enumerate(seq_lens)])
    pos = np.concatenate([np.arange(n, dtype=np.int64) for n in seq_lens])
    half = ROT // 2
    inv = (THETA ** (-np.arange(half, dtype=np.float32) / np.float32(half))).astype(np.float32)
    out = []
    for c in range(N_CORES):
        g0 = c * OWN - HALO
        idx = np.arange(g0, g0 + EXT)
        valid = (idx >= 0) & (idx < tot)
        ci = np.clip(idx, 0, tot - 1)
        sid = np.where(valid, seq_id[ci], 3)
        p = np.where(valid, pos[ci], 0).astype(np.float32)
        ang = p[None, :] * inv[:, None]
        cs, sn = np.cos(ang).astype(np.float32), np.sin(ang).astype(np.float32)
        ropeC = np.concatenate([cs, cs], 0)
        ropeS = np.concatenate([-sn, sn], 0)
        onehot = (sid[None, :] == np.arange(3)[:, None])
        seqA = onehot.astype(np.float32).astype(ml_dtypes.bfloat16)
        seqB = (np.float32(NEGBIG) * (1.0 - onehot.astype(np.float32))).astype(ml_dtypes.bfloat16)
        out.append(dict(ropeC=np.ascontiguousarray(ropeC), ropeS=np.ascontiguousarray(ropeS), seqA=seqA, seqB=seqB, idx=idx, valid=valid))
    return out


def _consts():
    i = np.arange(64)[:, None]
    k = np.arange(192)[None, :]
    bandA = np.where(np.abs(k - 64 - i) <= 64, 0.0, NEGBIG).astype(np.float32)
    i = np.arange(128)[:, None]
    k = np.arange(384)[None, :]
    bandB = np.where(np.abs(k - 128 - i) <= 128, 0.0, NEGBIG).astype(np.float32)
    ident = np.eye(128, dtype=np.float32).astype(ml_dtypes.bfloat16)
    prot = np.zeros((ROT, ROT), np.float32)
    for m in range(ROT):
        prot[(m + 16) % ROT, m] = 1.0
    return bandA, bandB, ident, prot.astype(ml_dtypes.bfloat16)


_NC_CACHE = {}


def run(cfg, x_prompt, x_sample, g_ffn1, w1_gate, w1_up, w1_down, g_mix, w_in, sink_b, w_branch_a, w_branch_b,
        w_out, g_ffn2, w2_gate, w2_up, w2_down, g_final, trace=False):
    D = cfg.D
    f = lambda a: np.ascontiguousarray(np.asarray(a, dtype=np.float32))
    xp, xs = f(x_prompt), f(x_sample)
    seq_lens = [xp.shape[1]] * xp.shape[0] + [xs.shape[1]] * xs.shape[0]
    X = np.concatenate([xp.reshape(-1, D), xs.reshape(-1, D)], 0)
    tot = X.shape[0]
    assert tot == N_CORES * OWN
    tabs = _host_tables(seq_lens)
    bandA, bandB, ident, prot = _consts()
    KC = cfg.KC
    gT = np.concatenate([f(g)[0].reshape(KC, 128).T for g in (g_ffn1, g_mix, g_ffn2)], 1)
    shared = dict(
        w1g=f(w1_gate)[0], w1u=f(w1_up)[0], w1d=f(w1_down)[0], win=f(w_in)[0], wa=f(w_branch_a)[0], wb=f(w_branch_b)[0],
        wo=f(w_out)[0], w2g=f(w2_gate)[0], w2u=f(w2_up)[0], w2d=f(w2_down)[0],
        gT=np.ascontiguousarray(gT), gfin=f(g_final), sink=f(sink_b)[0],
        bandA=bandA, bandB=bandB, ident=ident, prot=prot,
    )
    in_maps = []
    for c in range(N_CORES):
        t = tabs[c]
        xe = np.zeros((EXT, D), np.float32)
        xe[t["valid"]] = X[t["idx"][t["valid"]]]
        m = dict(shared)
        m.update(xe=xe, ropeC=t["ropeC"], ropeS=t["ropeS"], seqA=t["seqA"], seqB=t["seqB"])
        in_maps.append(m)
    key = (cfg.D, cfg.F, cfg.HP)
    if key not in _NC_CACHE:
        uo = []
        build(cfg, None, uo)
        _NC_CACHE[key] = build(cfg, uo)
    nc = _NC_CACHE[key]
    res = run_bass_kernel_spmd(nc, in_maps, core_ids=list(range(N_CORES)), trace=trace)
    Y = np.concatenate([np.asarray(r["y"]) for r in res.results], 0).astype(np.float32)
    n0 = xp.shape[0] * xp.shape[1]
    return (Y[:n0].reshape(xp.shape), Y[n0:].reshape(xs.shape)), res


def kernel(**inputs):
    out, _ = run(Cfg(), **inputs)
    return out
```

```python
import math
from contextlib import ExitStack

import numpy as np
import ml_dtypes

import concourse.bass as bass
import concourse.mybir as mybir
from concourse.bass_utils import run_bass_kernel_spmd

F32 = mybir.dt.float32
BF16 = mybir.dt.bfloat16
AF = mybir.ActivationFunctionType
ALU = mybir.AluOpType
AX = mybir.AxisListType

N_CORES = 8
OWN = 3072
HALO = 1024
EXT = OWN + 2 * HALO
T = 512
NT_EXT = EXT // T
OWN_T0 = HALO // T
NT_OWN = OWN // T
HD = 128
ROT = 32
THETA = 500000.0
EPS = 1e-6
NEGBIG = -30000.0
DIL = (1, 4, 16)
NQ = 40
NK = 28
VC = 3584
OW = 129


class Cfg:
    def __init__(self, D=4096, F=11008, HP=4):
        self.D, self.F, self.HP = D, F, HP
        self.KC = D // 128
        self.FC = F // 128
        self.IN_W = 12288 + 2 * D


class _Op:
    __slots__ = ("eng", "fn", "dma", "cum", "deps", "signal", "sigval")


class Sched:
    ENGS = ("sp", "act", "dve", "pool", "pe")

    def __init__(self):
        self.ops = []
        self.last_w = {}
        self.readers = {}
        self.dma_cum = {}
        self.last_on = {}

    def add(self, eng, fn, reads=(), writes=(), dma=None, extra=()):
        op = _Op()
        op.eng, op.fn, op.dma, op.signal, op.sigval = eng, fn, dma, False, 0
        idx = len(self.ops)
        deps = set(extra)
        for r in reads:
            w = self.last_w.get(r)
            if w is not None:
                deps.add(w)
        for r in writes:
            w = self.last_w.get(r)
            if w is not None:
                deps.add(w)
            rd = self.readers.get(r)
            if rd:
                deps.update(rd.values())
        if dma is not None:
            self.dma_cum[dma] = self.dma_cum.get(dma, 0) + 16
            op.cum = self.dma_cum[dma]
        else:
            op.cum = 0
        rkey = ("dma", dma) if dma is not None else eng
        for r in reads:
            self.readers.setdefault(r, {})[rkey] = idx
        for r in writes:
            self.last_w[r] = idx
            self.readers[r] = {}
        fd = []
        for j in deps:
            o = self.ops[j]
            if o.dma is not None:
                fd.append(j)
            elif o.eng == eng and dma is None and eng == "pe":
                continue
            else:
                o.signal = True
                fd.append(j)
        op.deps = fd
        self.ops.append(op)
        if fn is not None:
            self.last_on[eng] = idx
        return idx

    def barrier(self):
        lasts = dict(self.last_on)
        dma_last = {}
        for i, o in enumerate(self.ops):
            if o.dma is not None:
                dma_last[o.dma] = i
        ex = list(lasts.values()) + list(dma_last.values())
        for e in self.ENGS:
            self.add(e, None, extra=[j for j in ex if j != lasts.get(e) or self.ops[j].dma is not None])
        self.last_on = {}

    def emit(self, nc, stack):
        cnt = {e: 0 for e in self.ENGS}
        for o in self.ops:
            if o.signal and o.dma is None:
                cnt[o.eng] += 1
                o.sigval = cnt[o.eng]
        esem = {e: stack.enter_context(nc.semaphore("s_" + e)) for e in self.ENGS}
        dsem = {}
        for k in self.dma_cum:
            dsem[k] = stack.enter_context(nc.semaphore("d%d" % len(dsem)))
        streams = {e: [] for e in self.ENGS}
        for o in self.ops:
            streams[o.eng].append(o)
        ops = self.ops
        final = [(dsem[k], v) for k, v in self.dma_cum.items()]

        def run(eng_name, E):
            waited = {}
            for o in streams[eng_name]:
                for j in o.deps:
                    d = ops[j]
                    if d.dma is not None:
                        sem, val = dsem[d.dma], d.cum
                    else:
                        sem, val = esem[d.eng], d.sigval
                    key = id(sem)
                    if waited.get(key, 0) < val:
                        E.wait_ge(sem, val)
                        waited[key] = val
                if o.fn is None:
                    continue
                ins = o.fn(E)
                if o.dma is not None:
                    ins.then_inc(dsem[o.dma], 16)
                elif o.signal:
                    ins.then_inc(esem[eng_name], 1)
            if eng_name == "sp":
                for sem, val in final:
                    E.wait_ge(sem, val)

        block = stack.enter_context(nc.Block())

        @block.sync
        def _(e):
            run("sp", e)

        @block.scalar
        def _(e):
            run("act", e)

        @block.vector
        def _(e):
            run("dve", e)

        @block.gpsimd
        def _(e):
            run("pool", e)

        @block.tensor
        def _(e):
            run("pe", e)


class Pipe:
    def __init__(self, nslots=3, pd=2):
        self.n = 0
        self.ns = nslots
        self.pd = pd
        self.pending = []

    def stage(self, compute, load=None):
        slot = None
        if load is not None:
            slot = self.n % self.ns
            self.n += 1
            load(slot)
        self.pending.append((compute, slot))
        while len(self.pending) > self.pd:
            c, s = self.pending.pop(0)
            c(s)

    def flush(self):
        while self.pending:
            c, s = self.pending.pop(0)
            c(s)


def build(cfg, piece_order=None, use_order=None):
    if use_order is None:
        use_order = []
    D, F, KC, FC, HP, IN_W = cfg.D, cfg.F, cfg.KC, cfg.FC, cfg.HP, cfg.IN_W
    nc = bass.Bass("TRN2", target_bir_lowering=False)
    S = Sched()
    P = Pipe()

    def din(name, shape, dt=F32):
        return nc.dram_tensor(name, list(shape), dt, kind="ExternalInput").ap()

    def dscr(name, shape, dt):
        return nc.dram_tensor(name, list(shape), dt, kind="Internal").ap()

    xe = din("xe", [EXT, D])
    wf = {
        "w1g": din("w1g", [D, F]), "w1u": din("w1u", [D, F]), "w1d": din("w1d", [F, D]),
        "win": din("win", [D, IN_W]), "wa": din("wa", [1024, D]), "wb": din("wb", [2048, D]),
        "wo": din("wo", [D, D]),
        "w2g": din("w2g", [D, F]), "w2u": din("w2u", [D, F]), "w2d": din("w2d", [F, D]),
    }
    gT_in = din("gT", [128, 3 * KC])
    gfin_in = din("gfin", [D])
    sink_in = din("sink", [16])
    ropeC_in = din("ropeC", [ROT, EXT])
    ropeS_in = din("ropeS", [ROT, EXT])
    seqA_in = din("seqA", [3, EXT], BF16)
    seqB_in = din("seqB", [3, EXT], BF16)
    bandA_in = din("bandA", [64, 192])
    bandB_in = din("bandB", [128, 384])
    ident_in = din("ident", [128, 128], BF16)
    prot_in = din("prot", [ROT, ROT], BF16)
    y_out = nc.dram_tensor("y", [OWN, D], F32, kind="ExternalOutput").ap()

    wb16 = {k: dscr(k + "_b", list(v.shape), BF16) for k, v in wf.items()}
    K_scr = dscr("K_scr", [NK, 128, EXT], BF16)
    Q_scr = dscr("Q_scr", [NQ, 128, OWN], BF16)
    V_scr = dscr("V_scr", [EXT, VC], BF16)
    G_scr = dscr("G_scr", [2 * KC, 128, OWN], BF16)
    h_scr = dscr("h_scr", [OWN, D], F32)
    O_scr = dscr("O_scr", [OWN, 24, OW], F32)
    YA_scr = dscr("YA_scr", [8, 128, OWN], BF16)
    YB_scr = dscr("YB_scr", [16, 128, OWN], BF16)

    HPC = -(-FC // HP)
    parts = []
    c = 0
    for i in range(HP):
        n = FC // HP + (1 if i < FC % HP else 0)
        if n:
            parts.append((c, n))
        c += n
    WSLOT = max(KC * 512, 12288, 8192)
    sizes = {}
    off = {}
    cur = 0

    def region(name, nbytes):
        nonlocal cur
        off[name] = cur
        sizes[name] = nbytes
        cur += (nbytes + 63) // 64 * 64

    region("gT", 3 * KC * 4)
    region("ident", 256)
    region("prot", 64)
    region("sink", 64)
    region("stat", 256)
    region("sil", 2 * 2048)
    region("mt", 2 * 2048)
    SMALL_END = cur
    region("hbuf", max(4 * D * 4, 28672))
    region("xnT", KC * 512 * 2)
    region("hid", max(HPC * 1024, 1024))
    region("wst", 3 * WSLOT)
    region("aux", max(24576, 6 * D))
    ARENA = cur
    ARENA = max(ARENA, 190 * 1024)
    stack = ExitStack()
    arena = stack.enter_context(nc.sbuf_tensor("arena", [128, ARENA // 4], F32))
    banks = [stack.enter_context(nc.psum_tensor("ps%d" % i, [128, 512], F32)) for i in range(8)]

    def carve(o, nbytes, dt):
        v = arena[:, o // 4:(o + nbytes) // 4]
        return v.bitcast(dt) if dt != F32 else v

    def reg(name, dt, o=0, nbytes=None):
        return carve(off[name] + o, sizes[name] - o if nbytes is None else nbytes, dt)

    hbuf = reg("hbuf", F32, 0, 4 * D * 4).rearrange("p (b d) -> p b d", b=4)
    xnT = reg("xnT", BF16).rearrange("p (k t) -> p k t", t=T)
    hid = reg("hid", BF16).rearrange("p (k t) -> p k t", t=T)
    wst = [reg("wst", BF16, s * WSLOT, WSLOT) for s in range(3)]
    xnb = reg("aux", BF16, 0, D * 2)
    gfin = reg("aux", F32, D * 2, D * 4)
    yT = reg("aux", BF16, 0, 24576).rearrange("p (k t) -> p k t", t=T)
    gT = reg("gT", F32)
    ident = reg("ident", BF16)
    prot = reg("prot", BF16)[0:ROT, :]
    sink = reg("sink", F32)
    stat = reg("stat", F32)
    sil = [reg("sil", F32, i * 2048, 2048) for i in range(2)]
    mt = [reg("mt", F32, i * 2048, 2048) for i in range(2)]
    HB = off["hbuf"]
    xb = [carve(HB + i * 2048, 2048, BF16).rearrange("p (c t) -> p c t", t=T) for i in range(2)]
    sgt = [carve(HB + 4096 + i * 2048, 2048, BF16).rearrange("p (c t) -> p c t", t=T) for i in range(2)]
    vb = [carve(HB + 8192 + i * 4096, 4096, BF16).rearrange("p (b c) -> p b c", b=4) for i in range(2)]
    rC = carve(HB + 16384, 2048, F32)[0:ROT, :]
    rS = carve(HB + 18432, 2048, F32)[0:ROT, :]
    rt1 = [carve(HB + 20480 + i * 2048, 2048, F32)[0:ROT, :] for i in range(2)]
    rt2 = [carve(HB + 24576 + i * 2048, 2048, F32)[0:ROT, :] for i in range(2)]
    HBT = ["xb0", "xb1", "sg0", "sg1", "vb0", "vb1", "rC", "rS", "rt10", "rt11", "rt20", "rt21"]
    HB_ALL = ["hbuf"] + HBT

    bank_ctr = [0]

    def alloc_bank():
        b = bank_ctr[0] % 8
        bank_ctr[0] += 1
        return b

    def bk(b):
        return banks[b][:]

    def bkbf(b):
        return banks[b][:].bitcast(BF16)

    def bn(b):
        return "ps%d" % b

    ctr = {"sil": 0, "mt": 0, "xb": 0, "sg": 0, "vb": 0, "rt": 0}

    def rr(name, n=2):
        v = ctr[name] % n
        ctr[name] += 1
        return v

    def ld_const(dst, src, name):
        S.add("sp", lambda E, d=dst, s=src: E.dma_start(out=d, in_=s), writes=[name], dma="c_" + name)

    ld_const(gT, gT_in, "gT")
    ld_const(ident, ident_in, "ident")
    ld_const(prot, prot_in, "prot")
    ld_const(sink[:, 0:16], sink_in.partition_broadcast(128), "sink")

    def wpiece(k, row, col):
        return "W_" + k

    def cast_weights(names, extra=()):
        for k in names:
            src, dst = wf[k], wb16[k]
            rows, cols = src.shape
            step = max(1, (4 * 1024 * 1024) // cols)
            r0 = 0
            while r0 < rows:
                r1 = min(rows, r0 + step)
                S.add("pool", lambda E, d=dst[r0:r1, :], s=src[r0:r1, :]: E.dma_start(out=d, in_=s),
                      writes=["W_" + k], dma="cast_" + k, extra=list(extra))
                r0 = r1

    cast_weights(("w1g", "w1u", "w1d", "win"))

    def norm_to_T(gcol, src_names, tag):
        def comp(_):
            for b in range(4):
                ss = stat[:, b:b + 1]
                rs = stat[:, 8 + b:9 + b]
                S.add("dve", lambda E, o=ss: E.memset(o, 0.0), writes=["stat"])
                S.add("act", lambda E, o=xnb, i=hbuf[:, b, :], a=ss: E.activation(out=o, in_=i, func=AF.Square, accum_out=a),
                      reads=src_names, writes=["xnb", "stat"])
                S.add("dve", lambda E, o=rs, i=ss: E.tensor_scalar(out=o, in0=i, scalar1=1.0 / D, scalar2=EPS, op0=ALU.mult, op1=ALU.add),
                      reads=["stat"], writes=["stat"])
                S.add("act", lambda E, o=rs: E.sqrt(out=o, in_=o), reads=["stat"], writes=["stat"])
                S.add("dve", lambda E, o=rs: E.reciprocal(out=o, in_=o), reads=["stat"], writes=["stat"])
                S.add("act", lambda E, o=xnb, i=hbuf[:, b, :], s=rs: E.mul(out=o, in_=i, mul=s),
                      reads=src_names + ["stat"], writes=["xnb"])
                gsz = min(8, KC)
                for k0 in range(0, KC, gsz):
                    bb = alloc_bank()
                    pv = bkbf(bb)
                    for j in range(gsz):
                        S.add("pe", lambda E, o=pv[:, j * 128:(j + 1) * 128], i=xnb[:, (k0 + j) * 128:(k0 + j + 1) * 128]:
                              E.transpose(out=o, in_=i, identity=ident), reads=["xnb", "ident"], writes=[bn(bb)])
                    gsl = gT[:, gcol * KC + k0: gcol * KC + k0 + gsz].unsqueeze(2).to_broadcast([128, gsz, 128])
                    S.add("dve", lambda E, o=xnT[:, k0:k0 + gsz, b * 128:(b + 1) * 128],
                          i=pv[:, 0:gsz * 128].rearrange("p (k t) -> p k t", t=128), g=gsl:
                          E.tensor_tensor(out=o, in0=i, in1=g, op=ALU.mult),
                          reads=[bn(bb), "gT"], writes=["xnT"])
        P.stage(comp)

    def ffn(wg, wu, wd, wtag):
        Wg, Wu, Wd = wb16[wg], wb16[wu], wb16[wd]
        for (c0, nch) in parts:
            for cc in range(c0, c0 + nch, 2):
                n2 = min(2, c0 + nch - cc)
                st = {}

                def ldg(slot, cc=cc, n2=n2):
                    v = wst[slot][:, 0:KC * n2 * 128].rearrange("p (k c) -> p k c", k=KC)
                    S.add("sp", lambda E, d=v, s=Wg[:, cc * 128:(cc + n2) * 128].rearrange("(k p) c -> p k c", p=128):
                          E.dma_start(out=d, in_=s), reads=[wpiece(wg, 0, cc * 128)], writes=["wst%d" % slot], dma="wst%d" % slot)

                def ldu(slot, cc=cc, n2=n2):
                    v = wst[slot][:, 0:KC * n2 * 128].rearrange("p (k c) -> p k c", k=KC)
                    S.add("sp", lambda E, d=v, s=Wu[:, cc * 128:(cc + n2) * 128].rearrange("(k p) c -> p k c", p=128):
                          E.dma_start(out=d, in_=s), reads=[wpiece(wu, 0, cc * 128)], writes=["wst%d" % slot], dma="wst%d" % slot)

                def cg(slot, n2=n2, st=st):
                    v = wst[slot][:, 0:KC * n2 * 128].rearrange("p (k c) -> p k c", k=KC)
                    st["g"] = []
                    for ci in range(n2):
                        bb = alloc_bank()
                        st["g"].append(bb)
                        for kc in range(KC):
                            S.add("pe", lambda E, o=bk(bb), l=v[:, kc, ci * 128:(ci + 1) * 128], r=xnT[:, kc, :], a=(kc == 0), z=(kc == KC - 1):
                                  E.matmul(o, l, r, start=a, stop=z), reads=["wst%d" % slot, "xnT"], writes=[bn(bb)])

                def cu(slot, n2=n2, st=st, cc=cc, c0=c0):
                    v = wst[slot][:, 0:KC * n2 * 128].rearrange("p (k c) -> p k c", k=KC)
                    for ci in range(n2):
                        bb = alloc_bank()
                        for kc in range(KC):
                            S.add("pe", lambda E, o=bk(bb), l=v[:, kc, ci * 128:(ci + 1) * 128], r=xnT[:, kc, :], a=(kc == 0), z=(kc == KC - 1):
                                  E.matmul(o, l, r, start=a, stop=z), reads=["wst%d" % slot, "xnT"], writes=[bn(bb)])
                        gb = st["g"][ci]
                        si = rr("sil")
                        S.add("act", lambda E, o=sil[si], i=bk(gb): E.activation(out=o, in_=i, func=AF.Silu),
                              reads=[bn(gb)], writes=["sil%d" % si])
                        S.add("dve", lambda E, o=hid[:, cc - c0 + ci, :], a=sil[si], b=bk(bb): E.tensor_tensor(out=o, in0=a, in1=b, op=ALU.mult),
                              reads=["sil%d" % si, bn(bb)], writes=["hid"])

                P.stage(cg, ldg)
                P.stage(cu, ldu)
            CH = max(1, min(nch, WSLOT // 1024))
            subs = [(s0, min(CH, nch - s0)) for s0 in range(0, nch, CH)]
            for cgi in range(max(1, D // 512)):
                ncol = min(512, D)
                st = {}
                for si_, (s0, sn) in enumerate(subs):
                    def ldd(slot, s0=s0, sn=sn, cgi=cgi, c0=c0, ncol=ncol):
                        v = wst[slot][:, 0:sn * ncol].rearrange("p (k c) -> p k c", k=sn)
                        src = Wd[(c0 + s0) * 128:(c0 + s0 + sn) * 128, cgi * ncol:(cgi + 1) * ncol].rearrange("(k p) c -> p k c", p=128)
                        S.add("sp", lambda E, d=v, s=src: E.dma_start(out=d, in_=s), reads=[wpiece(wd, (c0 + s0) * 128, 0)],
                              writes=["wst%d" % slot], dma="wst%d" % slot)

                    def cd(slot, s0=s0, sn=sn, cgi=cgi, nch=nch, st=st, first=(si_ == 0), last=(si_ == len(subs) - 1), ncol=ncol):
                        v = wst[slot][:, 0:sn * ncol].rearrange("p (k c) -> p k c", k=sn)
                        if first:
                            st["b"] = [alloc_bank() for _ in range(4)]
                        for b in range(4):
                            bb = st["b"][b]
                            for k in range(sn):
                                S.add("pe", lambda E, o=bk(bb)[:, 0:ncol], l=hid[:, s0 + k, b * 128:(b + 1) * 128], r=v[:, k, :],
                                      a=(s0 + k == 0), z=(s0 + k == nch - 1): E.matmul(o, l, r, start=a, stop=z),
                                      reads=["wst%d" % slot, "hid"], writes=[bn(bb)])
                        if last:
                            for b in range(4):
                                bb = st["b"][b]
                                hs = hbuf[:, b, cgi * ncol:(cgi + 1) * ncol]
                                S.add("dve", lambda E, o=hs, i=bk(bb)[:, 0:ncol]: E.scalar_tensor_tensor(out=o, in0=i, scalar=0.5, in1=o, op0=ALU.mult, op1=ALU.add),
                                      reads=[bn(bb), "hbuf"], writes=["hbuf"])
                    P.stage(cd, ldd)

    def proj_fm(col0, ncols, kind, tcol, scr, scr0, t0e):
        Wi = wb16["win"]
        for c in range(col0, col0 + ncols, 256):
            def ld(slot, c=c):
                v = wst[slot][:, 0:KC * 256].rearrange("p (k c) -> p k c", k=KC)
                S.add("sp", lambda E, d=v, s=Wi[:, c:c + 256].rearrange("(k p) c -> p k c", p=128): E.dma_start(out=d, in_=s),
                      reads=[wpiece("win", 0, c)], writes=["wst%d" % slot], dma="wst%d" % slot)

            def cp(slot, c=c):
                v = wst[slot][:, 0:KC * 256].rearrange("p (k c) -> p k c", k=KC)
                if kind == "g":
                    oi = rr("sg")
                    obuf, oname = sgt[oi], "sg%d" % oi
                else:
                    oi = rr("xb")
                    obuf, oname = xb[oi], "xb%d" % oi
                for ci in range(2):
                    bb = alloc_bank()
                    for kc in range(KC):
                        S.add("pe", lambda E, o=bk(bb), l=v[:, kc, ci * 128:(ci + 1) * 128], r=xnT[:, kc, :], a=(kc == 0), z=(kc == KC - 1):
                              E.matmul(o, l, r, start=a, stop=z), reads=["wst%d" % slot, "xnT"], writes=[bn(bb)])
                    if kind == "g":
                        S.add("act", lambda E, o=obuf[:, ci, :], i=bk(bb): E.activation(out=o, in_=i, func=AF.Sigmoid),
                              reads=[bn(bb)], writes=[oname])
                    else:
                        S.add("act", lambda E, o=obuf[:, ci, :], i=bk(bb): E.copy(out=o, in_=i),
                              reads=[bn(bb)], writes=[oname])
                        b2 = alloc_bank()
                        ri = rr("rt")
                        S.add("pe", lambda E, o=bk(b2)[0:ROT, :], l=prot, r=obuf[0:ROT, ci, :]: E.matmul(o, l, r, start=True, stop=True),
                              reads=[oname, "prot"], writes=[bn(b2)])
                        S.add("dve", lambda E, o=rt1[ri], a=obuf[0:ROT, ci, :], b=rC: E.tensor_tensor(out=o, in0=a, in1=b, op=ALU.mult),
                              reads=[oname, "rC"], writes=["rt1%d" % ri])
                        S.add("dve", lambda E, o=rt2[ri], a=bk(b2)[0:ROT, :], b=rS: E.tensor_tensor(out=o, in0=a, in1=b, op=ALU.mult),
                              reads=[bn(b2), "rS"], writes=["rt2%d" % ri])
                        S.add("dve", lambda E, o=obuf[0:ROT, ci, :], a=rt1[ri], b=rt2[ri]: E.tensor_tensor(out=o, in0=a, in1=b, op=ALU.add),
                              reads=["rt1%d" % ri, "rt2%d" % ri], writes=[oname])
                h0 = scr0 + (c - col0) // 128
                dst = scr[h0:h0 + 2, :, tcol:tcol + T].rearrange("c p t -> p c t")
                S.add("sp", lambda E, d=dst, s=obuf: E.dma_start(out=d, in_=s), reads=[oname], dma=oname)
            P.stage(cp, ld)

    def proj_v(col0, ncols, vcol0, t0e):
        Wi = wb16["win"]
        KH = max(1, KC // 2)
        halves = [(k0, min(KH, KC - k0)) for k0 in range(0, KC, KH)]
        for c in range(col0, col0 + ncols, 512):
            st = {}
            for hi, (k0, kn) in enumerate(halves):
                def ld(slot, c=c, k0=k0, kn=kn):
                    v = wst[slot][:, 0:kn * 512].rearrange("p (k c) -> p k c", k=kn)
                    S.add("sp", lambda E, d=v, s=Wi[k0 * 128:(k0 + kn) * 128, c:c + 512].rearrange("(k p) c -> p k c", p=128): E.dma_start(out=d, in_=s),
                          reads=[wpiece("win", 0, c)], writes=["wst%d" % slot], dma="wst%d" % slot)

                def cp(slot, c=c, k0=k0, kn=kn, st=st, first=(hi == 0), last=(hi == len(halves) - 1)):
                    v = wst[slot][:, 0:kn * 512].rearrange("p (k c) -> p k c", k=kn)
                    if first:
                        st["b"] = [alloc_bank() for _ in range(4)]
                        st["o"] = rr("vb")
                    oi = st["o"]
                    for b in range(4):
                        bb = st["b"][b]
                        for k in range(kn):
                            S.add("pe", lambda E, o=bk(bb), l=xnT[:, k0 + k, b * 128:(b + 1) * 128], r=v[:, k, :], a=(k0 + k == 0), z=(k0 + k == KC - 1):
                                  E.matmul(o, l, r, start=a, stop=z), reads=["wst%d" % slot, "xnT"], writes=[bn(bb)])
                    if last:
                        for b in range(4):
                            bb = st["b"][b]
                            S.add("act", lambda E, o=vb[oi][:, b, :], i=bk(bb): E.copy(out=o, in_=i), reads=[bn(bb)], writes=["vb%d" % oi])
                        vc = vcol0 + (c - col0)
                        dst = V_scr[t0e:t0e + T, vc:vc + 512].rearrange("(b p) c -> p b c", p=128)
                        S.add("sp", lambda E, d=dst, s=vb[oi]: E.dma_start(out=d, in_=s), reads=["vb%d" % oi], dma="vb%d" % oi)
                P.stage(cp, ld)

    for ti in range(NT_EXT):
        t0e = ti * T
        own = OWN_T0 <= ti < OWN_T0 + NT_OWN
        tcol = (ti - OWN_T0) * T

        def ldx(_, t0e=t0e):
            S.add("sp", lambda E, d=hbuf, s=xe[t0e:t0e + T, :].rearrange("(b p) d -> p b d", p=128): E.dma_start(out=d, in_=s),
                  writes=HB_ALL, dma="hbuf")
        P.stage(ldx)
        if ti == 3:
            def cast_late(_):
                cast_weights(("wa", "wb", "wo", "w2g", "w2u", "w2d"), extra=[S.last_on["pe"]])
            P.stage(cast_late)
        norm_to_T(0, ["hbuf"], "n1")
        ffn("w1g", "w1u", "w1d", "f1")
        if own:
            def sth(_, tcol=tcol):
                S.add("sp", lambda E, d=h_scr[tcol:tcol + T, :].rearrange("(b p) d -> p b d", p=128), s=hbuf: E.dma_start(out=d, in_=s),
                      reads=HB_ALL, dma="hst")
            P.stage(sth)
        norm_to_T(1, HB_ALL, "nm")

        def ldrope(_, t0e=t0e):
            S.add("sp", lambda E, d=rC, s=ropeC_in[:, t0e:t0e + T]: E.dma_start(out=d, in_=s), writes=["rC"], dma="rC")
            S.add("sp", lambda E, d=rS, s=ropeS_in[:, t0e:t0e + T]: E.dma_start(out=d, in_=s), writes=["rS"], dma="rS")
        P.stage(ldrope)
        far = ti == 0 or ti == NT_EXT - 1
        if far:
            proj_fm(3072 + 2048, 1024, "k", t0e, K_scr, 16, t0e)
            proj_v(6144 + 2048, 1024, 2048, t0e)
        else:
            proj_fm(3072, 3072, "k", t0e, K_scr, 0, t0e)
            proj_fm(11264, 512, "k", t0e, K_scr, 24, t0e)
            proj_v(6144, 3072, 0, t0e)
            proj_v(11776, 512, 3072, t0e)
        if own:
            proj_fm(0, 3072, "q", tcol, Q_scr, 0, t0e)
            proj_fm(9216, 2048, "q", tcol, Q_scr, 24, t0e)
            proj_fm(12288, 2 * D, "g", tcol, G_scr, 0, t0e)
    P.flush()
    S.barrier()

    acur = [SMALL_END]

    def acarve(nbytes, dt):
        o = acur[0]
        acur[0] += (nbytes + 63) // 64 * 64
        assert acur[0] <= ARENA
        return carve(o, nbytes, dt)

    NSET = 8
    kTb = [acarve(EXT * 2, BF16) for _ in range(2)]
    qTb = [acarve(OWN * 2, BF16) for _ in range(2)]
    vsb = [acarve(80 * 128 * 2, BF16) for _ in range(2)]
    sA = acarve(EXT * 2, BF16)[0:3, :]
    sB = acarve(EXT * 2, BF16)[0:3, :]
    bandA = acarve(192 * 4, F32)[0:64, :]
    bandB = acarve(384 * 4, F32)
    Smb = [acarve(384 * 4, F32) for _ in range(NSET)]
    Pb = [acarve(384 * 2, BF16) for _ in range(NSET)]
    PTb = [acarve(384 * 2, BF16) for _ in range(NSET)]
    ast = acarve(1024, F32)
    nsink = acarve(64, F32)
    amark = acur[0]
    obufs = [acarve(48 * OW * 4, F32) for _ in range(2)]

    S.add("sp", lambda E: E.dma_start(out=sA, in_=seqA_in), writes=["sA"], dma="sA")
    S.add("sp", lambda E: E.dma_start(out=sB, in_=seqB_in), writes=["sB"], dma="sB")
    S.add("sp", lambda E: E.dma_start(out=bandA, in_=bandA_in), writes=["bandA"], dma="bandA")
    S.add("sp", lambda E: E.dma_start(out=bandB, in_=bandB_in), writes=["bandB"], dma="bandB")
    S.add("dve", lambda E: E.tensor_scalar(out=nsink[:, 0:16], in0=sink[:, 0:16], scalar1=-1.0, scalar2=None, op0=ALU.mult),
          reads=["sink"], writes=["nsink"])
    scale = 1.0 / math.sqrt(HD)
    uctr = [0]

    def attn_batch(units, QN, NKEY, band, bname, is_b):
        NB = 3
        KB = NKEY // NB
        R = []
        for u in units:
            k = uctr[0] % NSET
            sc = (uctr[0] % 16) * 8
            uctr[0] += 1
            b1 = alloc_bank()
            b2 = alloc_bank() if is_b else b1
            R.append((k, sc, b1, b2))

        def regs(b1, b2):
            if is_b:
                return bk(b1)[0:QN, 0:NKEY], bk(b1)[0:128, NKEY:NKEY + 128], bkbf(b2)[0:KB, 0:NB * QN]
            return bk(b1)[0:QN, 0:NKEY], bk(b1)[0:QN, NKEY:NKEY + 128], bkbf(b1)[0:KB, 2 * (NKEY + 128):2 * (NKEY + 128) + NB * QN]

        for u, (k, sc, b1, b2) in zip(units, R):
            sps, ops_, tps = regs(b1, b2)
            S.add("pe", lambda E, o=sps, l=u["qsl"], r=u["ksl"]: E.matmul(o, l, r, start=True, stop=False),
                  reads=[u["kname"], u["qname"]], writes=[bn(b1)])
            S.add("pe", lambda E, o=sps, l=u["qtok"], r=u["ktok"]: E.matmul(o, l, r, start=False, stop=True),
                  reads=["sA", "sB"], writes=[bn(b1)])
        for u, (k, sc, b1, b2) in zip(units, R):
            sps, ops_, tps = regs(b1, b2)
            S.add("dve", lambda E, o=Smb[k][0:QN, 0:NKEY], i=sps, b=band: E.scalar_tensor_tensor(out=o, in0=i, scalar=scale, in1=b, op0=ALU.mult, op1=ALU.add),
                  reads=[bn(b1), bname], writes=["Sm%d" % k])
        for u, (k, sc, b1, b2) in zip(units, R):
            negm = ast[0:QN, sc:sc + 1]
            S.add("dve", lambda E, o=negm, i=Smb[k][0:QN, 0:NKEY]: E.reduce_max(out=o, in_=i, axis=AX.X, negate=True),
                  reads=["Sm%d" % k], writes=["ast%d" % sc])
            if is_b:
                S.add("dve", lambda E, o=negm, i=nsink[0:QN, u["hq"]:u["hq"] + 1]: E.tensor_tensor(out=o, in0=o, in1=i, op=ALU.min),
                      reads=["nsink"], writes=["ast%d" % sc])
            S.add("dve", lambda E, o=ast[0:QN, sc + 1:sc + 2]: E.memset(o, 0.0), writes=["astd%d" % sc])
        for u, (k, sc, b1, b2) in zip(units, R):
            negm = ast[0:QN, sc:sc + 1]
            den = ast[0:QN, sc + 1:sc + 2]
            S.add("act", lambda E, o=Pb[k][0:QN, 0:NKEY], i=Smb[k][0:QN, 0:NKEY], b=negm, a=den: E.activation(out=o, in_=i, func=AF.Exp, bias=b, scale=1.0, accum_out=a),
                  reads=["Sm%d" % k, "ast%d" % sc], writes=["P%d" % k, "astd%d" % sc])
            if is_b:
                S.add("act", lambda E, o=ast[0:QN, sc + 3:sc + 4], i=sink[0:QN, u["hq"]:u["hq"] + 1], b=negm: E.activation(out=o, in_=i, func=AF.Exp, bias=b, scale=1.0),
                      reads=["ast%d" % sc, "sink"], writes=["aste%d" % sc])
        for u, (k, sc, b1, b2) in zip(units, R):
            den = ast[0:QN, sc + 1:sc + 2]
            rden = ast[0:QN, sc + 2:sc + 3]
            if is_b:
                S.add("dve", lambda E, o=den, b=ast[0:QN, sc + 3:sc + 4]: E.tensor_tensor(out=o, in0=o, in1=b, op=ALU.add),
                      reads=["aste%d" % sc], writes=["astd%d" % sc])
            S.add("dve", lambda E, o=rden, i=den: E.reciprocal(out=o, in_=i), reads=["astd%d" % sc], writes=["astr%d" % sc])
            if is_b:
                S.add("dve", lambda E, o=Pb[k][0:QN, 0:NKEY], s=rden: E.tensor_scalar(out=o, in0=o, scalar1=s, scalar2=None, op0=ALU.mult),
                      reads=["astr%d" % sc], writes=["P%d" % k])
            else:
                S.add("act", lambda E, o=ast[0:QN, sc + 3:sc + 4], i=den: E.activation(out=o, in_=i, func=AF.Ln),
                      reads=["astd%d" % sc], writes=["aste%d" % sc])
        for u, (k, sc, b1, b2) in zip(units, R):
            sps, ops_, tps = regs(b1, b2)
            for b in range(NB):
                S.add("pe", lambda E, o=tps[:, b * QN:(b + 1) * QN], i=Pb[k][0:QN, b * KB:(b + 1) * KB]:
                      E.transpose(out=o, in_=i, identity=ident[0:QN, 0:QN]), reads=["P%d" % k, "ident"], writes=[bn(b2)])
        for u, (k, sc, b1, b2) in zip(units, R):
            sps, ops_, tps = regs(b1, b2)
            S.add("act", lambda E, o=PTb[k][0:KB, 0:NB * QN], i=tps: E.copy(out=o, in_=i), reads=[bn(b2)], writes=["PT%d" % k])
            if not is_b:
                S.add("pool", lambda E, o=u["lse"], a=ast[0:QN, sc + 3:sc + 4], b=ast[0:QN, sc:sc + 1]: E.tensor_tensor(out=o, in0=a, in1=b, op=ALU.subtract),
                      reads=["aste%d" % sc, "ast%d" % sc], writes=[u["oname"] + "l"])
        for u, (k, sc, b1, b2) in zip(units, R):
            sps, ops_, tps = regs(b1, b2)
            for b in range(NB):
                if is_b:
                    S.add("pe", lambda E, o=ops_, l=u["vblk"][b], r=PTb[k][0:KB, b * QN:(b + 1) * QN], a=(b == 0), z=(b == NB - 1):
                          E.matmul(o, l, r, start=a, stop=z), reads=["PT%d" % k, u["vname"]], writes=[bn(b1)])
                else:
                    S.add("pe", lambda E, o=ops_, l=PTb[k][0:KB, b * QN:(b + 1) * QN], r=u["vblk"][b], a=(b == 0), z=(b == NB - 1):
                          E.matmul(o, l, r, start=a, stop=z), reads=["PT%d" % k, u["vname"]], writes=[bn(b1)])
        for u, (k, sc, b1, b2) in zip(units, R):
            sps, ops_, tps = regs(b1, b2)
            if is_b:
                S.add("act", lambda E, o=u["out"], i=ops_: E.copy(out=o, in_=i), reads=[bn(b1)], writes=[u["oname"]])
            else:
                S.add("dve", lambda E, o=u["out"], i=ops_, s=ast[0:QN, sc + 2:sc + 3]: E.tensor_scalar(out=o, in0=i, scalar1=s, scalar2=None, op0=ALU.mult),
                      reads=[bn(b1), "astr%d" % sc], writes=[u["oname"]])

    GB = 4
    for g, r in enumerate(DIL):
        nb = EXT // (64 * r)
        nt = OWN // (64 * r)
        blk_base = HALO // (64 * r) - 1
        for h in range(8):
            hh = g * 8 + h
            s2 = hh % 2
            kT, qT, vs, ob = kTb[s2], qTb[s2], vsb[s2], obufs[s2]
            kname, qname, vname, oname = "kT%d" % s2, "qT%d" % s2, "vs%d" % s2, "ob%d" % s2
            cut = 0 if g == 2 else T
            S.add("sp", lambda E, d=kT[:, cut:EXT - cut], s=K_scr[hh][:, cut:EXT - cut]: E.dma_start(out=d, in_=s), writes=[kname], dma=kname)
            S.add("sp", lambda E, d=qT, s=Q_scr[hh]: E.dma_start(out=d, in_=s), writes=[qname], dma=qname)
            vb0 = cut // (64 * r)
            vv = vs[0:64, :].rearrange("p (j b d) -> p j b d", j=r, b=nb)
            if cut == 0:
                vsrc = bass.AP(V_scr.tensor, g * 1024 + h * 128, [[r * VC, 64], [VC, r], [r * 64 * VC, nb], [1, 128]])
                S.add("sp", lambda E, d=vv, s=vsrc: E.dma_start(out=d, in_=s), writes=[vname], dma=vname)
            else:
                for j in range(r):
                    vsrc = bass.AP(V_scr.tensor, g * 1024 + h * 128 + j * VC + vb0 * r * 64 * VC, [[r * VC, 64], [r * 64 * VC, nb - 2 * vb0], [1, 128]])
                    S.add("sp", lambda E, d=vv[:, j, vb0:nb - vb0, :], s=vsrc: E.dma_start(out=d, in_=s), writes=[vname], dma=vname)
            obv = ob[0:64, :].rearrange("p (j n c) -> p j n c", j=r, n=nt)
            units = []
            for j in range(r):
                for n in range(nt):
                    q0 = r * 64 * n + j
                    k0 = HALO + r * 64 * (n - 1) + j

                    def ss(a, cnt, r=r):
                        return slice(a, a + (cnt - 1) * r + 1, r)
                    qe = HALO + q0
                    units.append(dict(qsl=qT[:, ss(q0, 64)], ksl=kT[:, ss(k0, 192)], qtok=sA[:, ss(qe, 64)], ktok=sB[:, ss(k0, 192)],
                                      kname=kname, qname=qname, vname=vname, oname=oname,
                                      vblk=[vv[:, j, blk_base + n + b, :] for b in range(3)],
                                      out=obv[:, j, n, 0:128], lse=obv[:, j, n, 128:129]))
            for i0 in range(0, len(units), GB):
                attn_batch(units[i0:i0 + GB], 64, 192, bandA, "bandA", False)
            odst = bass.AP(O_scr.tensor, hh * OW, [[r * 24 * OW, 64], [24 * OW, r], [64 * r * 24 * OW, nt], [1, OW]])
            S.add("sp", lambda E, d=odst, s=obv: E.dma_start(out=d, in_=s), reads=[oname, oname + "l"], writes=["O_scr"], dma="ost")
    S.barrier()
    acur[0] = amark
    ybufs = [acarve(OWN * 2, BF16) for _ in range(2)]
    obs = [acarve(24 * OW * 4, F32) for _ in range(2)]
    wts = acarve(64 * 4, F32)
    yaf = acarve(1024 * 4, F32)
    yab = acarve(1024 * 2, BF16)
    yaTb = [acarve(8 * 128 * 2, BF16) for _ in range(2)]
    def combine_block(tb):
        s2 = tb % 2
        ob = obs[s2].rearrange("p (h c) -> p h c", c=OW)
        S.add("sp", lambda E, d=ob, s=O_scr[tb * 128:(tb + 1) * 128]: E.dma_start(out=d, in_=s), reads=["O_scr"], writes=["obs%d" % s2], dma="obs%d" % s2)
        lv = ob[:, :, 128].rearrange("p (g h) -> p g h", g=3)
        M = wts[:, 0:8]
        w = wts[:, 8:32].rearrange("p (g h) -> p g h", g=3)
        ws = wts[:, 32:40]
        S.add("dve", lambda E, o=M, a=lv[:, 0, :], b=lv[:, 1, :]: E.tensor_tensor(out=o, in0=a, in1=b, op=ALU.max), reads=["obs%d" % s2], writes=["wts"])
        S.add("dve", lambda E, o=M, b=lv[:, 2, :]: E.tensor_tensor(out=o, in0=o, in1=b, op=ALU.max), reads=["obs%d" % s2], writes=["wts"])
        S.add("dve", lambda E, o=w, a=lv, b=M.unsqueeze(1).to_broadcast([128, 3, 8]): E.tensor_tensor(out=o, in0=a, in1=b, op=ALU.subtract),
              reads=["obs%d" % s2], writes=["wts"])
        S.add("act", lambda E, o=wts[:, 8:32]: E.activation(out=o, in_=o, func=AF.Exp), reads=["wts"], writes=["wts"])
        S.add("dve", lambda E, o=ws, a=w[:, 0, :], b=w[:, 1, :]: E.tensor_tensor(out=o, in0=a, in1=b, op=ALU.add), reads=["wts"], writes=["wts"])
        S.add("dve", lambda E, o=ws, b=w[:, 2, :]: E.tensor_tensor(out=o, in0=o, in1=b, op=ALU.add), writes=["wts"])
        S.add("dve", lambda E, o=ws: E.reciprocal(out=o, in_=o), writes=["wts"])
        S.add("dve", lambda E, o=w, b=ws.unsqueeze(1).to_broadcast([128, 3, 8]): E.tensor_tensor(out=o, in0=o, in1=b, op=ALU.mult), writes=["wts"])
        yv = yaf.rearrange("p (h d) -> p h d", d=128)
        for gi in range(3):
            og = ob[:, gi * 8:(gi + 1) * 8, 0:128]
            wg_ = w[:, gi, :].unsqueeze(2).to_broadcast([128, 8, 128])
            if gi == 0:
                S.add("dve", lambda E, o=yv, a=og, b=wg_: E.tensor_tensor(out=o, in0=a, in1=b, op=ALU.mult), reads=["obs%d" % s2, "wts"], writes=["yaf"])
            else:
                S.add("pool", lambda E, o=og, a=og, b=wg_: E.tensor_tensor(out=o, in0=a, in1=b, op=ALU.mult), reads=["wts"], writes=["obs%d" % s2])
                S.add("dve", lambda E, o=yv, a=yv, b=og: E.tensor_tensor(out=o, in0=a, in1=b, op=ALU.add), reads=["obs%d" % s2], writes=["yaf"])
        S.add("act", lambda E, o=yab, i=yaf: E.copy(out=o, in_=i), reads=["yaf"], writes=["yab"])
        tb_ = alloc_bank()
        for hq in range(8):
            S.add("pe", lambda E, o=bkbf(tb_)[:, hq * 128:(hq + 1) * 128], i=yab[:, hq * 128:(hq + 1) * 128]: E.transpose(out=o, in_=i, identity=ident),
                  reads=["yab", "ident"], writes=[bn(tb_)])
        yTv = yaTb[s2].rearrange("p (h t) -> p h t", h=8)
        S.add("dve", lambda E, o=yaTb[s2], i=bkbf(tb_): E.tensor_copy(out=o, in_=i),
              reads=[bn(tb_)], writes=["yaT%d" % s2])
        S.add("sp", lambda E, d=YA_scr[:, :, tb * 128:(tb + 1) * 128].rearrange("h p t -> p h t"), s=yTv: E.dma_start(out=d, in_=s),
              reads=["yaT%d" % s2], dma="yaT%d" % s2)

    comb_next = [0]

    def maybe_combine(force=False):
        if comb_next[0] < OWN // 128:
            combine_block(comb_next[0])
            comb_next[0] += 1

    for hq in range(16):
        kv = hq // 4
        s2 = hq % 2
        qT, yb = qTb[s2], ybufs[s2]
        if hq % 4 == 0:
            kT, vs = kTb[kv % 2], vsb[kv % 2]
            kname, vname = "kT%d" % (kv % 2), "vs%d" % (kv % 2)
            S.add("sp", lambda E, d=kT[:, T:EXT - T], s=K_scr[24 + kv][:, T:EXT - T]: E.dma_start(out=d, in_=s), writes=[kname], dma=kname)
            vvB = vs[:, 0:40 * 128].rearrange("p (b d) -> p b d", d=128)
            S.add("sp", lambda E, d=vvB[:, 4:36, :], s=V_scr[T:EXT - T, 3072 + kv * 128:3072 + (kv + 1) * 128].rearrange("(b p) d -> p b d", p=128): E.dma_start(out=d, in_=s),
                  writes=[vname], dma=vname)
        qname = "qT%d" % s2
        S.add("sp", lambda E, d=qT, s=Q_scr[24 + hq]: E.dma_start(out=d, in_=s), writes=[qname], dma=qname)
        units = []
        for n in range(OWN // 128):
            q0 = 128 * n
            k0 = HALO + 128 * (n - 1)
            units.append(dict(qsl=qT[:, q0:q0 + 128], ksl=kT[:, k0:k0 + 384], qtok=sA[:, HALO + q0:HALO + q0 + 128], ktok=sB[:, k0:k0 + 384],
                              kname=kname, qname=qname, vname=vname, oname="yb%d" % s2, hq=hq,
                              vblk=[vvB[:, HALO // 128 + n - 1 + b, :] for b in range(3)], out=yb[:, q0:q0 + 128]))
        for bi, i0 in enumerate(range(0, len(units), GB)):
            attn_batch(units[i0:i0 + GB], 128, 384, bandB, "bandB", True)
            if bi % 3 == 1:
                maybe_combine()
        S.add("sp", lambda E, d=YB_scr[hq], s=yb: E.dma_start(out=d, in_=s), reads=["yb%d" % s2], dma="yb%d" % s2)
    while comb_next[0] < OWN // 128:
        maybe_combine()
    S.barrier()

    Wa, Wb_, Wo = wb16["wa"], wb16["wb"], wb16["wo"]
    for to in range(NT_OWN):
        tcol = to * T

        def ldy(_, tcol=tcol):
            S.add("sp", lambda E, d=yT[:, 0:8, :], s=YA_scr[:, :, tcol:tcol + T].rearrange("h p t -> p h t"): E.dma_start(out=d, in_=s),
                  writes=["xnb", "gfin"], dma="yT")
            S.add("sp", lambda E, d=yT[:, 8:24, :], s=YB_scr[:, :, tcol:tcol + T].rearrange("h p t -> p h t"): E.dma_start(out=d, in_=s),
                  writes=["xnb", "gfin"], dma="yT")
        P.stage(ldy)
        for dc0 in range(0, KC, 2):
            nd = min(2, KC - dc0)

            def ldb(slot, dc0=dc0, nd=nd):
                va = wst[slot][:, 0:8 * nd * 128].rearrange("p (k c) -> p k c", k=8)
                vbw = wst[slot][:, 8 * 256:8 * 256 + 16 * nd * 128].rearrange("p (k c) -> p k c", k=16)
                S.add("sp", lambda E, d=va, s=Wa[:, dc0 * 128:(dc0 + nd) * 128].rearrange("(k p) c -> p k c", p=128): E.dma_start(out=d, in_=s),
                      reads=[wpiece("wa", 0, dc0 * 128)], writes=["wst%d" % slot], dma="wst%d" % slot)
                S.add("sp", lambda E, d=vbw, s=Wb_[:, dc0 * 128:(dc0 + nd) * 128].rearrange("(k p) c -> p k c", p=128): E.dma_start(out=d, in_=s),
                      reads=[wpiece("wb", 0, dc0 * 128)], writes=["wst%d" % slot], dma="wst%d" % slot)

            def cb(slot, dc0=dc0, nd=nd, tcol=tcol):
                va = wst[slot][:, 0:8 * nd * 128].rearrange("p (k c) -> p k c", k=8)
                vbw = wst[slot][:, 8 * 256:8 * 256 + 16 * nd * 128].rearrange("p (k c) -> p k c", k=16)
                for ci in range(nd):
                    dc = dc0 + ci
                    gi = rr("sg")
                    gsrc = G_scr.rearrange("(a c) p t -> p a c t", a=2)[:, :, dc, tcol:tcol + T]
                    S.add("sp", lambda E, d=sgt[gi], s=gsrc: E.dma_start(out=d, in_=s), writes=["sg%d" % gi], dma="sg%d" % gi)
                    ba = alloc_bank()
                    for kc in range(8):
                        S.add("pe", lambda E, o=bk(ba), l=va[:, kc, ci * 128:(ci + 1) * 128], r=yT[:, kc, :], a=(kc == 0), z=(kc == 7):
                              E.matmul(o, l, r, start=a, stop=z), reads=["wst%d" % slot, "xnb", "gfin"], writes=[bn(ba)])
                    bb = alloc_bank()
                    for kc in range(16):
                        S.add("pe", lambda E, o=bk(bb), l=vbw[:, kc, ci * 128:(ci + 1) * 128], r=yT[:, 8 + kc, :], a=(kc == 0), z=(kc == 15):
                              E.matmul(o, l, r, start=a, stop=z), reads=["wst%d" % slot, "xnb", "gfin"], writes=[bn(bb)])
                    m1, m2 = rr("mt"), rr("sil")
                    S.add("dve", lambda E, o=mt[m1], a=bk(ba), b=sgt[gi][:, 0, :]: E.tensor_tensor(out=o, in0=a, in1=b, op=ALU.mult),
                          reads=[bn(ba), "sg%d" % gi], writes=["mt%d" % m1])
                    S.add("dve", lambda E, o=sil[m2], a=bk(bb), b=sgt[gi][:, 1, :]: E.tensor_tensor(out=o, in0=a, in1=b, op=ALU.mult),
                          reads=[bn(bb), "sg%d" % gi], writes=["sil%d" % m2])
                    S.add("pool", lambda E, o=xnT[:, dc, :], a=mt[m1], b=sil[m2]: E.tensor_tensor(out=o, in0=a, in1=b, op=ALU.add),
                          reads=["mt%d" % m1, "sil%d" % m2], writes=["xnT"])
            P.stage(cb, ldb)

        def ldh(_, tcol=tcol):
            S.add("sp", lambda E, d=hbuf, s=h_scr[tcol:tcol + T, :].rearrange("(b p) d -> p b d", p=128): E.dma_start(out=d, in_=s),
                  writes=HB_ALL, dma="hbuf")
        P.stage(ldh)
        KH = max(1, KC // 2)
        halves = [(k0, min(KH, KC - k0)) for k0 in range(0, KC, KH)]
        for c in range(0, D, 512):
            st = {}
            for hi, (k0, kn) in enumerate(halves):
                def ldo(slot, c=c, k0=k0, kn=kn):
                    v = wst[slot][:, 0:kn * 512].rearrange("p (k c) -> p k c", k=kn)
                    S.add("sp", lambda E, d=v, s=Wo[k0 * 128:(k0 + kn) * 128, c:c + 512].rearrange("(k p) c -> p k c", p=128): E.dma_start(out=d, in_=s),
                          reads=[wpiece("wo", 0, c)], writes=["wst%d" % slot], dma="wst%d" % slot)

                def co(slot, c=c, k0=k0, kn=kn, st=st, first=(hi == 0), last=(hi == len(halves) - 1)):
                    v = wst[slot][:, 0:kn * 512].rearrange("p (k c) -> p k c", k=kn)
                    if first:
                        st["b"] = [alloc_bank() for _ in range(4)]
                    for b in range(4):
                        bb = st["b"][b]
                        for k in range(kn):
                            S.add("pe", lambda E, o=bk(bb), l=xnT[:, k0 + k, b * 128:(b + 1) * 128], r=v[:, k, :], a=(k0 + k == 0), z=(k0 + k == KC - 1):
                                  E.matmul(o, l, r, start=a, stop=z), reads=["wst%d" % slot, "xnT"], writes=[bn(bb)])
                    if last:
                        for b in range(4):
                            bb = st["b"][b]
                            hs = hbuf[:, b, c:c + 512]
                            S.add("dve", lambda E, o=hs, i=bk(bb): E.tensor_tensor(out=o, in0=i, in1=o, op=ALU.add),
                                  reads=[bn(bb), "hbuf"], writes=["hbuf"])
                P.stage(co, ldo)
        norm_to_T(2, ["hbuf"], "n2")
        ffn("w2g", "w2u", "w2d", "f2")

        def fin(_, tcol=tcol):
            S.add("sp", lambda E, d=gfin, s=gfin_in.partition_broadcast(128): E.dma_start(out=d, in_=s), writes=["gfin"], dma="gfin")
            for b in range(4):
                ss = stat[:, 16 + b:17 + b]
                rs = stat[:, 24 + b:25 + b]
                S.add("dve", lambda E, o=ss: E.memset(o, 0.0), writes=["stat"])
                S.add("act", lambda E, o=xnb, i=hbuf[:, b, :], a=ss: E.activation(out=o, in_=i, func=AF.Square, accum_out=a),
                      reads=["hbuf"], writes=["xnb", "stat"])
                S.add("dve", lambda E, o=rs, i=ss: E.tensor_scalar(out=o, in0=i, scalar1=1.0 / D, scalar2=EPS, op0=ALU.mult, op1=ALU.add),
                      reads=["stat"], writes=["stat"])
                S.add("act", lambda E, o=rs: E.sqrt(out=o, in_=o), reads=["stat"], writes=["stat"])
                S.add("dve", lambda E, o=rs: E.reciprocal(out=o, in_=o), reads=["stat"], writes=["stat"])
                S.add("act", lambda E, o=hbuf[:, b, :], s=rs: E.mul(out=o, in_=o, mul=s),
                      reads=["hbuf", "stat"], writes=["hbuf"])
                S.add("dve", lambda E, o=hbuf[:, b, :], g=gfin: E.tensor_tensor(out=o, in0=o, in1=g, op=ALU.mult),
                      reads=["hbuf", "gfin"], writes=["hbuf"])
            S.add("sp", lambda E, d=y_out[tcol:tcol + T, :].rearrange("(b p) d -> p b d", p=128), s=hbuf: E.dma_start(out=d, in_=s),
                  reads=HB_ALL, dma="yst")
        P.stage(fin)
    P.flush()
    S.emit(nc, stack)
    stack.close()
    return nc


def _host_tables(seq_lens):
    tot = sum(seq_lens)
    seq_id = np.concatenate([np.full(n, i, np.int64) for i, n in enumerate(seq_lens)])
    pos = np.concatenate([np.arange(n, dtype=np.int64) for n in seq_lens])
    half = ROT // 2
    inv = (THETA ** (-np.arange(half, dtype=np.float32) / np.float32(half))).astype(np.float32)
    out = []
    for c in range(N_CORES):
        g0 = c * OWN - HALO
        idx = np.arange(g0, g0 + EXT)
        valid = (idx >= 0) & (idx < tot)
        ci = np.clip(idx, 0, tot - 1)
        sid = np.where(valid, seq_id[ci], 3)
        p = np.where(valid, pos[ci], 0).astype(np.float32)
        ang = p[None, :] * inv[:, None]
        cs, sn = np.cos(ang).astype(np.float32), np.sin(ang).astype(np.float32)
        ropeC = np.concatenate([cs, cs], 0)
        ropeS = np.concatenate([-sn, sn], 0)
        onehot = (sid[None, :] == np.arange(3)[:, None])
        seqA = onehot.astype(np.float32).astype(ml_dtypes.bfloat16)
        seqB = (np.float32(NEGBIG) * (1.0 - onehot.astype(np.float32))).astype(ml_dtypes.bfloat16)
        out.append(dict(ropeC=np.ascontiguousarray(ropeC), ropeS=np.ascontiguousarray(ropeS), seqA=seqA, seqB=seqB, idx=idx, valid=valid))
    return out


def _consts():
    i = np.arange(64)[:, None]
    k = np.arange(192)[None, :]
    bandA = np.where(np.abs(k - 64 - i) <= 64, 0.0, NEGBIG).astype(np.float32)
    i = np.arange(128)[:, None]
    k = np.arange(384)[None, :]
    bandB = np.where(np.abs(k - 128 - i) <= 128, 0.0, NEGBIG).astype(np.float32)
    ident = np.eye(128, dtype=np.float32).astype(ml_dtypes.bfloat16)
    prot = np.zeros((ROT, ROT), np.float32)
    for m in range(ROT):
        prot[(m + 16) % ROT, m] = 1.0
    return bandA, bandB, ident, prot.astype(ml_dtypes.bfloat16)


_NC_CACHE = {}


def run(cfg, x_prompt, x_sample, g_ffn1, w1_gate, w1_up, w1_down, g_mix, w_in, sink_b, w_branch_a, w_branch_b,
        w_out, g_ffn2, w2_gate, w2_up, w2_down, g_final, trace=False):
    D = cfg.D
    f = lambda a: np.ascontiguousarray(np.asarray(a, dtype=np.float32))
    xp, xs = f(x_prompt), f(x_sample)
    seq_lens = [xp.shape[1]] * xp.shape[0] + [xs.shape[1]] * xs.shape[0]
    X = np.concatenate([xp.reshape(-1, D), xs.reshape(-1, D)], 0)
    tot = X.shape[0]
    assert tot == N_CORES * OWN
    tabs = _host_tables(seq_lens)
    bandA, bandB, ident, prot = _consts()
    KC = cfg.KC
    gT = np.concatenate([f(g)[0].reshape(KC, 128).T for g in (g_ffn1, g_mix, g_ffn2)], 1)
    shared = dict(
        w1g=f(w1_gate)[0], w1u=f(w1_up)[0], w1d=f(w1_down)[0], win=f(w_in)[0], wa=f(w_branch_a)[0], wb=f(w_branch_b)[0],
        wo=f(w_out)[0], w2g=f(w2_gate)[0], w2u=f(w2_up)[0], w2d=f(w2_down)[0],
        gT=np.ascontiguousarray(gT), gfin=f(g_final), sink=f(sink_b)[0],
        bandA=bandA, bandB=bandB, ident=ident, prot=prot,
    )
    in_maps = []
    for c in range(N_CORES):
        t = tabs[c]
        xe = np.zeros((EXT, D), np.float32)
        xe[t["valid"]] = X[t["idx"][t["valid"]]]
        m = dict(shared)
        m.update(xe=xe, ropeC=t["ropeC"], ropeS=t["ropeS"], seqA=t["seqA"], seqB=t["seqB"])
        in_maps.append(m)
    key = (cfg.D, cfg.F, cfg.HP)
    if key not in _NC_CACHE:
        _NC_CACHE[key] = build(cfg)
    nc = _NC_CACHE[key]
    res = run_bass_kernel_spmd(nc, in_maps, core_ids=list(range(N_CORES)), trace=trace)
    Y = np.concatenate([np.asarray(r["y"]) for r in res.results], 0).astype(np.float32)
    n0 = xp.shape[0] * xp.shape[1]
    return (Y[:n0].reshape(xp.shape), Y[n0:].reshape(xs.shape)), res


def kernel(**inputs):
    out, _ = run(Cfg(), **inputs)
    return out
```

```python
import math
from contextlib import ExitStack

import numpy as np
import ml_dtypes

import concourse.bass as bass
import concourse.mybir as mybir
from concourse.bass_utils import run_bass_kernel_spmd

F32 = mybir.dt.float32
BF16 = mybir.dt.bfloat16
AF = mybir.ActivationFunctionType
ALU = mybir.AluOpType
AX = mybir.AxisListType

N_CORES = 8
OWN = 3072
HALO = 1024
EXT = OWN + 2 * HALO
T = 512
NT_EXT = EXT // T
OWN_T0 = HALO // T
NT_OWN = OWN // T
HD = 128
ROT = 32
THETA = 500000.0
EPS = 1e-6
NEGBIG = -30000.0
DIL = (1, 4, 16)
NQ = 40
NK = 28
VC = 3584
OW = 129


class Cfg:
    def __init__(self, D=4096, F=11008, HP=4):
        self.D, self.F, self.HP = D, F, HP
        self.KC = D // 128
        self.FC = F // 128
        self.IN_W = 12288 + 2 * D


class _Op:
    __slots__ = ("eng", "fn", "dma", "cum", "deps", "signal", "sigval")


class Sched:
    ENGS = ("sp", "act", "dve", "pool", "pe")

    def __init__(self):
        self.ops = []
        self.last_w = {}
        self.readers = {}
        self.dma_cum = {}
        self.last_on = {}

    def add(self, eng, fn, reads=(), writes=(), dma=None, extra=()):
        op = _Op()
        op.eng, op.fn, op.dma, op.signal, op.sigval = eng, fn, dma, False, 0
        idx = len(self.ops)
        deps = set(extra)
        for r in reads:
            w = self.last_w.get(r)
            if w is not None:
                deps.add(w)
        for r in writes:
            w = self.last_w.get(r)
            if w is not None:
                deps.add(w)
            rd = self.readers.get(r)
            if rd:
                deps.update(rd.values())
        if dma is not None:
            self.dma_cum[dma] = self.dma_cum.get(dma, 0) + 16
            op.cum = self.dma_cum[dma]
        else:
            op.cum = 0
        rkey = ("dma", dma) if dma is not None else eng
        for r in reads:
            self.readers.setdefault(r, {})[rkey] = idx
        for r in writes:
            self.last_w[r] = idx
            self.readers[r] = {}
        fd = []
        for j in deps:
            o = self.ops[j]
            if o.dma is not None:
                fd.append(j)
            elif o.eng == eng and dma is None and eng == "pe":
                continue
            else:
                o.signal = True
                fd.append(j)
        op.deps = fd
        self.ops.append(op)
        if fn is not None:
            self.last_on[eng] = idx
        return idx

    def barrier(self):
        lasts = dict(self.last_on)
        dma_last = {}
        for i, o in enumerate(self.ops):
            if o.dma is not None:
                dma_last[o.dma] = i
        ex = list(lasts.values()) + list(dma_last.values())
        for e in self.ENGS:
            self.add(e, None, extra=[j for j in ex if j != lasts.get(e) or self.ops[j].dma is not None])
        self.last_on = {}

    def emit(self, nc, stack):
        cnt = {e: 0 for e in self.ENGS}
        for o in self.ops:
            if o.signal and o.dma is None:
                cnt[o.eng] += 1
                o.sigval = cnt[o.eng]
        esem = {e: stack.enter_context(nc.semaphore("s_" + e)) for e in self.ENGS}
        dsem = {}
        for k in self.dma_cum:
            dsem[k] = stack.enter_context(nc.semaphore("d%d" % len(dsem)))
        streams = {e: [] for e in self.ENGS}
        for o in self.ops:
            streams[o.eng].append(o)
        ops = self.ops
        final = [(dsem[k], v) for k, v in self.dma_cum.items()]

        def run(eng_name, E):
            waited = {}
            for o in streams[eng_name]:
                for j in o.deps:
                    d = ops[j]
                    if d.dma is not None:
                        sem, val = dsem[d.dma], d.cum
                    else:
                        sem, val = esem[d.eng], d.sigval
                    key = id(sem)
                    if waited.get(key, 0) < val:
                        E.wait_ge(sem, val)
                        waited[key] = val
                if o.fn is None:
                    continue
                ins = o.fn(E)
                if o.dma is not None:
                    ins.then_inc(dsem[o.dma], 16)
                elif o.signal:
                    ins.then_inc(esem[eng_name], 1)
            if eng_name == "sp":
                for sem, val in final:
                    E.wait_ge(sem, val)

        block = stack.enter_context(nc.Block())

        @block.sync
        def _(e):
            run("sp", e)

        @block.scalar
        def _(e):
            run("act", e)

        @block.vector
        def _(e):
            run("dve", e)

        @block.gpsimd
        def _(e):
            run("pool", e)

        @block.tensor
        def _(e):
            run("pe", e)


class Pipe:
    def __init__(self, nslots=3, pd=2):
        self.n = 0
        self.ns = nslots
        self.pd = pd
        self.pending = []
        self.tick = None

    def stage(self, compute, load=None):
        slot = None
        if load is not None:
            slot = self.n % self.ns
            self.n += 1
            load(slot)
        self.pending.append((compute, slot))
        while len(self.pending) > self.pd:
            c, s = self.pending.pop(0)
            c(s)
            if self.tick is not None:
                self.tick()

    def flush(self):
        while self.pending:
            c, s = self.pending.pop(0)
            c(s)


def build(cfg, piece_order=None, use_order=None):
    if use_order is None:
        use_order = []
    D, F, KC, FC, HP, IN_W = cfg.D, cfg.F, cfg.KC, cfg.FC, cfg.HP, cfg.IN_W
    nc = bass.Bass("TRN2", target_bir_lowering=False)
    S = Sched()
    P = Pipe()

    def din(name, shape, dt=F32):
        return nc.dram_tensor(name, list(shape), dt, kind="ExternalInput").ap()

    def dscr(name, shape, dt):
        return nc.dram_tensor(name, list(shape), dt, kind="Internal").ap()

    xe = din("xe", [EXT, D])
    wf = {
        "w1g": din("w1g", [D, F]), "w1u": din("w1u", [D, F]), "w1d": din("w1d", [F, D]),
        "win": din("win", [D, IN_W]), "wa": din("wa", [1024, D]), "wb": din("wb", [2048, D]),
        "wo": din("wo", [D, D]),
        "w2g": din("w2g", [D, F]), "w2u": din("w2u", [D, F]), "w2d": din("w2d", [F, D]),
    }
    gT_in = din("gT", [128, 3 * KC])
    gfin_in = din("gfin", [D])
    sink_in = din("sink", [16])
    ropeC_in = din("ropeC", [ROT, EXT])
    ropeS_in = din("ropeS", [ROT, EXT])
    seqA_in = din("seqA", [3, EXT], BF16)
    seqB_in = din("seqB", [3, EXT], BF16)
    bandA_in = din("bandA", [64, 192])
    bandB_in = din("bandB", [128, 384])
    ident_in = din("ident", [128, 128], BF16)
    prot_in = din("prot", [ROT, ROT], BF16)
    y_out = nc.dram_tensor("y", [OWN, D], F32, kind="ExternalOutput").ap()

    wb16 = {k: dscr(k + "_b", list(v.shape), BF16) for k, v in wf.items()}
    K_scr = dscr("K_scr", [NK, 128, EXT], BF16)
    Q_scr = dscr("Q_scr", [NQ, 128, OWN], BF16)
    V_scr = dscr("V_scr", [EXT, VC], BF16)
    G_scr = dscr("G_scr", [2 * KC, 128, OWN], BF16)
    h_scr = dscr("h_scr", [OWN, D], F32)
    O_scr = dscr("O_scr", [OWN, 24, OW], F32)
    YA_scr = dscr("YA_scr", [8, 128, OWN], BF16)
    YB_scr = dscr("YB_scr", [16, 128, OWN], BF16)

    HPC = -(-FC // HP)
    parts = []
    c = 0
    for i in range(HP):
        n = FC // HP + (1 if i < FC % HP else 0)
        if n:
            parts.append((c, n))
        c += n
    WSLOT = max(KC * 512, 12288, 8192)
    sizes = {}
    off = {}
    cur = 0

    def region(name, nbytes):
        nonlocal cur
        off[name] = cur
        sizes[name] = nbytes
        cur += (nbytes + 63) // 64 * 64

    region("gT", 3 * KC * 4)
    region("ident", 256)
    region("prot", 64)
    region("sink", 64)
    region("stat", 256)
    region("sil", 2 * 2048)
    region("mt", 2 * 2048)
    SMALL_END = cur
    region("hbuf", max(4 * D * 4, 28672))
    region("xnT", KC * 512 * 2)
    region("hid", max(HPC * 1024, 1024))
    region("wst", 3 * WSLOT)
    region("aux", max(24576, 6 * D))
    ARENA = cur
    ARENA = max(ARENA, 190 * 1024)
    stack = ExitStack()
    arena = stack.enter_context(nc.sbuf_tensor("arena", [128, ARENA // 4], F32))
    banks = [stack.enter_context(nc.psum_tensor("ps%d" % i, [128, 512], F32)) for i in range(8)]

    def carve(o, nbytes, dt):
        v = arena[:, o // 4:(o + nbytes) // 4]
        return v.bitcast(dt) if dt != F32 else v

    def reg(name, dt, o=0, nbytes=None):
        return carve(off[name] + o, sizes[name] - o if nbytes is None else nbytes, dt)

    hbuf = reg("hbuf", F32, 0, 4 * D * 4).rearrange("p (b d) -> p b d", b=4)
    xnT = reg("xnT", BF16).rearrange("p (k t) -> p k t", t=T)
    hid = reg("hid", BF16).rearrange("p (k t) -> p k t", t=T)
    wst = [reg("wst", BF16, s * WSLOT, WSLOT) for s in range(3)]
    xnb = reg("aux", BF16, 0, D * 2)
    gfin = reg("aux", F32, D * 2, D * 4)
    yT = reg("aux", BF16, 0, 24576).rearrange("p (k t) -> p k t", t=T)
    gT = reg("gT", F32)
    ident = reg("ident", BF16)
    prot = reg("prot", BF16)[0:ROT, :]
    sink = reg("sink", F32)
    stat = reg("stat", F32)
    sil = [reg("sil", F32, i * 2048, 2048) for i in range(2)]
    mt = [reg("mt", F32, i * 2048, 2048) for i in range(2)]
    HB = off["hbuf"]
    xb = [carve(HB + i * 2048, 2048, BF16).rearrange("p (c t) -> p c t", t=T) for i in range(2)]
    sgt = [carve(HB + 4096 + i * 2048, 2048, BF16).rearrange("p (c t) -> p c t", t=T) for i in range(2)]
    vb = [carve(HB + 8192 + i * 4096, 4096, BF16).rearrange("p (b c) -> p b c", b=4) for i in range(2)]
    rC = carve(HB + 16384, 2048, F32)[0:ROT, :]
    rS = carve(HB + 18432, 2048, F32)[0:ROT, :]
    rt1 = [carve(HB + 20480 + i * 2048, 2048, F32)[0:ROT, :] for i in range(2)]
    rt2 = [carve(HB + 24576 + i * 2048, 2048, F32)[0:ROT, :] for i in range(2)]
    HBT = ["xb0", "xb1", "sg0", "sg1", "vb0", "vb1", "rC", "rS", "rt10", "rt11", "rt20", "rt21"]
    HB_ALL = ["hbuf"] + HBT

    bank_ctr = [0]

    def alloc_bank():
        b = bank_ctr[0] % 8
        bank_ctr[0] += 1
        return b

    def bk(b):
        return banks[b][:]

    def bkbf(b):
        return banks[b][:].bitcast(BF16)

    def bn(b):
        return "ps%d" % b

    ctr = {"sil": 0, "mt": 0, "xb": 0, "sg": 0, "vb": 0, "rt": 0}

    def rr(name, n=2):
        v = ctr[name] % n
        ctr[name] += 1
        return v

    def ld_const(dst, src, name):
        S.add("sp", lambda E, d=dst, s=src: E.dma_start(out=d, in_=s), writes=[name], dma="c_" + name)

    ld_const(gT, gT_in, "gT")
    ld_const(ident, ident_in, "ident")
    ld_const(prot, prot_in, "prot")
    ld_const(sink[:, 0:16], sink_in.partition_broadcast(128), "sink")

    def wpiece(k, row, col):
        return "W_" + k

    def cast_weights(names, extra=()):
        for k in names:
            src, dst = wf[k], wb16[k]
            rows, cols = src.shape
            step = max(1, (4 * 1024 * 1024) // cols)
            r0 = 0
            while r0 < rows:
                r1 = min(rows, r0 + step)
                S.add("pool", lambda E, d=dst[r0:r1, :], s=src[r0:r1, :]: E.dma_start(out=d, in_=s),
                      writes=["W_" + k], dma="cast_" + k, extra=list(extra))
                r0 = r1

    cast_weights(("w1g", "w1u", "w1d", "win"))
    late_q = []
    tickc = [0]

    def late_emit():
        k, r0, r1 = late_q.pop(0)
        S.add("pool", lambda E, d=wb16[k][r0:r1, :], s=wf[k][r0:r1, :]: E.dma_start(out=d, in_=s),
              writes=["W_" + k], dma="cast_" + k, extra=[S.last_on["pe"]])

    def late_tick():
        tickc[0] += 1
        if late_q and tickc[0] % 5 == 0:
            late_emit()

    def norm_to_T(gcol, src_names, tag):
        def comp(_):
            for b in range(4):
                ss = stat[:, b:b + 1]
                rs = stat[:, 8 + b:9 + b]
                S.add("dve", lambda E, o=ss: E.memset(o, 0.0), writes=["stat"])
                S.add("act", lambda E, o=xnb, i=hbuf[:, b, :], a=ss: E.activation(out=o, in_=i, func=AF.Square, accum_out=a),
                      reads=src_names, writes=["xnb", "stat"])
                S.add("dve", lambda E, o=rs, i=ss: E.tensor_scalar(out=o, in0=i, scalar1=1.0 / D, scalar2=EPS, op0=ALU.mult, op1=ALU.add),
                      reads=["stat"], writes=["stat"])
                S.add("act", lambda E, o=rs: E.sqrt(out=o, in_=o), reads=["stat"], writes=["stat"])
                S.add("dve", lambda E, o=rs: E.reciprocal(out=o, in_=o), reads=["stat"], writes=["stat"])
                S.add("act", lambda E, o=xnb, i=hbuf[:, b, :], s=rs: E.mul(out=o, in_=i, mul=s),
                      reads=src_names + ["stat"], writes=["xnb"])
                gsz = min(8, KC)
                for k0 in range(0, KC, gsz):
                    bb = alloc_bank()
                    pv = bkbf(bb)
                    for j in range(gsz):
                        S.add("pe", lambda E, o=pv[:, j * 128:(j + 1) * 128], i=xnb[:, (k0 + j) * 128:(k0 + j + 1) * 128]:
                              E.transpose(out=o, in_=i, identity=ident), reads=["xnb", "ident"], writes=[bn(bb)])
                    gsl = gT[:, gcol * KC + k0: gcol * KC + k0 + gsz].unsqueeze(2).to_broadcast([128, gsz, 128])
                    S.add("dve", lambda E, o=xnT[:, k0:k0 + gsz, b * 128:(b + 1) * 128],
                          i=pv[:, 0:gsz * 128].rearrange("p (k t) -> p k t", t=128), g=gsl:
                          E.tensor_tensor(out=o, in0=i, in1=g, op=ALU.mult),
                          reads=[bn(bb), "gT"], writes=["xnT"])
        P.stage(comp)

    def ffn(wg, wu, wd, wtag):
        Wg, Wu, Wd = wb16[wg], wb16[wu], wb16[wd]
        for (c0, nch) in parts:
            for cc in range(c0, c0 + nch, 2):
                n2 = min(2, c0 + nch - cc)
                st = {}

                def ldg(slot, cc=cc, n2=n2):
                    v = wst[slot][:, 0:KC * n2 * 128].rearrange("p (k c) -> p k c", k=KC)
                    S.add("sp", lambda E, d=v, s=Wg[:, cc * 128:(cc + n2) * 128].rearrange("(k p) c -> p k c", p=128):
                          E.dma_start(out=d, in_=s), reads=[wpiece(wg, 0, cc * 128)], writes=["wst%d" % slot], dma="wst%d" % slot)

                def ldu(slot, cc=cc, n2=n2):
                    v = wst[slot][:, 0:KC * n2 * 128].rearrange("p (k c) -> p k c", k=KC)
                    S.add("sp", lambda E, d=v, s=Wu[:, cc * 128:(cc + n2) * 128].rearrange("(k p) c -> p k c", p=128):
                          E.dma_start(out=d, in_=s), reads=[wpiece(wu, 0, cc * 128)], writes=["wst%d" % slot], dma="wst%d" % slot)

                def cg(slot, n2=n2, st=st):
                    v = wst[slot][:, 0:KC * n2 * 128].rearrange("p (k c) -> p k c", k=KC)
                    st["g"] = []
                    for ci in range(n2):
                        bb = alloc_bank()
                        st["g"].append(bb)
                        for kc in range(KC):
                            S.add("pe", lambda E, o=bk(bb), l=v[:, kc, ci * 128:(ci + 1) * 128], r=xnT[:, kc, :], a=(kc == 0), z=(kc == KC - 1):
                                  E.matmul(o, l, r, start=a, stop=z), reads=["wst%d" % slot, "xnT"], writes=[bn(bb)])

                def cu(slot, n2=n2, st=st, cc=cc, c0=c0):
                    v = wst[slot][:, 0:KC * n2 * 128].rearrange("p (k c) -> p k c", k=KC)
                    for ci in range(n2):
                        bb = alloc_bank()
                        for kc in range(KC):
                            S.add("pe", lambda E, o=bk(bb), l=v[:, kc, ci * 128:(ci + 1) * 128], r=xnT[:, kc, :], a=(kc == 0), z=(kc == KC - 1):
                                  E.matmul(o, l, r, start=a, stop=z), reads=["wst%d" % slot, "xnT"], writes=[bn(bb)])
                        gb = st["g"][ci]
                        si = rr("sil")
                        S.add("act", lambda E, o=sil[si], i=bk(gb): E.activation(out=o, in_=i, func=AF.Silu),
                              reads=[bn(gb)], writes=["sil%d" % si])
                        S.add("dve", lambda E, o=hid[:, cc - c0 + ci, :], a=sil[si], b=bk(bb): E.tensor_tensor(out=o, in0=a, in1=b, op=ALU.mult),
                              reads=["sil%d" % si, bn(bb)], writes=["hid"])

                P.stage(cg, ldg)
                P.stage(cu, ldu)
            CH = max(1, min(nch, WSLOT // 1024))
            subs = [(s0, min(CH, nch - s0)) for s0 in range(0, nch, CH)]
            for cgi in range(max(1, D // 512)):
                ncol = min(512, D)
                st = {}
                for si_, (s0, sn) in enumerate(subs):
                    def ldd(slot, s0=s0, sn=sn, cgi=cgi, c0=c0, ncol=ncol):
                        v = wst[slot][:, 0:sn * ncol].rearrange("p (k c) -> p k c", k=sn)
                        src = Wd[(c0 + s0) * 128:(c0 + s0 + sn) * 128, cgi * ncol:(cgi + 1) * ncol].rearrange("(k p) c -> p k c", p=128)
                        S.add("sp", lambda E, d=v, s=src: E.dma_start(out=d, in_=s), reads=[wpiece(wd, (c0 + s0) * 128, 0)],
                              writes=["wst%d" % slot], dma="wst%d" % slot)

                    def cd(slot, s0=s0, sn=sn, cgi=cgi, nch=nch, st=st, first=(si_ == 0), last=(si_ == len(subs) - 1), ncol=ncol):
                        v = wst[slot][:, 0:sn * ncol].rearrange("p (k c) -> p k c", k=sn)
                        if first:
                            st["b"] = [alloc_bank() for _ in range(4)]
                        for b in range(4):
                            bb = st["b"][b]
                            for k in range(sn):
                                S.add("pe", lambda E, o=bk(bb)[:, 0:ncol], l=hid[:, s0 + k, b * 128:(b + 1) * 128], r=v[:, k, :],
                                      a=(s0 + k == 0), z=(s0 + k == nch - 1): E.matmul(o, l, r, start=a, stop=z),
                                      reads=["wst%d" % slot, "hid"], writes=[bn(bb)])
                        if last:
                            for b in range(4):
                                bb = st["b"][b]
                                hs = hbuf[:, b, cgi * ncol:(cgi + 1) * ncol]
                                S.add("dve", lambda E, o=hs, i=bk(bb)[:, 0:ncol]: E.scalar_tensor_tensor(out=o, in0=i, scalar=0.5, in1=o, op0=ALU.mult, op1=ALU.add),
                                      reads=[bn(bb), "hbuf"], writes=["hbuf"])
                    P.stage(cd, ldd)

    def proj_fm(col0, ncols, kind, tcol, scr, scr0, t0e):
        Wi = wb16["win"]
        for c in range(col0, col0 + ncols, 256):
            def ld(slot, c=c):
                v = wst[slot][:, 0:KC * 256].rearrange("p (k c) -> p k c", k=KC)
                S.add("sp", lambda E, d=v, s=Wi[:, c:c + 256].rearrange("(k p) c -> p k c", p=128): E.dma_start(out=d, in_=s),
                      reads=[wpiece("win", 0, c)], writes=["wst%d" % slot], dma="wst%d" % slot)

            def cp(slot, c=c):
                v = wst[slot][:, 0:KC * 256].rearrange("p (k c) -> p k c", k=KC)
                if kind == "g":
                    oi = rr("sg")
                    obuf, oname = sgt[oi], "sg%d" % oi
                else:
                    oi = rr("xb")
                    obuf, oname = xb[oi], "xb%d" % oi
                for ci in range(2):
                    bb = alloc_bank()
                    for kc in range(KC):
                        S.add("pe", lambda E, o=bk(bb), l=v[:, kc, ci * 128:(ci + 1) * 128], r=xnT[:, kc, :], a=(kc == 0), z=(kc == KC - 1):
                              E.matmul(o, l, r, start=a, stop=z), reads=["wst%d" % slot, "xnT"], writes=[bn(bb)])
                    if kind == "g":
                        S.add("act", lambda E, o=obuf[:, ci, :], i=bk(bb): E.activation(out=o, in_=i, func=AF.Sigmoid),
                              reads=[bn(bb)], writes=[oname])
                    else:
                        S.add("act", lambda E, o=obuf[:, ci, :], i=bk(bb): E.copy(out=o, in_=i),
                              reads=[bn(bb)], writes=[oname])
                        b2 = alloc_bank()
                        ri = rr("rt")
                        S.add("pe", lambda E, o=bk(b2)[0:ROT, :], l=prot, r=obuf[0:ROT, ci, :]: E.matmul(o, l, r, start=True, stop=True),
                              reads=[oname, "prot"], writes=[bn(b2)])
                        S.add("dve", lambda E, o=rt1[ri], a=obuf[0:ROT, ci, :], b=rC: E.tensor_tensor(out=o, in0=a, in1=b, op=ALU.mult),
                              reads=[oname, "rC"], writes=["rt1%d" % ri])
                        S.add("dve", lambda E, o=rt2[ri], a=bk(b2)[0:ROT, :], b=rS: E.tensor_tensor(out=o, in0=a, in1=b, op=ALU.mult),
                              reads=[bn(b2), "rS"], writes=["rt2%d" % ri])
                        S.add("dve", lambda E, o=obuf[0:ROT, ci, :], a=rt1[ri], b=rt2[ri]: E.tensor_tensor(out=o, in0=a, in1=b, op=ALU.add),
                              reads=["rt1%d" % ri, "rt2%d" % ri], writes=[oname])
                h0 = scr0 + (c - col0) // 128
                dst = scr[h0:h0 + 2, :, tcol:tcol + T].rearrange("c p t -> p c t")
                S.add("sp", lambda E, d=dst, s=obuf: E.dma_start(out=d, in_=s), reads=[oname], dma=oname)
            P.stage(cp, ld)

    def proj_v(col0, ncols, vcol0, t0e):
        Wi = wb16["win"]
        KH = max(1, KC // 2)
        halves = [(k0, min(KH, KC - k0)) for k0 in range(0, KC, KH)]
        for c in range(col0, col0 + ncols, 512):
            st = {}
            for hi, (k0, kn) in enumerate(halves):
                def ld(slot, c=c, k0=k0, kn=kn):
                    v = wst[slot][:, 0:kn * 512].rearrange("p (k c) -> p k c", k=kn)
                    S.add("sp", lambda E, d=v, s=Wi[k0 * 128:(k0 + kn) * 128, c:c + 512].rearrange("(k p) c -> p k c", p=128): E.dma_start(out=d, in_=s),
                          reads=[wpiece("win", 0, c)], writes=["wst%d" % slot], dma="wst%d" % slot)

                def cp(slot, c=c, k0=k0, kn=kn, st=st, first=(hi == 0), last=(hi == len(halves) - 1)):
                    v = wst[slot][:, 0:kn * 512].rearrange("p (k c) -> p k c", k=kn)
                    if first:
                        st["b"] = [alloc_bank() for _ in range(4)]
                        st["o"] = rr("vb")
                    oi = st["o"]
                    for b in range(4):
                        bb = st["b"][b]
                        for k in range(kn):
                            S.add("pe", lambda E, o=bk(bb), l=xnT[:, k0 + k, b * 128:(b + 1) * 128], r=v[:, k, :], a=(k0 + k == 0), z=(k0 + k == KC - 1):
                                  E.matmul(o, l, r, start=a, stop=z), reads=["wst%d" % slot, "xnT"], writes=[bn(bb)])
                    if last:
                        for b in range(4):
                            bb = st["b"][b]
                            S.add("act", lambda E, o=vb[oi][:, b, :], i=bk(bb): E.copy(out=o, in_=i), reads=[bn(bb)], writes=["vb%d" % oi])
                        vc = vcol0 + (c - col0)
                        dst = V_scr[t0e:t0e + T, vc:vc + 512].rearrange("(b p) c -> p b c", p=128)
                        S.add("sp", lambda E, d=dst, s=vb[oi]: E.dma_start(out=d, in_=s), reads=["vb%d" % oi], dma="vb%d" % oi)
                P.stage(cp, ld)

    for ti in range(NT_EXT):
        t0e = ti * T
        own = OWN_T0 <= ti < OWN_T0 + NT_OWN
        tcol = (ti - OWN_T0) * T

        def ldx(_, t0e=t0e):
            S.add("sp", lambda E, d=hbuf, s=xe[t0e:t0e + T, :].rearrange("(b p) d -> p b d", p=128): E.dma_start(out=d, in_=s),
                  writes=HB_ALL, dma="hbuf")
        P.stage(ldx)
        if ti == 3:
            def cast_late(_):
                for k in ("wa", "wb", "wo", "w2g", "w2u", "w2d"):
                    rows, cols = wf[k].shape
                    step = max(1, (1024 * 1024) // cols)
                    for r0 in range(0, rows, step):
                        late_q.append((k, r0, min(rows, r0 + step)))
                P.tick = late_tick
            P.stage(cast_late)
        norm_to_T(0, ["hbuf"], "n1")
        ffn("w1g", "w1u", "w1d", "f1")
        if own:
            def sth(_, tcol=tcol):
                S.add("sp", lambda E, d=h_scr[tcol:tcol + T, :].rearrange("(b p) d -> p b d", p=128), s=hbuf: E.dma_start(out=d, in_=s),
                      reads=HB_ALL, dma="hst")
            P.stage(sth)
        norm_to_T(1, HB_ALL, "nm")

        def ldrope(_, t0e=t0e):
            S.add("sp", lambda E, d=rC, s=ropeC_in[:, t0e:t0e + T]: E.dma_start(out=d, in_=s), writes=["rC"], dma="rC")
            S.add("sp", lambda E, d=rS, s=ropeS_in[:, t0e:t0e + T]: E.dma_start(out=d, in_=s), writes=["rS"], dma="rS")
        P.stage(ldrope)
        far = ti == 0 or ti == NT_EXT - 1
        if far:
            proj_fm(3072 + 2048, 1024, "k", t0e, K_scr, 16, t0e)
            proj_v(6144 + 2048, 1024, 2048, t0e)
        else:
            proj_fm(3072, 3072, "k", t0e, K_scr, 0, t0e)
            proj_fm(11264, 512, "k", t0e, K_scr, 24, t0e)
            proj_v(6144, 3072, 0, t0e)
            proj_v(11776, 512, 3072, t0e)
        if own:
            proj_fm(0, 3072, "q", tcol, Q_scr, 0, t0e)
            proj_fm(9216, 2048, "q", tcol, Q_scr, 24, t0e)
            proj_fm(12288, 2 * D, "g", tcol, G_scr, 0, t0e)
    P.flush()
    P.tick = None
    while late_q:
        late_emit()
    S.barrier()

    acur = [SMALL_END]

    def acarve(nbytes, dt):
        o = acur[0]
        acur[0] += (nbytes + 63) // 64 * 64
        assert acur[0] <= ARENA
        return carve(o, nbytes, dt)

    NSET = 8
    kTb = [acarve(EXT * 2, BF16) for _ in range(2)]
    qTb = [acarve(OWN * 2, BF16) for _ in range(2)]
    vsb = [acarve(80 * 128 * 2, BF16) for _ in range(2)]
    sA = acarve(EXT * 2, BF16)[0:3, :]
    sB = acarve(EXT * 2, BF16)[0:3, :]
    bandA = acarve(192 * 4, F32)[0:64, :]
    bandB = acarve(384 * 4, F32)
    Smb = [acarve(384 * 4, F32) for _ in range(NSET)]
    Pb = [acarve(384 * 2, BF16) for _ in range(NSET)]
    PTb = [acarve(384 * 2, BF16) for _ in range(NSET)]
    ast = acarve(1024, F32)
    nsink = acarve(64, F32)
    amark = acur[0]
    obufs = [acarve(48 * OW * 4, F32) for _ in range(2)]

    S.add("sp", lambda E: E.dma_start(out=sA, in_=seqA_in), writes=["sA"], dma="sA")
    S.add("sp", lambda E: E.dma_start(out=sB, in_=seqB_in), writes=["sB"], dma="sB")
    S.add("sp", lambda E: E.dma_start(out=bandA, in_=bandA_in), writes=["bandA"], dma="bandA")
    S.add("sp", lambda E: E.dma_start(out=bandB, in_=bandB_in), writes=["bandB"], dma="bandB")
    S.add("dve", lambda E: E.tensor_scalar(out=nsink[:, 0:16], in0=sink[:, 0:16], scalar1=-1.0, scalar2=None, op0=ALU.mult),
          reads=["sink"], writes=["nsink"])
    scale = 1.0 / math.sqrt(HD)
    uctr = [0]

    def attn_batch(units, QN, NKEY, band, bname, is_b):
        NB = 3
        KB = NKEY // NB
        R = []
        for u in units:
            k = uctr[0] % NSET
            sc = (uctr[0] % 16) * 8
            uctr[0] += 1
            b1 = alloc_bank()
            b2 = alloc_bank() if is_b else b1
            R.append((k, sc, b1, b2))

        def regs(b1, b2):
            if is_b:
                return bk(b1)[0:QN, 0:NKEY], bk(b1)[0:128, NKEY:NKEY + 128], bkbf(b2)[0:KB, 0:NB * QN]
            return bk(b1)[0:QN, 0:NKEY], bk(b1)[0:QN, NKEY:NKEY + 128], bkbf(b1)[0:KB, 2 * (NKEY + 128):2 * (NKEY + 128) + NB * QN]

        for u, (k, sc, b1, b2) in zip(units, R):
            sps, ops_, tps = regs(b1, b2)
            S.add("pe", lambda E, o=sps, l=u["qsl"], r=u["ksl"]: E.matmul(o, l, r, start=True, stop=False),
                  reads=[u["kname"], u["qname"]], writes=[bn(b1)])
            S.add("pe", lambda E, o=sps, l=u["qtok"], r=u["ktok"]: E.matmul(o, l, r, start=False, stop=True),
                  reads=["sA", "sB"], writes=[bn(b1)])
        for u, (k, sc, b1, b2) in zip(units, R):
            sps, ops_, tps = regs(b1, b2)
            S.add("dve", lambda E, o=Smb[k][0:QN, 0:NKEY], i=sps, b=band: E.scalar_tensor_tensor(out=o, in0=i, scalar=scale, in1=b, op0=ALU.mult, op1=ALU.add),
                  reads=[bn(b1), bname], writes=["Sm%d" % k])
        for u, (k, sc, b1, b2) in zip(units, R):
            negm = ast[0:QN, sc:sc + 1]
            S.add("dve", lambda E, o=negm, i=Smb[k][0:QN, 0:NKEY]: E.reduce_max(out=o, in_=i, axis=AX.X, negate=True),
                  reads=["Sm%d" % k], writes=["ast%d" % sc])
            if is_b:
                S.add("dve", lambda E, o=negm, i=nsink[0:QN, u["hq"]:u["hq"] + 1]: E.tensor_tensor(out=o, in0=o, in1=i, op=ALU.min),
                      reads=["nsink"], writes=["ast%d" % sc])
            S.add("dve", lambda E, o=ast[0:QN, sc + 1:sc + 2]: E.memset(o, 0.0), writes=["astd%d" % sc])
        for u, (k, sc, b1, b2) in zip(units, R):
            negm = ast[0:QN, sc:sc + 1]
            den = ast[0:QN, sc + 1:sc + 2]
            S.add("act", lambda E, o=Pb[k][0:QN, 0:NKEY], i=Smb[k][0:QN, 0:NKEY], b=negm, a=den: E.activation(out=o, in_=i, func=AF.Exp, bias=b, scale=1.0, accum_out=a),
                  reads=["Sm%d" % k, "ast%d" % sc], writes=["P%d" % k, "astd%d" % sc])
            if is_b:
                S.add("act", lambda E, o=ast[0:QN, sc + 3:sc + 4], i=sink[0:QN, u["hq"]:u["hq"] + 1], b=negm: E.activation(out=o, in_=i, func=AF.Exp, bias=b, scale=1.0),
                      reads=["ast%d" % sc, "sink"], writes=["aste%d" % sc])
        for u, (k, sc, b1, b2) in zip(units, R):
            den = ast[0:QN, sc + 1:sc + 2]
            rden = ast[0:QN, sc + 2:sc + 3]
            if is_b:
                S.add("dve", lambda E, o=den, b=ast[0:QN, sc + 3:sc + 4]: E.tensor_tensor(out=o, in0=o, in1=b, op=ALU.add),
                      reads=["aste%d" % sc], writes=["astd%d" % sc])
            S.add("dve", lambda E, o=rden, i=den: E.reciprocal(out=o, in_=i), reads=["astd%d" % sc], writes=["astr%d" % sc])
            if is_b:
                S.add("dve", lambda E, o=Pb[k][0:QN, 0:NKEY], s=rden: E.tensor_scalar(out=o, in0=o, scalar1=s, scalar2=None, op0=ALU.mult),
                      reads=["astr%d" % sc], writes=["P%d" % k])
            else:
                S.add("act", lambda E, o=ast[0:QN, sc + 3:sc + 4], i=den: E.activation(out=o, in_=i, func=AF.Ln),
                      reads=["astd%d" % sc], writes=["aste%d" % sc])
        for u, (k, sc, b1, b2) in zip(units, R):
            sps, ops_, tps = regs(b1, b2)
            for b in range(NB):
                S.add("pe", lambda E, o=tps[:, b * QN:(b + 1) * QN], i=Pb[k][0:QN, b * KB:(b + 1) * KB]:
                      E.transpose(out=o, in_=i, identity=ident[0:QN, 0:QN]), reads=["P%d" % k, "ident"], writes=[bn(b2)])
        for u, (k, sc, b1, b2) in zip(units, R):
            sps, ops_, tps = regs(b1, b2)
            S.add("act", lambda E, o=PTb[k][0:KB, 0:NB * QN], i=tps: E.copy(out=o, in_=i), reads=[bn(b2)], writes=["PT%d" % k])
            if not is_b:
                S.add("pool", lambda E, o=u["lse"], a=ast[0:QN, sc + 3:sc + 4], b=ast[0:QN, sc:sc + 1]: E.tensor_tensor(out=o, in0=a, in1=b, op=ALU.subtract),
                      reads=["aste%d" % sc, "ast%d" % sc], writes=[u["oname"] + "l"])
        for u, (k, sc, b1, b2) in zip(units, R):
            sps, ops_, tps = regs(b1, b2)
            for b in range(NB):
                if is_b:
                    S.add("pe", lambda E, o=ops_, l=u["vblk"][b], r=PTb[k][0:KB, b * QN:(b + 1) * QN], a=(b == 0), z=(b == NB - 1):
                          E.matmul(o, l, r, start=a, stop=z), reads=["PT%d" % k, u["vname"]], writes=[bn(b1)])
                else:
                    S.add("pe", lambda E, o=ops_, l=PTb[k][0:KB, b * QN:(b + 1) * QN], r=u["vblk"][b], a=(b == 0), z=(b == NB - 1):
                          E.matmul(o, l, r, start=a, stop=z), reads=["PT%d" % k, u["vname"]], writes=[bn(b1)])
        for u, (k, sc, b1, b2) in zip(units, R):
            sps, ops_, tps = regs(b1, b2)
            if is_b:
                S.add("act", lambda E, o=u["out"], i=ops_: E.copy(out=o, in_=i), reads=[bn(b1)], writes=[u["oname"]])
            else:
                S.add("dve", lambda E, o=u["out"], i=ops_, s=ast[0:QN, sc + 2:sc + 3]: E.tensor_scalar(out=o, in0=i, scalar1=s, scalar2=None, op0=ALU.mult),
                      reads=[bn(b1), "astr%d" % sc], writes=[u["oname"]])

    GB = 4
    for g, r in enumerate(DIL):
        nb = EXT // (64 * r)
        nt = OWN // (64 * r)
        blk_base = HALO // (64 * r) - 1
        for h in range(8):
            hh = g * 8 + h
            s2 = hh % 2
            kT, qT, vs, ob = kTb[s2], qTb[s2], vsb[s2], obufs[s2]
            kname, qname, vname, oname = "kT%d" % s2, "qT%d" % s2, "vs%d" % s2, "ob%d" % s2
            cut = 0 if g == 2 else T
            S.add("sp", lambda E, d=kT[:, cut:EXT - cut], s=K_scr[hh][:, cut:EXT - cut]: E.dma_start(out=d, in_=s), writes=[kname], dma=kname)
            S.add("sp", lambda E, d=qT, s=Q_scr[hh]: E.dma_start(out=d, in_=s), writes=[qname], dma=qname)
            vb0 = cut // (64 * r)
            vv = vs[0:64, :].rearrange("p (j b d) -> p j b d", j=r, b=nb)
            if cut == 0:
                vsrc = bass.AP(V_scr.tensor, g * 1024 + h * 128, [[r * VC, 64], [VC, r], [r * 64 * VC, nb], [1, 128]])
                S.add("sp", lambda E, d=vv, s=vsrc: E.dma_start(out=d, in_=s), writes=[vname], dma=vname)
            else:
                for j in range(r):
                    vsrc = bass.AP(V_scr.tensor, g * 1024 + h * 128 + j * VC + vb0 * r * 64 * VC, [[r * VC, 64], [r * 64 * VC, nb - 2 * vb0], [1, 128]])
                    S.add("sp", lambda E, d=vv[:, j, vb0:nb - vb0, :], s=vsrc: E.dma_start(out=d, in_=s), writes=[vname], dma=vname)
            obv = ob[0:64, :].rearrange("p (j n c) -> p j n c", j=r, n=nt)
            units = []
            for j in range(r):
                for n in range(nt):
                    q0 = r * 64 * n + j
                    k0 = HALO + r * 64 * (n - 1) + j

                    def ss(a, cnt, r=r):
                        return slice(a, a + (cnt - 1) * r + 1, r)
                    qe = HALO + q0
                    units.append(dict(qsl=qT[:, ss(q0, 64)], ksl=kT[:, ss(k0, 192)], qtok=sA[:, ss(qe, 64)], ktok=sB[:, ss(k0, 192)],
                                      kname=kname, qname=qname, vname=vname, oname=oname,
                                      vblk=[vv[:, j, blk_base + n + b, :] for b in range(3)],
                                      out=obv[:, j, n, 0:128], lse=obv[:, j, n, 128:129]))
            for i0 in range(0, len(units), GB):
                attn_batch(units[i0:i0 + GB], 64, 192, bandA, "bandA", False)
            odst = bass.AP(O_scr.tensor, hh * OW, [[r * 24 * OW, 64], [24 * OW, r], [64 * r * 24 * OW, nt], [1, OW]])
            S.add("sp", lambda E, d=odst, s=obv: E.dma_start(out=d, in_=s), reads=[oname, oname + "l"], writes=["O_scr"], dma="ost")
    S.barrier()
    acur[0] = amark
    ybufs = [acarve(OWN * 2, BF16) for _ in range(2)]
    obs = [acarve(24 * OW * 4, F32) for _ in range(2)]
    wts = acarve(64 * 4, F32)
    yaf = acarve(1024 * 4, F32)
    yab = acarve(1024 * 2, BF16)
    yaTb = [acarve(8 * 128 * 2, BF16) for _ in range(2)]
    def combine_block(tb):
        s2 = tb % 2
        ob = obs[s2].rearrange("p (h c) -> p h c", c=OW)
        S.add("sp", lambda E, d=ob, s=O_scr[tb * 128:(tb + 1) * 128]: E.dma_start(out=d, in_=s), reads=["O_scr"], writes=["obs%d" % s2], dma="obs%d" % s2)
        lv = ob[:, :, 128].rearrange("p (g h) -> p g h", g=3)
        M = wts[:, 0:8]
        w = wts[:, 8:32].rearrange("p (g h) -> p g h", g=3)
        ws = wts[:, 32:40]
        S.add("dve", lambda E, o=M, a=lv[:, 0, :], b=lv[:, 1, :]: E.tensor_tensor(out=o, in0=a, in1=b, op=ALU.max), reads=["obs%d" % s2], writes=["wts"])
        S.add("dve", lambda E, o=M, b=lv[:, 2, :]: E.tensor_tensor(out=o, in0=o, in1=b, op=ALU.max), reads=["obs%d" % s2], writes=["wts"])
        S.add("dve", lambda E, o=w, a=lv, b=M.unsqueeze(1).to_broadcast([128, 3, 8]): E.tensor_tensor(out=o, in0=a, in1=b, op=ALU.subtract),
              reads=["obs%d" % s2], writes=["wts"])
        S.add("act", lambda E, o=wts[:, 8:32]: E.activation(out=o, in_=o, func=AF.Exp), reads=["wts"], writes=["wts"])
        S.add("dve", lambda E, o=ws, a=w[:, 0, :], b=w[:, 1, :]: E.tensor_tensor(out=o, in0=a, in1=b, op=ALU.add), reads=["wts"], writes=["wts"])
        S.add("dve", lambda E, o=ws, b=w[:, 2, :]: E.tensor_tensor(out=o, in0=o, in1=b, op=ALU.add), writes=["wts"])
        S.add("dve", lambda E, o=ws: E.reciprocal(out=o, in_=o), writes=["wts"])
        S.add("dve", lambda E, o=w, b=ws.unsqueeze(1).to_broadcast([128, 3, 8]): E.tensor_tensor(out=o, in0=o, in1=b, op=ALU.mult), writes=["wts"])
        yv = yaf.rearrange("p (h d) -> p h d", d=128)
        for gi in range(3):
            og = ob[:, gi * 8:(gi + 1) * 8, 0:128]
            wg_ = w[:, gi, :].unsqueeze(2).to_broadcast([128, 8, 128])
            if gi == 0:
                S.add("dve", lambda E, o=yv, a=og, b=wg_: E.tensor_tensor(out=o, in0=a, in1=b, op=ALU.mult), reads=["obs%d" % s2, "wts"], writes=["yaf"])
            else:
                S.add("pool", lambda E, o=og, a=og, b=wg_: E.tensor_tensor(out=o, in0=a, in1=b, op=ALU.mult), reads=["wts"], writes=["obs%d" % s2])
                S.add("dve", lambda E, o=yv, a=yv, b=og: E.tensor_tensor(out=o, in0=a, in1=b, op=ALU.add), reads=["obs%d" % s2], writes=["yaf"])
        S.add("act", lambda E, o=yab, i=yaf: E.copy(out=o, in_=i), reads=["yaf"], writes=["yab"])
        tb_ = alloc_bank()
        for hq in range(8):
            S.add("pe", lambda E, o=bkbf(tb_)[:, hq * 128:(hq + 1) * 128], i=yab[:, hq * 128:(hq + 1) * 128]: E.transpose(out=o, in_=i, identity=ident),
                  reads=["yab", "ident"], writes=[bn(tb_)])
        yTv = yaTb[s2].rearrange("p (h t) -> p h t", h=8)
        S.add("dve", lambda E, o=yaTb[s2], i=bkbf(tb_): E.tensor_copy(out=o, in_=i),
              reads=[bn(tb_)], writes=["yaT%d" % s2])
        S.add("sp", lambda E, d=YA_scr[:, :, tb * 128:(tb + 1) * 128].rearrange("h p t -> p h t"), s=yTv: E.dma_start(out=d, in_=s),
              reads=["yaT%d" % s2], dma="yaT%d" % s2)

    comb_next = [0]

    def maybe_combine(force=False):
        if comb_next[0] < OWN // 128:
            combine_block(comb_next[0])
            comb_next[0] += 1

    for hq in range(16):
        kv = hq // 4
        s2 = hq % 2
        qT, yb = qTb[s2], ybufs[s2]
        if hq % 4 == 0:
            kT, vs = kTb[kv % 2], vsb[kv % 2]
            kname, vname = "kT%d" % (kv % 2), "vs%d" % (kv % 2)
            S.add("sp", lambda E, d=kT[:, T:EXT - T], s=K_scr[24 + kv][:, T:EXT - T]: E.dma_start(out=d, in_=s), writes=[kname], dma=kname)
            vvB = vs[:, 0:40 * 128].rearrange("p (b d) -> p b d", d=128)
            S.add("sp", lambda E, d=vvB[:, 4:36, :], s=V_scr[T:EXT - T, 3072 + kv * 128:3072 + (kv + 1) * 128].rearrange("(b p) d -> p b d", p=128): E.dma_start(out=d, in_=s),
                  writes=[vname], dma=vname)
        qname = "qT%d" % s2
        S.add("sp", lambda E, d=qT, s=Q_scr[24 + hq]: E.dma_start(out=d, in_=s), writes=[qname], dma=qname)
        units = []
        for n in range(OWN // 128):
            q0 = 128 * n
            k0 = HALO + 128 * (n - 1)
            units.append(dict(qsl=qT[:, q0:q0 + 128], ksl=kT[:, k0:k0 + 384], qtok=sA[:, HALO + q0:HALO + q0 + 128], ktok=sB[:, k0:k0 + 384],
                              kname=kname, qname=qname, vname=vname, oname="yb%d" % s2, hq=hq,
                              vblk=[vvB[:, HALO // 128 + n - 1 + b, :] for b in range(3)], out=yb[:, q0:q0 + 128]))
        for bi, i0 in enumerate(range(0, len(units), GB)):
            attn_batch(units[i0:i0 + GB], 128, 384, bandB, "bandB", True)
            if bi % 3 == 1:
                maybe_combine()
        S.add("sp", lambda E, d=YB_scr[hq], s=yb: E.dma_start(out=d, in_=s), reads=["yb%d" % s2], dma="yb%d" % s2)
    while comb_next[0] < OWN // 128:
        maybe_combine()
    S.barrier()

    Wa, Wb_, Wo = wb16["wa"], wb16["wb"], wb16["wo"]
    for to in range(NT_OWN):
        tcol = to * T

        def ldy(_, tcol=tcol):
            S.add("sp", lambda E, d=yT[:, 0:8, :], s=YA_scr[:, :, tcol:tcol + T].rearrange("h p t -> p h t"): E.dma_start(out=d, in_=s),
                  writes=["xnb", "gfin"], dma="yT")
            S.add("sp", lambda E, d=yT[:, 8:24, :], s=YB_scr[:, :, tcol:tcol + T].rearrange("h p t -> p h t"): E.dma_start(out=d, in_=s),
                  writes=["xnb", "gfin"], dma="yT")
        P.stage(ldy)
        for dc0 in range(0, KC, 2):
            nd = min(2, KC - dc0)

            def ldb(slot, dc0=dc0, nd=nd):
                va = wst[slot][:, 0:8 * nd * 128].rearrange("p (k c) -> p k c", k=8)
                vbw = wst[slot][:, 8 * 256:8 * 256 + 16 * nd * 128].rearrange("p (k c) -> p k c", k=16)
                S.add("sp", lambda E, d=va, s=Wa[:, dc0 * 128:(dc0 + nd) * 128].rearrange("(k p) c -> p k c", p=128): E.dma_start(out=d, in_=s),
                      reads=[wpiece("wa", 0, dc0 * 128)], writes=["wst%d" % slot], dma="wst%d" % slot)
                S.add("sp", lambda E, d=vbw, s=Wb_[:, dc0 * 128:(dc0 + nd) * 128].rearrange("(k p) c -> p k c", p=128): E.dma_start(out=d, in_=s),
                      reads=[wpiece("wb", 0, dc0 * 128)], writes=["wst%d" % slot], dma="wst%d" % slot)

            def cb(slot, dc0=dc0, nd=nd, tcol=tcol):
                va = wst[slot][:, 0:8 * nd * 128].rearrange("p (k c) -> p k c", k=8)
                vbw = wst[slot][:, 8 * 256:8 * 256 + 16 * nd * 128].rearrange("p (k c) -> p k c", k=16)
                for ci in range(nd):
                    dc = dc0 + ci
                    gi = rr("sg")
                    gsrc = G_scr.rearrange("(a c) p t -> p a c t", a=2)[:, :, dc, tcol:tcol + T]
                    S.add("sp", lambda E, d=sgt[gi], s=gsrc: E.dma_start(out=d, in_=s), writes=["sg%d" % gi], dma="sg%d" % gi)
                    ba = alloc_bank()
                    for kc in range(8):
                        S.add("pe", lambda E, o=bk(ba), l=va[:, kc, ci * 128:(ci + 1) * 128], r=yT[:, kc, :], a=(kc == 0), z=(kc == 7):
                              E.matmul(o, l, r, start=a, stop=z), reads=["wst%d" % slot, "xnb", "gfin"], writes=[bn(ba)])
                    bb = alloc_bank()
                    for kc in range(16):
                        S.add("pe", lambda E, o=bk(bb), l=vbw[:, kc, ci * 128:(ci + 1) * 128], r=yT[:, 8 + kc, :], a=(kc == 0), z=(kc == 15):
                              E.matmul(o, l, r, start=a, stop=z), reads=["wst%d" % slot, "xnb", "gfin"], writes=[bn(bb)])
                    m1, m2 = rr("mt"), rr("sil")
                    S.add("dve", lambda E, o=mt[m1], a=bk(ba), b=sgt[gi][:, 0, :]: E.tensor_tensor(out=o, in0=a, in1=b, op=ALU.mult),
                          reads=[bn(ba), "sg%d" % gi], writes=["mt%d" % m1])
                    S.add("dve", lambda E, o=sil[m2], a=bk(bb), b=sgt[gi][:, 1, :]: E.tensor_tensor(out=o, in0=a, in1=b, op=ALU.mult),
                          reads=[bn(bb), "sg%d" % gi], writes=["sil%d" % m2])
                    S.add("pool", lambda E, o=xnT[:, dc, :], a=mt[m1], b=sil[m2]: E.tensor_tensor(out=o, in0=a, in1=b, op=ALU.add),
                          reads=["mt%d" % m1, "sil%d" % m2], writes=["xnT"])
            P.stage(cb, ldb)

        def ldh(_, tcol=tcol):
            S.add("sp", lambda E, d=hbuf, s=h_scr[tcol:tcol + T, :].rearrange("(b p) d -> p b d", p=128): E.dma_start(out=d, in_=s),
                  writes=HB_ALL, dma="hbuf")
        P.stage(ldh)
        KH = max(1, KC // 2)
        halves = [(k0, min(KH, KC - k0)) for k0 in range(0, KC, KH)]
        for c in range(0, D, 512):
            st = {}
            for hi, (k0, kn) in enumerate(halves):
                def ldo(slot, c=c, k0=k0, kn=kn):
                    v = wst[slot][:, 0:kn * 512].rearrange("p (k c) -> p k c", k=kn)
                    S.add("sp", lambda E, d=v, s=Wo[k0 * 128:(k0 + kn) * 128, c:c + 512].rearrange("(k p) c -> p k c", p=128): E.dma_start(out=d, in_=s),
                          reads=[wpiece("wo", 0, c)], writes=["wst%d" % slot], dma="wst%d" % slot)

                def co(slot, c=c, k0=k0, kn=kn, st=st, first=(hi == 0), last=(hi == len(halves) - 1)):
                    v = wst[slot][:, 0:kn * 512].rearrange("p (k c) -> p k c", k=kn)
                    if first:
                        st["b"] = [alloc_bank() for _ in range(4)]
                    for b in range(4):
                        bb = st["b"][b]
                        for k in range(kn):
                            S.add("pe", lambda E, o=bk(bb), l=xnT[:, k0 + k, b * 128:(b + 1) * 128], r=v[:, k, :], a=(k0 + k == 0), z=(k0 + k == KC - 1):
                                  E.matmul(o, l, r, start=a, stop=z), reads=["wst%d" % slot, "xnT"], writes=[bn(bb)])
                    if last:
                        for b in range(4):
                            bb = st["b"][b]
                            hs = hbuf[:, b, c:c + 512]
                            S.add("dve", lambda E, o=hs, i=bk(bb): E.tensor_tensor(out=o, in0=i, in1=o, op=ALU.add),
                                  reads=[bn(bb), "hbuf"], writes=["hbuf"])
                P.stage(co, ldo)
        norm_to_T(2, ["hbuf"], "n2")
        ffn("w2g", "w2u", "w2d", "f2")

        def fin(_, tcol=tcol):
            S.add("sp", lambda E, d=gfin, s=gfin_in.partition_broadcast(128): E.dma_start(out=d, in_=s), writes=["gfin"], dma="gfin")
            for b in range(4):
                ss = stat[:, 16 + b:17 + b]
                rs = stat[:, 24 + b:25 + b]
                S.add("dve", lambda E, o=ss: E.memset(o, 0.0), writes=["stat"])
                S.add("act", lambda E, o=xnb, i=hbuf[:, b, :], a=ss: E.activation(out=o, in_=i, func=AF.Square, accum_out=a),
                      reads=["hbuf"], writes=["xnb", "stat"])
                S.add("dve", lambda E, o=rs, i=ss: E.tensor_scalar(out=o, in0=i, scalar1=1.0 / D, scalar2=EPS, op0=ALU.mult, op1=ALU.add),
                      reads=["stat"], writes=["stat"])
                S.add("act", lambda E, o=rs: E.sqrt(out=o, in_=o), reads=["stat"], writes=["stat"])
                S.add("dve", lambda E, o=rs: E.reciprocal(out=o, in_=o), reads=["stat"], writes=["stat"])
                S.add("act", lambda E, o=hbuf[:, b, :], s=rs: E.mul(out=o, in_=o, mul=s),
                      reads=["hbuf", "stat"], writes=["hbuf"])
                S.add("dve", lambda E, o=hbuf[:, b, :], g=gfin: E.tensor_tensor(out=o, in0=o, in1=g, op=ALU.mult),
                      reads=["hbuf", "gfin"], writes=["hbuf"])
            S.add("sp", lambda E, d=y_out[tcol:tcol + T, :].rearrange("(b p) d -> p b d", p=128), s=hbuf: E.dma_start(out=d, in_=s),
                  reads=HB_ALL, dma="yst")
        P.stage(fin)
    P.flush()
    S.emit(nc, stack)
    stack.close()
    return nc


def _host_tables(seq_lens):
    tot = sum(seq_lens)
    seq_id = np.concatenate([np.full(n, i, np.int64) for i, n in enumerate(seq_lens)])
    pos = np.concatenate([np.arange(n, dtype=np.int64) for n in seq_lens])
    half = ROT // 2
    inv = (THETA ** (-np.arange(half, dtype=np.float32) / np.float32(half))).astype(np.float32)
    out = []
    for c in range(N_CORES):
        g0 = c * OWN - HALO
        idx = np.arange(g0, g0 + EXT)
        valid = (idx >= 0) & (idx < tot)
        ci = np.clip(idx, 0, tot - 1)
        sid = np.where(valid, seq_id[ci], 3)
        p = np.where(valid, pos[ci], 0).astype(np.float32)
        ang = p[None, :] * inv[:, None]
        cs, sn = np.cos(ang).astype(np.float32), np.sin(ang).astype(np.float32)
        ropeC = np.concatenate([cs, cs], 0)
        ropeS = np.concatenate([-sn, sn], 0)
        onehot = (sid[None, :] == np.arange(3)[:, None])
        seqA = onehot.astype(np.float32).astype(ml_dtypes.bfloat16)
        seqB = (np.float32(NEGBIG) * (1.0 - onehot.astype(np.float32))).astype(ml_dtypes.bfloat16)
        out.append(dict(ropeC=np.ascontiguousarray(ropeC), ropeS=np.ascontiguousarray(ropeS), seqA=seqA, seqB=seqB, idx=idx, valid=valid))
    return out


def _consts():
    i = np.arange(64)[:, None]
    k = np.arange(192)[None, :]
    bandA = np.where(np.abs(k - 64 - i) <= 64, 0.0, NEGBIG).astype(np.float32)
    i = np.arange(128)[:, None]
    k = np.arange(384)[None, :]
    bandB = np.where(np.abs(k - 128 - i) <= 128, 0.0, NEGBIG).astype(np.float32)
    ident = np.eye(128, dtype=np.float32).astype(ml_dtypes.bfloat16)
    prot = np.zeros((ROT, ROT), np.float32)
    for m in range(ROT):
        prot[(m + 16) % ROT, m] = 1.0
    return bandA, bandB, ident, prot.astype(ml_dtypes.bfloat16)


_NC_CACHE = {}


def run(cfg, x_prompt, x_sample, g_ffn1, w1_gate, w1_up, w1_down, g_mix, w_in, sink_b, w_branch_a, w_branch_b,
        w_out, g_ffn2, w2_gate, w2_up, w2_down, g_final, trace=False):
    D = cfg.D
    f = lambda a: np.ascontiguousarray(np.asarray(a, dtype=np.float32))
    xp, xs = f(x_prompt), f(x_sample)
    seq_lens = [xp.shape[1]] * xp.shape[0] + [xs.shape[1]] * xs.shape[0]
    X = np.concatenate([xp.reshape(-1, D), xs.reshape(-1, D)], 0)
    tot = X.shape[0]
    assert tot == N_CORES * OWN
    tabs = _host_tables(seq_lens)
    bandA, bandB, ident, prot = _consts()
    KC = cfg.KC
    gT = np.concatenate([f(g)[0].reshape(KC, 128).T for g in (g_ffn1, g_mix, g_ffn2)], 1)
    shared = dict(
        w1g=f(w1_gate)[0], w1u=f(w1_up)[0], w1d=f(w1_down)[0], win=f(w_in)[0], wa=f(w_branch_a)[0], wb=f(w_branch_b)[0],
        wo=f(w_out)[0], w2g=f(w2_gate)[0], w2u=f(w2_up)[0], w2d=f(w2_down)[0],
        gT=np.ascontiguousarray(gT), gfin=f(g_final), sink=f(sink_b)[0],
        bandA=bandA, bandB=bandB, ident=ident, prot=prot,
    )
    in_maps = []
    for c in range(N_CORES):
        t = tabs[c]
        xe = np.zeros((EXT, D), np.float32)
        xe[t["valid"]] = X[t["idx"][t["valid"]]]
        m = dict(shared)
        m.update(xe=xe, ropeC=t["ropeC"], ropeS=t["ropeS"], seqA=t["seqA"], seqB=t["seqB"])
        in_maps.append(m)
    key = (cfg.D, cfg.F, cfg.HP)
    if key not in _NC_CACHE:
        _NC_CACHE[key] = build(cfg)
    nc = _NC_CACHE[key]
    res = run_bass_kernel_spmd(nc, in_maps, core_ids=list(range(N_CORES)), trace=trace)
    Y = np.concatenate([np.asarray(r["y"]) for r in res.results], 0).astype(np.float32)
    n0 = xp.shape[0] * xp.shape[1]
    return (Y[:n0].reshape(xp.shape), Y[n0:].reshape(xs.shape)), res


def kernel(**inputs):
    out, _ = run(Cfg(), **inputs)
    return out
```
